# Optimizing a Trainium2 kernel written in Bass

```python
import jax
import jax.numpy as jnp
from jax import lax
import numpy as np

D_MODEL = 2048
BATCH = 1
SEQ = 8192
DEPTH = 4

N_MIXERS = 3
N_CONV_LAYERS = (DEPTH + 2) // 3
N_MLA_LAYERS = (DEPTH + 1) // 3
N_MLSTM_LAYERS = DEPTH // 3
D_FF = ((8 * D_MODEL + 3 * 256 - 1) // (3 * 256)) * 256
EPS = 1e-6
MOD_SCALE = 0.5
CONV_WIDTH = 3
MLA_HEADS = 16
MLA_Q_LORA = 768
MLA_KV_LORA = 512
MLA_NOPE = 128
MLA_ROPE = 64
MLA_V = 128
ROPE_THETA = 10000.0
ATTN_BLOCK = 128
NEG_INF = -1e30
MLSTM_HEADS = 4
MLSTM_QK = D_MODEL // 2 // MLSTM_HEADS
MLSTM_V = D_MODEL // MLSTM_HEADS
MLSTM_CHUNK = 64
GATE_SOFTCAP = 15.0
FGATE_BIAS = 3.0

kernel_name = 'hybrid_conv_mla_mlstm_adaln_trunk'


def rms_norm(x, g):
    xf = x.astype(jnp.float32)
    y = xf * lax.rsqrt(jnp.mean(xf * xf, axis=-1, keepdims=True) + EPS)
    return (y * g.astype(jnp.float32)).astype(x.dtype)


def apply_rope(x, positions):
    d = x.shape[-1]
    half = d // 2
    inv_freq = jnp.power(ROPE_THETA, -2.0 * jnp.arange(half, dtype=jnp.float32) / d)
    ang = positions.astype(jnp.float32)[..., None] * inv_freq
    cos = jnp.cos(ang)[:, :, None, :]
    sin = jnp.sin(ang)[:, :, None, :]
    xf = x.astype(jnp.float32)
    x1, x2 = xf[..., :half], xf[..., half:]
    return jnp.concatenate([x1 * cos - x2 * sin, x1 * sin + x2 * cos], axis=-1).astype(x.dtype)


def swiglu(h, w_gate, w_up, w_down):
    return (jax.nn.silu(h @ w_gate) * (h @ w_up)) @ w_down


def short_conv_mixer(h, w_in, conv_w, w_out):
    b_gate, c_gate, u = jnp.split(h @ w_in, 3, axis=-1)
    z = lax.conv_general_dilated(
        c_gate * u, conv_w[:, None, :],
        window_strides=(1,), padding=[(CONV_WIDTH - 1, 0)],
        dimension_numbers=('NWC', 'WIO', 'NWC'), feature_group_count=D_MODEL)
    return (b_gate * z) @ w_out


def causal_block_attention(q_nope, q_pe, k_nope, k_pe, v):
    b, s, nh, _ = q_nope.shape
    dv = v.shape[-1]
    nq = s // ATTN_BLOCK
    scale = (MLA_NOPE + MLA_ROPE) ** -0.5
    k_idx = jnp.arange(s)

    def to_blocks(t):
        return jnp.moveaxis(t.reshape(b, nq, ATTN_BLOCK, *t.shape[2:]), 1, 0)

    def block(args):
        qn, qp, start = args
        sc = (jnp.einsum('bqhd,bkhd->bhqk', qn, k_nope)
              + jnp.einsum('bqhd,bkd->bhqk', qp, k_pe)).astype(jnp.float32) * scale
        q_idx = start + jnp.arange(ATTN_BLOCK)
        sc = jnp.where(k_idx[None, :] <= q_idx[:, None], sc, NEG_INF)
        p = jax.nn.softmax(sc, axis=-1).astype(v.dtype)
        return jnp.einsum('bhqk,bkhd->bqhd', p, v)

    starts = jnp.arange(nq) * ATTN_BLOCK
    out = lax.map(block, (to_blocks(q_nope), to_blocks(q_pe), starts))
    return jnp.moveaxis(out, 0, 1).reshape(b, s, nh * dv)


def mla_mixer(h, positions, w_dq, q_norm_g, w_uq, w_dkv, kv_norm_g, w_ukv, w_o):
    b, s, _ = h.shape
    c_q = rms_norm(h @ w_dq, q_norm_g)
    q = (c_q @ w_uq).reshape(b, s, MLA_HEADS, MLA_NOPE + MLA_ROPE)
    q_nope, q_pe = q[..., :MLA_NOPE], apply_rope(q[..., MLA_NOPE:], positions)
    kv_a = h @ w_dkv
    c_kv = rms_norm(kv_a[..., :MLA_KV_LORA], kv_norm_g)
    k_pe = apply_rope(kv_a[..., None, MLA_KV_LORA:], positions)[:, :, 0, :]
    kv = (c_kv @ w_ukv).reshape(b, s, MLA_HEADS, MLA_NOPE + MLA_V)
    k_nope, v = kv[..., :MLA_NOPE], kv[..., MLA_NOPE:]
    return causal_block_attention(q_nope, q_pe, k_nope, k_pe, v) @ w_o


def mlstm_chunkwise(q, k, v, i_pre, log_f):
    b, nh, s, dk = q.shape
    dv = v.shape[-1]
    L = MLSTM_CHUNK
    nc = s // L

    def chunks(t):
        return jnp.moveaxis(t.reshape(b, nh, nc, L, *t.shape[3:]), 2, 0)

    k = k * dk ** -0.5
    f_cum = jnp.cumsum(chunks(log_f), axis=-1)
    tril = jnp.tril(jnp.ones((L, L), dtype=bool))

    def step(carry, inp):
        c_st, n_st, m_st = carry
        qc, kc, vc, ic, fc = inp
        log_d = jnp.where(tril, fc[..., :, None] - fc[..., None, :] + ic[..., None, :], -jnp.inf)
        log_inter = fc + m_st[..., None]
        m_row = jnp.maximum(jnp.max(log_d, axis=-1), log_inter)
        w_intra = jnp.exp(log_d - m_row[..., None])
        w_inter = jnp.exp(log_inter - m_row)
        scores = jnp.einsum('bhjd,bhsd->bhjs', qc, kc) * w_intra
        num = (jnp.einsum('bhjs,bhsv->bhjv', scores, vc)
               + w_inter[..., None] * jnp.einsum('bhjd,bhdv->bhjv', qc, c_st))
        den = jnp.sum(scores, axis=-1) + w_inter * jnp.einsum('bhjd,bhd->bhj', qc, n_st)
        h = num / jnp.maximum(jnp.abs(den), jnp.exp(-m_row))[..., None]
        f_last = fc[..., -1]
        log_w = f_last[..., None] - fc + ic
        m_new = jnp.maximum(f_last + m_st, jnp.max(log_w, axis=-1))
        w_s = jnp.exp(log_w - m_new[..., None])
        decay = jnp.exp(f_last + m_st - m_new)
        c_new = decay[..., None, None] * c_st + jnp.einsum('bhsd,bhsv->bhdv', kc * w_s[..., None], vc)
        n_new = decay[..., None] * n_st + jnp.einsum('bhs,bhsd->bhd', w_s, kc)
        return (c_new, n_new, m_new), h

    init = (jnp.zeros((b, nh, dk, dv), jnp.float32),
            jnp.zeros((b, nh, dk), jnp.float32),
            jnp.zeros((b, nh), jnp.float32))
    _, hs = lax.scan(step, init, (chunks(q), chunks(k), chunks(v), chunks(i_pre), f_cum))
    return jnp.moveaxis(hs, 0, 2).reshape(b, nh, s, dv)


def mlstm_mixer(h, w_in, b_gates, head_norm_g, w_out):
    b, s, _ = h.shape
    nh = MLSTM_HEADS
    qk_w = nh * MLSTM_QK
    v_w = nh * MLSTM_V
    q, k, v, o, gates = jnp.split(
        h @ w_in, [qk_w, 2 * qk_w, 2 * qk_w + v_w, 2 * qk_w + v_w + D_MODEL], axis=-1)

    def heads(t, d):
        return t.reshape(b, s, nh, d).transpose(0, 2, 1, 3).astype(jnp.float32)

    gates = gates.astype(jnp.float32) + b_gates.astype(jnp.float32)
    gates = GATE_SOFTCAP * jnp.tanh(gates / GATE_SOFTCAP)
    i_pre = gates[..., :nh].transpose(0, 2, 1)
    log_f = jax.nn.log_sigmoid(gates[..., nh:]).transpose(0, 2, 1)
    h_t = mlstm_chunkwise(heads(q, MLSTM_QK), heads(k, MLSTM_QK), heads(v, MLSTM_V), i_pre, log_f)
    h_t = rms_norm(h_t.transpose(0, 2, 1, 3), head_norm_g.reshape(nh, MLSTM_V))
    h_t = h_t.reshape(b, s, v_w).astype(h.dtype)
    return (jax.nn.sigmoid(o) * h_t) @ w_out


def setup_inputs(seed: int = 0) -> dict:
    key = jax.random.key(seed)
    keys = iter(jax.random.split(key, 40))

    def normal(shape, scale):
        return scale * jax.random.normal(next(keys), shape, jnp.float32)

    def gain(shape):
        return 1.0 + normal(shape, 0.02)

    x = normal((BATCH, SEQ, D_MODEL), 1.0)
    c = normal((BATCH, D_MODEL), 1.0)
    start = jax.random.randint(next(keys), (BATCH, 1), 0, 1024, dtype=jnp.int32)
    positions = start + jnp.arange(SEQ, dtype=jnp.int32)[None, :]
    mlstm_in_w = 2 * MLSTM_HEADS * MLSTM_QK + MLSTM_HEADS * MLSTM_V + D_MODEL + 2 * MLSTM_HEADS
    mlstm_b_gates = jnp.concatenate(
        [normal((N_MLSTM_LAYERS, MLSTM_HEADS), 0.1),
         FGATE_BIAS + normal((N_MLSTM_LAYERS, MLSTM_HEADS), 0.1)], axis=-1)
    return {
        'x': x,
        'c': c,
        'positions': positions,
        'mod_w': normal((DEPTH, D_MODEL, 6 * D_MODEL), MOD_SCALE * D_MODEL ** -0.5),
        'mod_b': normal((DEPTH, 6 * D_MODEL), 0.02),
        'norm1_g': gain((DEPTH, D_MODEL)),
        'norm2_g': gain((DEPTH, D_MODEL)),
        'ffn_w_gate': normal((DEPTH, D_MODEL, D_FF), D_MODEL ** -0.5),
        'ffn_w_up': normal((DEPTH, D_MODEL, D_FF), D_MODEL ** -0.5),
        'ffn_w_down': normal((DEPTH, D_FF, D_MODEL), D_FF ** -0.5),
        'conv_w_in': normal((N_CONV_LAYERS, D_MODEL, 3 * D_MODEL), D_MODEL ** -0.5),
        'conv_w': normal((N_CONV_LAYERS, CONV_WIDTH, D_MODEL), CONV_WIDTH ** -0.5),
        'conv_w_out': normal((N_CONV_LAYERS, D_MODEL, D_MODEL), D_MODEL ** -0.5),
        'mla_w_dq': normal((N_MLA_LAYERS, D_MODEL, MLA_Q_LORA), D_MODEL ** -0.5),
        'mla_q_norm_g': gain((N_MLA_LAYERS, MLA_Q_LORA)),
        'mla_w_uq': normal((N_MLA_LAYERS, MLA_Q_LORA, MLA_HEADS * (MLA_NOPE + MLA_ROPE)), MLA_Q_LORA ** -0.5),
        'mla_w_dkv': normal((N_MLA_LAYERS, D_MODEL, MLA_KV_LORA + MLA_ROPE), D_MODEL ** -0.5),
        'mla_kv_norm_g': gain((N_MLA_LAYERS, MLA_KV_LORA)),
        'mla_w_ukv': normal((N_MLA_LAYERS, MLA_KV_LORA, MLA_HEADS * (MLA_NOPE + MLA_V)), MLA_KV_LORA ** -0.5),
        'mla_w_o': normal((N_MLA_LAYERS, MLA_HEADS * MLA_V, D_MODEL), (MLA_HEADS * MLA_V) ** -0.5),
        'mlstm_w_in': normal((N_MLSTM_LAYERS, D_MODEL, mlstm_in_w), D_MODEL ** -0.5),
        'mlstm_b_gates': mlstm_b_gates,
        'mlstm_head_norm_g': gain((N_MLSTM_LAYERS, MLSTM_HEADS * MLSTM_V)),
        'mlstm_w_out': normal((N_MLSTM_LAYERS, D_MODEL, D_MODEL), D_MODEL ** -0.5),
        'final_norm_g': gain((D_MODEL,)),
    }


def reference(x, c, positions, mod_w, mod_b, norm1_g, norm2_g, ffn_w_gate, ffn_w_up, ffn_w_down,
              conv_w_in, conv_w, conv_w_out,
              mla_w_dq, mla_q_norm_g, mla_w_uq, mla_w_dkv, mla_kv_norm_g, mla_w_ukv, mla_w_o,
              mlstm_w_in, mlstm_b_gates, mlstm_head_norm_g, mlstm_w_out, final_norm_g):
    c_act = jax.nn.silu(c)
    for i in range(DEPTH):
        mod = (c_act @ mod_w[i] + mod_b[i])[:, None, :]
        sh1, sc1, g1, sh2, sc2, g2 = jnp.split(mod, 6, axis=-1)
        h = rms_norm(x, norm1_g[i]) * (1.0 + sc1) + sh1
        kind, j = i % N_MIXERS, i // N_MIXERS
        if kind == 0:
            y = short_conv_mixer(h, conv_w_in[j], conv_w[j], conv_w_out[j])
        elif kind == 1:
            y = mla_mixer(h, positions, mla_w_dq[j], mla_q_norm_g[j], mla_w_uq[j],
                          mla_w_dkv[j], mla_kv_norm_g[j], mla_w_ukv[j], mla_w_o[j])
        else:
            y = mlstm_mixer(h, mlstm_w_in[j], mlstm_b_gates[j], mlstm_head_norm_g[j], mlstm_w_out[j])
        x = x + g1 * y
        h = rms_norm(x, norm2_g[i]) * (1.0 + sc2) + sh2
        x = x + g2 * swiglu(h, ffn_w_gate[i], ffn_w_up[i], ffn_w_down[i])
    return rms_norm(x, final_norm_g)
```

```python
import math
from contextlib import ExitStack
import numpy as np
import ml_dtypes
import concourse.bass as bass
import concourse.mybir as mybir
from concourse.bass_utils import run_bass_kernel_spmd

F32 = mybir.dt.float32
BF16 = mybir.dt.bfloat16
I32 = mybir.dt.int32
ALU = mybir.AluOpType
AF = mybir.ActivationFunctionType
NPBF = ml_dtypes.bfloat16

NCORES = 8
D = 2048
SEQ = 8192
T = SEQ // NCORES
DFF = 5632
EPS = 1e-6
COMPUTE = ("pe", "act", "dve", "pool")
ALLENG = ("pe", "act", "dve", "pool", "sp")


class Buf:
    __slots__ = ("name", "writer", "readers", "dma_sem", "dma_cnt", "dma_last")

    def __init__(self, name):
        self.name = name
        self.writer = None
        self.readers = []
        self.dma_sem = None
        self.dma_cnt = 0
        self.dma_last = None


class View:
    __slots__ = ("ap", "bufs")

    def __init__(self, ap, bufs):
        self.ap = ap
        self.bufs = list(bufs)

    def __getitem__(self, idx):
        return View(self.ap[idx], self.bufs)


class Op:
    __slots__ = ("eng", "fn", "waits", "signal", "idx", "is_dma", "dsem", "dval")

    def __init__(self, eng, fn):
        self.eng = eng
        self.fn = fn
        self.waits = []
        self.signal = False
        self.idx = None
        self.is_dma = False
        self.dsem = None
        self.dval = 0


class Sched:
    def __init__(self, nc, same_engine_sync=True):
        self.nc = nc
        self.es = ExitStack()
        self.ops = {e: [] for e in ALLENG}
        self.same_engine_sync = same_engine_sync
        self.nbuf = 0

    def buf(self, name=None):
        self.nbuf += 1
        return Buf(name or f"b{self.nbuf}")

    def sb(self, name, shape, dtype):
        t = self.es.enter_context(self.nc.sbuf_tensor("sb_" + name, list(shape), dtype))
        return View(t[:], [self.buf(name)])

    def ps(self, name, shape, dtype=F32):
        t = self.es.enter_context(self.nc.psum_tensor(name, list(shape), dtype))
        return View(t[:], [self.buf(name)])

    def _deps(self, op, reads, writes):
        deps = []
        for v in reads:
            for b in v.bufs:
                if b.writer is not None:
                    deps.append(b.writer)
        for v in writes:
            for b in v.bufs:
                if b.writer is not None:
                    deps.append(b.writer)
                deps.extend(b.readers)
        seen = set()
        for d in deps:
            if d is op or id(d) in seen:
                continue
            seen.add(id(d))
            if (not d.is_dma) and d.eng == op.eng:
                if d.eng == "pe" or not self.same_engine_sync:
                    continue
            op.waits.append(d)
        for v in reads:
            for b in v.bufs:
                b.readers.append(op)
        for v in writes:
            for b in v.bufs:
                b.writer = op
                b.readers = []

    def op(self, eng, fn, reads=(), writes=()):
        o = Op(eng, fn)
        self._deps(o, reads, writes)
        self.ops[eng].append(o)
        return o

    def dma(self, eng, out, in_, chan=None, extra=None):
        pairs = [(out, in_)] + list(extra or [])
        if chan is None:
            chan = out.bufs[0] if out.bufs else in_.bufs[0]
        if chan.dma_sem is None:
            chan.dma_sem = self.es.enter_context(self.nc.semaphore("d_" + chan.name))
        sem = chan.dma_sem

        def fn(e, pairs=pairs, sem=sem):
            for (o_, i_) in pairs:
                e.dma_start(out=o_.ap, in_=i_.ap).then_inc(sem, 16)
            return None

        o = Op(eng, fn)
        o.is_dma = True
        o.dsem = sem
        chan.dma_cnt += 16 * len(pairs)
        o.dval = chan.dma_cnt
        if chan.dma_last is not None:
            o.waits.append(chan.dma_last)
        chan.dma_last = o
        self._deps(o, [p[1] for p in pairs], [p[0] for p in pairs])
        self.ops[eng].append(o)
        return o

    def emit(self, final_waits=()):
        nc = self.nc
        sems = {e: self.es.enter_context(nc.semaphore("s_" + e)) for e in COMPUTE}
        for e in ALLENG:
            for o in self.ops[e]:
                for d in o.waits:
                    if not d.is_dma:
                        d.signal = True
        for fo in final_waits:
            if not fo.is_dma:
                fo.signal = True
        for e in ALLENG:
            c = 0
            for o in self.ops[e]:
                if o.signal and not o.is_dma:
                    c += 1
                    o.idx = c

        def run_stream(e, eng):
            seen = {}
            for o in self.ops[e]:
                for d in o.waits:
                    if d.is_dma:
                        key = ("d", id(d.dsem))
                        if seen.get(key, 0) >= d.dval:
                            continue
                        seen[key] = d.dval
                        eng.wait_ge(d.dsem, d.dval)
                    else:
                        key = ("c", d.eng)
                        if seen.get(key, 0) >= d.idx:
                            continue
                        seen[key] = d.idx
                        eng.wait_ge(sems[d.eng], d.idx)
                ins = o.fn(eng)
                if o.signal and not o.is_dma:
                    ins.then_inc(sems[e], 1)
            if e == "sp":
                for fo in final_waits:
                    if fo.is_dma:
                        eng.wait_ge(fo.dsem, fo.dval)
                    else:
                        eng.wait_ge(sems[fo.eng], fo.idx)

        with nc.Block() as block:
            @block.tensor
            def _(eng):
                run_stream("pe", eng)

            @block.scalar
            def _(eng):
                run_stream("act", eng)

            @block.vector
            def _(eng):
                run_stream("dve", eng)

            @block.gpsimd
            def _(eng):
                run_stream("pool", eng)

            @block.sync
            def _(eng):
                run_stream("sp", eng)
        self.es.close()


class FM:
    def __init__(self, p, name, nch, Tn, dt, tw=512):
        self.t = p.S.es.enter_context(p.nc.sbuf_tensor("fm_" + name, [128, nch, Tn], dt))
        self.tw = tw
        self.ntt = Tn // tw
        self.b = [[p.S.buf(f"{name}_{c}_{t}") for t in range(self.ntt)] for c in range(nch)]

    def v(self, c, tt, rows=128):
        return View(self.t[:rows, c, tt * self.tw:(tt + 1) * self.tw], [self.b[c][tt]])

    def vc(self, c, rows=128):
        return View(self.t[:rows, c, :], self.b[c])


class Prog:
    def __init__(self, wslots=3, slot_elems=8192):
        self.nc = bass.Bass("TRN2", target_bir_lowering=False)
        self.S = Sched(self.nc)
        self.in_names = []
        self.out_names = []
        self.out_ops = []
        self.banks = [self.S.ps(f"psb{i}", [128, 512], F32) for i in range(8)]
        self.rr = {}
        self.scr_pool = {}
        self.ones = self.S.sb("ones_bf", [128, 128], BF16)
        o = self.ones
        self.S.op("pool", lambda e: e.memset(o.ap, 1.0), writes=[o])
        self.slots = [self.S.sb(f"wslot{i}", [128, slot_elems], BF16) for i in range(wslots)]
        self.slot_elems = slot_elems
        self.slot_i = 0

    def inp(self, name, shape, dt):
        self.in_names.append(name)
        return self.nc.dram_tensor(name, list(shape), dt, kind="ExternalInput").ap()

    def out(self, name, shape, dt):
        self.out_names.append(name)
        return self.nc.dram_tensor(name, list(shape), dt, kind="ExternalOutput").ap()

    def load(self, sbv, dram_ap, eng="sp"):
        return self.S.dma(eng, sbv, View(dram_ap, []))

    def store(self, dram_ap, sbv, eng="sp"):
        o = self.S.dma(eng, View(dram_ap, []), sbv, chan=sbv.bufs[0])
        self.out_ops.append(o)
        return o

    def bank(self, lo=0, hi=6):
        k = (lo, hi)
        i = self.rr.get(k, 0)
        self.rr[k] = (i + 1) % (hi - lo)
        return self.banks[lo + i]

    def scr(self, name, shape, dt, n=2):
        if name not in self.scr_pool:
            self.scr_pool[name] = ([self.S.sb(f"{name}{i}", shape, dt) for i in range(n)], [0])
        lst, ctr = self.scr_pool[name]
        v = lst[ctr[0] % len(lst)]
        ctr[0] += 1
        return v

    def wslot(self):
        s = self.slots[self.slot_i % len(self.slots)]
        self.slot_i += 1
        return s

    def finish(self):
        self.S.emit(final_waits=self.out_ops)
        return self.nc


def linear(p, K, groups, rhs, ntt, evac, interleave=False, tw=512):
    S = p.S
    KC = K // 128
    for gi, segs in enumerate(groups):
        slot = p.wslot()
        tot = sum(s[2] for s in segs)
        assert KC * tot <= p.slot_elems, (KC, tot)
        sv3 = slot.ap[:, :KC * tot].rearrange("p (k n) -> p k n", n=tot)
        pairs = []
        offs = []
        off = 0
        for (w, c0, n) in segs:
            pairs.append((View(sv3[:, :, off:off + n], slot.bufs),
                          View(w[:, c0:c0 + n].rearrange("(k p) n -> p k n", p=128), [])))
            offs.append(off)
            off += n
        S.dma("pool", pairs[0][0], pairs[0][1], extra=pairs[1:])
        chunks = []
        for si, (w, c0, n) in enumerate(segs):
            for ci in range(0, n, 128):
                chunks.append((ci // 128, si, offs[si] + ci, min(128, n - ci)))
        if interleave:
            chunks.sort(key=lambda t: (t[0], t[1]))
        for (ci, si, o0, m) in chunks:
            for tt in range(ntt):
                ps = p.bank()
                for k in range(KC):
                    r = rhs(k, tt)
                    S.op("pe", lambda e, ps=ps, k=k, r=r, o0=o0, m=m, sv3=sv3: e.matmul(
                        ps.ap[:m, :tw], lhsT=sv3[:, k, o0:o0 + m], rhs=r.ap, start=(k == 0), stop=(k == KC - 1)),
                        reads=[slot, r], writes=[ps])
                evac(gi, si, ci, m, tt, ps)


def linear_tm(p, K, w, ncols, lhs, ntb, evac, cw=512):
    S = p.S
    KC = K // 128
    gcols = (p.slot_elems // KC) // cw * cw
    for g0 in range(0, ncols, gcols):
        gn = min(gcols, ncols - g0)
        slot = p.wslot()
        sv3 = slot.ap[:, :KC * gn].rearrange("p (k n) -> p k n", n=gn)
        S.dma("pool", View(sv3, slot.bufs), View(w[:, g0:g0 + gn].rearrange("(k p) n -> p k n", p=128), []))
        for tb in range(ntb):
            for c0 in range(0, gn, cw):
                ps = p.bank()
                for k in range(KC):
                    l = lhs(k, tb)
                    S.op("pe", lambda e, ps=ps, k=k, l=l, c0=c0, sv3=sv3: e.matmul(
                        ps.ap[:, :cw], lhsT=l.ap, rhs=sv3[:, k, c0:c0 + cw], start=(k == 0), stop=(k == KC - 1)),
                        reads=[slot, l], writes=[ps])
                evac(tb, g0 + c0, ps)


def norm_fm(p, src, nch, ntt, a, b, dst, inv_n):
    S = p.S
    for tt in range(ntt):
        ps = p.bank(6, 8)
        for c in range(nch):
            sq = p.scr("sq", [128, 512], BF16, 3)
            s_ = src(c, tt)
            S.op("act", lambda e, o=sq, i=s_: e.activation(out=o.ap, in_=i.ap, func=AF.Square), reads=[s_], writes=[sq])
            S.op("pe", lambda e, ps=ps, sq=sq, c=c: e.matmul(ps.ap, lhsT=p.ones.ap, rhs=sq.ap, start=(c == 0), stop=(c == nch - 1)),
                 reads=[sq, p.ones], writes=[ps])
        r = p.scr("rstd", [128, 512], F32, 2)
        S.op("act", lambda e, r=r, ps=ps: e.activation(out=r.ap, in_=ps.ap, func=AF.Sqrt, bias=EPS, scale=inv_n), reads=[ps], writes=[r])
        S.op("dve", lambda e, r=r: e.reciprocal(out=r.ap, in_=r.ap), reads=[r], writes=[r])
        for c in range(nch):
            tmp = p.scr("ntmp", [128, 512], F32, 3)
            s_ = src(c, tt)
            d_ = dst(c, tt)
            S.op("dve", lambda e, tmp=tmp, s_=s_, r=r: e.tensor_tensor(out=tmp.ap, in0=s_.ap, in1=r.ap, op=ALU.mult), reads=[s_, r], writes=[tmp])
            if b is not None:
                S.op("act", lambda e, d_=d_, tmp=tmp, c=c: e.activation(out=d_.ap, in_=tmp.ap, func=AF.Identity, bias=b.ap[:, c:c + 1], scale=a.ap[:, c:c + 1]),
                     reads=[tmp, a, b], writes=[d_])
            else:
                S.op("act", lambda e, d_=d_, tmp=tmp, c=c: e.activation(out=d_.ap, in_=tmp.ap, func=AF.Identity, bias=0.0, scale=a.ap[:, c:c + 1]),
                     reads=[tmp, a], writes=[d_])


class TL(Prog):
    def __init__(self, resident=True, wslots=3):
        super().__init__(wslots=wslots)
        p = self
        self.resident = resident
        if resident:
            self.xT = FM(p, "xT", 16, T, F32)
        self.hT = FM(p, "hT", 16, T, BF16)
        self.modT = self.S.sb("modT", [128, 384], F32)
        self.ng1 = self.S.sb("ng1", [128, 64], F32)
        self.ng2 = self.S.sb("ng2", [128, 64], F32)
        self.load(self.modT, self.inp("modT", [128, 384], F32))
        self.load(self.ng1, self.inp("ng1T", [128, 64], F32))
        self.load(self.ng2, self.inp("ng2T", [128, 64], F32))

    def load_x(self, name):
        xin = self.inp(name, [D, T], F32)
        if not self.resident:
            def src(c, tt):
                t_ = self.scr("xs", [128, 512], F32, 3)
                self.load(t_, xin[c * 128:(c + 1) * 128, tt * 512:(tt + 1) * 512])
                return t_
            self.x_src = src
            return
        self.x_src = self.xT.v
        for c in range(16):
            self.load(self.xT.vc(c), xin[c * 128:(c + 1) * 128, :])

    def store_x(self, name):
        xo = self.out(name, [D, T], F32)
        for c in range(16):
            self.store(xo[c * 128:(c + 1) * 128, :], self.xT.vc(c))

    def adaln(self, layer, which):
        S = self.S
        base = layer * 96 + which * 48
        ng = self.ng1 if which == 0 else self.ng2
        a = S.sb(f"ada_{layer}_{which}", [128, 16], F32)
        S.op("dve", lambda e: e.scalar_tensor_tensor(out=a.ap, in0=self.modT.ap[:, base + 16:base + 32], scalar=1.0,
                                                      in1=ng.ap[:, layer * 16:(layer + 1) * 16], op0=ALU.add, op1=ALU.mult),
             reads=[self.modT, ng], writes=[a])
        sh = View(self.modT.ap[:, base:base + 16], self.modT.bufs)
        g = View(self.modT.ap[:, base + 32:base + 48], self.modT.bufs)
        return a, sh, g

    def resid_evac(self, g, cpg=4):
        S = self.S

        def ev(gi, si, ci, m, tt, ps, g=g):
            c = gi * cpg + ci
            xv = self.xT.v(c, tt)
            S.op("dve", lambda e: e.scalar_tensor_tensor(out=xv.ap, in0=ps.ap, scalar=g.ap[:, c:c + 1], in1=xv.ap, op0=ALU.mult, op1=ALU.add),
                 reads=[ps, g, xv], writes=[xv])
        return ev

    def ffn(self, layer):
        p, S = self, self.S
        a, sh, g = self.adaln(layer, 1)
        norm_fm(p, self.x_src, 16, 2, a, sh, self.hT.v, 1.0 / D)
        wg = self.inp(f"wg{layer}", [D, DFF], F32)
        wu = self.inp(f"wu{layer}", [D, DFF], F32)
        wd = self.inp(f"wd{layer}", [DFF, D], F32)
        if not hasattr(self, "aT"):
            self.aT = FM(p, "aT", 6, T, BF16)
        aT = self.aT
        parts = [6, 6, 6, 6, 5, 5, 5, 5]
        for q in range(8):
            j0 = sum(parts[:q])
            nj = parts[q]
            sizes = [2, 2, 2] if nj == 6 else [2, 2, 1]
            groups = []
            jj = j0
            gstart = []
            for sz in sizes:
                groups.append([(wg, jj * 128, sz * 128), (wu, jj * 128, sz * 128)])
                gstart.append(jj - j0)
                jj += sz
            sgs = {}

            def ev(gi, si, ci, m, tt, ps):
                jl = gstart[gi] + ci
                if si == 0:
                    sg = p.scr("sg", [128, 512], F32, 5)
                    sgs[(jl, tt)] = sg
                    S.op("act", lambda e: e.activation(out=sg.ap, in_=ps.ap, func=AF.Silu), reads=[ps], writes=[sg])
                else:
                    sg = sgs[(jl, tt)]
                    av = aT.v(jl, tt)
                    S.op("dve", lambda e: e.tensor_tensor(out=av.ap, in0=sg.ap, in1=ps.ap, op=ALU.mult), reads=[sg, ps], writes=[av])
            linear(p, D, groups, self.hT.v, 2, ev, interleave=True)
            wdq = wd[j0 * 128:(j0 + nj) * 128, :]
            linear(p, nj * 128, [[(wdq, n * 1024, 1024)] for n in range(2)], aT.v, 2, self.resid_evac(g, 8))

    def conv1(self, layer, widx):
        p, S = self, self.S
        a, sh, g = self.adaln(layer, 0)
        norm_fm(p, self.x_src, 16, 2, a, sh, self.hT.v, 1.0 / D)
        w = self.inp(f"cw_in{widx}", [D, 3 * D], F32)
        oB = self.out("convB", [D, T], BF16)
        oCU = self.out("convCU", [D, T], F32)
        groups = [[(w, n * 128, 128), (w, D + n * 128, 128), (w, 2 * D + n * 128, 128)] for n in range(16)]
        st = {}

        def ev(gi, si, ci, m, tt, ps):
            cs = slice(tt * 512, (tt + 1) * 512)
            if si == 0:
                if tt == 0:
                    st["B"] = p.scr("stB", [128, T], BF16, 2)
                b_ = st["B"]
                S.op("act", lambda e: e.activation(out=b_.ap[:, cs], in_=ps.ap, func=AF.Copy), reads=[ps], writes=[b_])
                if tt == 1:
                    p.store(oB[gi * 128:(gi + 1) * 128, :], b_)
            elif si == 1:
                c_ = p.scr("cC", [128, 512], F32, 3)
                st[("C", tt)] = c_
                S.op("act", lambda e: e.activation(out=c_.ap, in_=ps.ap, func=AF.Copy), reads=[ps], writes=[c_])
            else:
                if tt == 0:
                    st["CU"] = p.scr("stCU", [128, T], F32, 2)
                cu = st["CU"]
                c_ = st[("C", tt)]
                S.op("dve", lambda e: e.tensor_tensor(out=cu.ap[:, cs], in0=c_.ap, in1=ps.ap, op=ALU.mult), reads=[c_, ps], writes=[cu])
                if tt == 1:
                    p.store(oCU[gi * 128:(gi + 1) * 128, :], cu)
        linear(p, D, groups, self.hT.v, 2, ev)

    def conv2(self, layer, widx):
        p, S = self, self.S
        base = layer * 96
        g = View(self.modT.ap[:, base + 32:base + 48], self.modT.bufs)
        iB = self.inp("convB_in", [D, T], BF16)
        iCU = self.inp("convCU_in", [D, T], F32)
        ihalo = self.inp("halo", [D, 2], F32)
        icw = self.inp(f"cw{widx}", [128, 48], F32)
        wout = self.inp(f"cw_out{widx}", [D, D], F32)
        cw = S.sb("convw", [128, 48], F32)
        self.load(cw, icw)
        for c in range(16):
            cu = p.scr("cu_in", [128, T + 2], F32, 2)
            bc = p.scr("b_in", [128, T], BF16, 2)
            S.dma("sp", View(cu.ap[:, 0:2], cu.bufs), View(ihalo[c * 128:(c + 1) * 128, :], []),
                  extra=[(View(cu.ap[:, 2:], cu.bufs), View(iCU[c * 128:(c + 1) * 128, :], []))])
            self.load(bc, iB[c * 128:(c + 1) * 128, :])
            z = p.scr("convz", [128, T], F32, 2)
            S.op("dve", lambda e, z=z, cu=cu, c=c: e.tensor_scalar(out=z.ap, in0=cu.ap[:, 2:2 + T], scalar1=cw.ap[:, 32 + c:33 + c], scalar2=None, op0=ALU.mult),
                 reads=[cu, cw], writes=[z])
            S.op("dve", lambda e, z=z, cu=cu, c=c: e.scalar_tensor_tensor(out=z.ap, in0=cu.ap[:, 1:1 + T], scalar=cw.ap[:, 16 + c:17 + c], in1=z.ap, op0=ALU.mult, op1=ALU.add),
                 reads=[cu, cw, z], writes=[z])
            S.op("dve", lambda e, z=z, cu=cu, c=c: e.scalar_tensor_tensor(out=z.ap, in0=cu.ap[:, 0:T], scalar=cw.ap[:, c:c + 1], in1=z.ap, op0=ALU.mult, op1=ALU.add),
                 reads=[cu, cw, z], writes=[z])
            hv = self.hT.vc(c)
            S.op("pool", lambda e, z=z, bc=bc, hv=hv: e.tensor_tensor(out=hv.ap, in0=z.ap, in1=bc.ap, op=ALU.mult), reads=[z, bc], writes=[hv])
        linear(p, D, [[(wout, n * 512, 512)] for n in range(4)], self.hT.v, 2, self.resid_evac(g))

    def final(self):
        p, S = self, self.S
        fg = S.sb("fng", [128, 16], F32)
        self.load(fg, self.inp("fngT", [128, 16], F32))
        oo = self.out("outT", [D, T], F32)
        for tt in range(2):
            ps = p.bank(6, 8)
            for c in range(16):
                sq = p.scr("sq", [128, 512], BF16, 3)
                s_ = self.xT.v(c, tt)
                S.op("act", lambda e, o=sq, i=s_: e.activation(out=o.ap, in_=i.ap, func=AF.Square), reads=[s_], writes=[sq])
                S.op("pe", lambda e, ps=ps, sq=sq, c=c: e.matmul(ps.ap, lhsT=p.ones.ap, rhs=sq.ap, start=(c == 0), stop=(c == 15)),
                     reads=[sq, p.ones], writes=[ps])
            r = p.scr("rstd", [128, 512], F32, 2)
            S.op("act", lambda e, r=r, ps=ps: e.activation(out=r.ap, in_=ps.ap, func=AF.Sqrt, bias=EPS, scale=1.0 / D), reads=[ps], writes=[r])
            S.op("dve", lambda e, r=r: e.reciprocal(out=r.ap, in_=r.ap), reads=[r], writes=[r])
            for c in range(16):
                s_ = self.xT.v(c, tt)
                d_ = p.scr("ntmp", [128, 512], F32, 3)
                S.op("dve", lambda e, d_=d_, s_=s_, r=r, c=c: e.scalar_tensor_tensor(out=d_.ap, in0=s_.ap, scalar=fg.ap[:, c:c + 1], in1=r.ap, op0=ALU.mult, op1=ALU.mult),
                     reads=[s_, r, fg], writes=[d_])
                p.store(oo[c * 128:(c + 1) * 128, tt * 512:(tt + 1) * 512], d_)

    def evac_store(self, name, out_ap, dt, row_of, scale=None, eng="act"):
        p, S = self, self.S
        st = {}

        def ev(gi, si, ci, m, tt, ps):
            cs = slice(tt * 512, (tt + 1) * 512)
            if tt == 0:
                st["t"] = p.scr("st_" + name, [128, T], dt, 2)
            t_ = st["t"]
            if scale is None:
                S.op("act", lambda e: e.activation(out=t_.ap[:m, cs], in_=ps.ap[:m, :], func=AF.Copy), reads=[ps], writes=[t_])
            else:
                S.op("act", lambda e: e.activation(out=t_.ap[:m, cs], in_=ps.ap[:m, :], func=AF.Copy, scale=scale), reads=[ps], writes=[t_])
            if tt == 1:
                r0 = row_of(gi, si, ci)
                p.store(out_ap[r0:r0 + m, :], View(t_.ap[:m, :], t_.bufs))
        return ev

    def rope_tables(self):
        p, S = self, self.S
        posi = S.sb("posi", [128, T], I32)
        self.load(posi, self.inp("pos_rep", [128, T], I32))
        rc = S.sb("ropec", [128, 2], F32)
        self.load(rc, self.inp("ropec", [128, 2], F32))
        ang = S.sb("ang", [128, T], F32)
        S.op("dve", lambda e: e.tensor_copy(out=ang.ap, in_=posi.ap), reads=[posi], writes=[ang])
        S.op("dve", lambda e: e.tensor_scalar(out=ang.ap, in0=ang.ap, scalar1=rc.ap[:, 0:1], scalar2=None, op0=ALU.mult), reads=[ang, rc], writes=[ang])
        C1 = 6.28125
        C2 = 2 * math.pi - C1
        outs = []
        for name, shift in (("sin", 0.0), ("cos", math.pi / 2)):
            y = S.sb("rt_" + name, [128, T], F32)
            ni = p.scr("rt_ni", [128, T], I32, 1)
            nf = p.scr("rt_nf", [128, T], F32, 1)
            S.op("dve", lambda e, y=y, shift=shift: e.tensor_scalar(out=y.ap, in0=ang.ap, scalar1=shift, scalar2=None, op0=ALU.add), reads=[ang], writes=[y])
            S.op("dve", lambda e, y=y, nf=nf: e.tensor_scalar(out=nf.ap, in0=y.ap, scalar1=1.0 / (2 * math.pi), scalar2=None, op0=ALU.mult), reads=[y], writes=[nf])
            S.op("dve", lambda e, ni=ni, nf=nf: e.tensor_copy(out=ni.ap, in_=nf.ap), reads=[nf], writes=[ni])
            S.op("dve", lambda e, ni=ni, nf=nf: e.tensor_copy(out=nf.ap, in_=ni.ap), reads=[ni], writes=[nf])
            S.op("dve", lambda e, y=y, nf=nf: e.scalar_tensor_tensor(out=y.ap, in0=nf.ap, scalar=-C1, in1=y.ap, op0=ALU.mult, op1=ALU.add), reads=[nf, y], writes=[y])
            S.op("dve", lambda e, y=y, nf=nf: e.scalar_tensor_tensor(out=y.ap, in0=nf.ap, scalar=-C2, in1=y.ap, op0=ALU.mult, op1=ALU.add), reads=[nf, y], writes=[y])
            S.op("dve", lambda e, y=y: e.tensor_scalar(out=y.ap, in0=y.ap, scalar1=math.pi, scalar2=-math.pi, op0=ALU.min, op1=ALU.max), reads=[y], writes=[y])
            S.op("act", lambda e, y=y: e.activation(out=y.ap, in_=y.ap, func=AF.Sin), reads=[y], writes=[y])
            outs.append(y)
        sin, cos = outs
        S.op("dve", lambda e: e.tensor_scalar(out=sin.ap, in0=sin.ap, scalar1=rc.ap[:, 1:2], scalar2=-1.0, op0=ALU.mult, op1=ALU.mult), reads=[sin, rc], writes=[sin])
        self.cos2, self.sin2s = cos, sin

    def rope_evac(self, name, out_ap, row_of):
        p, S = self, self.S
        st = {}

        def ev(gi, si, ci, m, tt, ps):
            cs = slice(tt * 512, (tt + 1) * 512)
            if si == 0:
                t1 = p.scr("rp_t1", [128, 512], F32, 4)
                st[(ci, tt)] = t1
                S.op("dve", lambda e: e.tensor_tensor(out=t1.ap[:m, :], in0=ps.ap[:m, :], in1=self.cos2.ap[:m, cs], op=ALU.mult), reads=[ps, self.cos2], writes=[t1])
            else:
                t1 = st[(ci, tt)]
                t2 = p.scr("rp_t2", [128, 512], F32, 2)
                S.op("dve", lambda e: e.tensor_tensor(out=t2.ap[:m, :], in0=ps.ap[:m, :], in1=self.sin2s.ap[:m, cs], op=ALU.mult), reads=[ps, self.sin2s], writes=[t2])
                if tt == 0:
                    st["o"] = p.scr("st_" + name, [128, T], BF16, 2)
                o_ = st["o"]
                S.op("pool", lambda e: e.tensor_tensor(out=o_.ap[:m, cs], in0=t1.ap[:m, :], in1=t2.ap[:m, :], op=ALU.add), reads=[t1, t2], writes=[o_])
                if tt == 1:
                    r0 = row_of(gi, ci)
                    p.store(out_ap[r0:r0 + m, :], View(o_.ap[:m, :], o_.bufs))
        return ev

    def mla_pre(self):
        p, S = self, self.S
        layer = 1
        a, sh, g = self.adaln(layer, 0)
        norm_fm(p, self.x_src, 16, 2, a, sh, self.hT.v, 1.0 / D)
        self.rope_tables()
        w_dq = self.inp("w_dq", [D, 768], F32)
        w_uqn = self.inp("w_uqn", [768, 2048], F32)
        w_uqp = self.inp("w_uqp", [768, 1024], F32)
        w_uqps = self.inp("w_uqps", [768, 1024], F32)
        w_dkvc = self.inp("w_dkvc", [D, 512], F32)
        w_dkvp = self.inp("w_dkvp", [D, 64], F32)
        w_dkvps = self.inp("w_dkvps", [D, 64], F32)
        w_uk = self.inp("w_uk", [512, 2048], F32)
        w_uv = self.inp("w_uv", [512, 2048], F32)
        qng = S.sb("qng", [128, 6], F32)
        kvng = S.sb("kvng", [128, 4], F32)
        self.load(qng, self.inp("qngT", [128, 6], F32))
        self.load(kvng, self.inp("kvngT", [128, 4], F32))
        o_qn = self.out("qnT", [2048, T], BF16)
        o_qp = self.out("qpT", [1024, T], BF16)
        o_kn = self.out("knT", [2048, T], BF16)
        o_kp = self.out("kpT", [64, T], BF16)
        o_v = self.out("v_tm", [T, 2048], BF16)
        cqpre = FM(p, "cqpre", 6, T, F32)
        cq = FM(p, "cq", 6, T, BF16)

        def ev_pre(dstfm, cbase):
            def ev(gi, si, ci, m, tt, ps):
                d_ = dstfm.v(cbase(gi) + ci, tt)
                S.op("act", lambda e: e.activation(out=d_.ap, in_=ps.ap, func=AF.Copy), reads=[ps], writes=[d_])
            return ev
        linear(p, D, [[(w_dq, 0, 512)], [(w_dq, 512, 256)]], self.hT.v, 2, ev_pre(cqpre, lambda gi: gi * 4))
        norm_fm(p, cqpre.v, 6, 2, qng, None, cq.v, 1.0 / 768)
        linear(p, 768, [[(w_uqn, n * 1024, 1024)] for n in range(2)], cq.v, 2,
               self.evac_store("qn", o_qn, BF16, lambda gi, si, ci: gi * 1024 + ci * 128))
        linear(p, 768, [[(w_uqp, n * 512, 512), (w_uqps, n * 512, 512)] for n in range(2)], cq.v, 2,
               self.rope_evac("qp", o_qp, lambda gi, ci: gi * 512 + ci * 128), interleave=True)
        linear(p, D, [[(w_dkvc, 0, 512)]], self.hT.v, 2, ev_pre(cqpre, lambda gi: 0))
        norm_fm(p, cqpre.v, 4, 2, kvng, None, cq.v, 1.0 / 512)
        linear(p, D, [[(w_dkvp, 0, 64), (w_dkvps, 0, 64)]], self.hT.v, 2,
               self.rope_evac("kp", o_kp, lambda gi, ci: 0), interleave=True)
        linear(p, 512, [[(w_uk, 0, 2048)]], cq.v, 2,
               self.evac_store("kn", o_kn, BF16, lambda gi, si, ci: ci * 128))

        def ev_v(tb, c0, ps):
            t_ = p.scr("st_v", [128, 512], BF16, 3)
            S.op("act", lambda e: e.activation(out=t_.ap, in_=ps.ap, func=AF.Copy), reads=[ps], writes=[t_])
            p.store(o_v[tb * 128:(tb + 1) * 128, c0:c0 + 512], t_)
        linear_tm(p, 512, w_uv, 2048, lambda k, tb: View(cq.t[:, k, tb * 128:(tb + 1) * 128], [cq.b[k][tb // 4]]), 8, ev_v)

    def attn_out(self):
        p, S = self, self.S
        layer = 1
        base = layer * 96
        g = View(self.modT.ap[:, base + 32:base + 48], self.modT.bufs)
        ia = self.inp("attT", [D, T], BF16)
        wo = self.inp("w_o", [D, D], F32)
        for c in range(16):
            self.load(self.hT.vc(c), ia[c * 128:(c + 1) * 128, :])
        linear(p, D, [[(wo, n * 512, 512)] for n in range(4)], self.hT.v, 2, self.resid_evac(g))

    def mlstm_pre(self):
        p, S = self, self.S
        layer = 2
        a, sh, g = self.adaln(layer, 0)
        norm_fm(p, self.x_src, 16, 2, a, sh, self.hT.v, 1.0 / D)
        wq = self.inp("m_wq", [D, 1024], F32)
        wk = self.inp("m_wk", [D, 1024], F32)
        wv = self.inp("m_wv", [D, 2048], F32)
        wo = self.inp("m_wo", [D, 2048], F32)
        wgt = self.inp("m_wg", [D, 8], F32)
        bg = S.sb("m_bg", [8, 1], F32)
        self.load(bg, self.inp("m_bg", [8, 1], F32))
        o_q = self.out("mqT", [1024, T], BF16)
        o_k = self.out("mkT", [1024, T], BF16)
        o_ktm = self.out("mk_tm", [T, 1024], BF16)
        o_vtm = self.out("mv_tm", [T, 2048], BF16)
        o_gi = self.out("gi", [8, T], F32)
        o_gf = self.out("gf", [8, T], F32)
        o_so = self.out("sigoT", [D, T], BF16)
        linear(p, D, [[(wq, n * 512, 512)] for n in range(2)], self.hT.v, 2,
               self.evac_store("mq", o_q, BF16, lambda gi, si, ci: gi * 512 + ci * 128))
        linear(p, D, [[(wk, n * 512, 512)] for n in range(2)], self.hT.v, 2,
               self.evac_store("mk", o_k, BF16, lambda gi, si, ci: gi * 512 + ci * 128, scale=1.0 / 16))
        st = {}

        def ev_so(gi, si, ci, m, tt, ps):
            cs = slice(tt * 512, (tt + 1) * 512)
            if tt == 0:
                st["t"] = p.scr("st_so", [128, T], BF16, 2)
            t_ = st["t"]
            S.op("act", lambda e: e.activation(out=t_.ap[:, cs], in_=ps.ap, func=AF.Sigmoid), reads=[ps], writes=[t_])
            if tt == 1:
                r0 = gi * 512 + ci * 128
                p.store(o_so[r0:r0 + 128, :], t_)
        linear(p, D, [[(wo, n * 512, 512)] for n in range(4)], self.hT.v, 2, ev_so)
        gst = {}

        def ev_g(gi, si, ci, m, tt, ps):
            cs = slice(tt * 512, (tt + 1) * 512)
            if tt == 0:
                gst["i"] = S.sb("g_i", [8, T], F32)
                gst["f"] = S.sb("g_f", [8, T], F32)
            gi_, gf_ = gst["i"], gst["f"]
            S.op("act", lambda e: e.activation(out=gi_.ap[:, cs], in_=ps.ap[:8, :], func=AF.Identity, bias=bg.ap[:, 0:1], scale=1.0), reads=[ps, bg], writes=[gi_])
            S.op("act", lambda e: e.activation(out=gi_.ap[:, cs], in_=gi_.ap[:, cs], func=AF.Tanh, scale=1.0 / 15), reads=[gi_], writes=[gi_])
            S.op("dve", lambda e: e.tensor_scalar(out=gi_.ap[:, cs], in0=gi_.ap[:, cs], scalar1=15.0, scalar2=None, op0=ALU.mult), reads=[gi_], writes=[gi_])
            S.op("act", lambda e: e.activation(out=gf_.ap[:, cs], in_=gi_.ap[:, cs], func=AF.Exp, scale=-1.0), reads=[gi_], writes=[gf_])
            S.op("act", lambda e: e.activation(out=gf_.ap[:, cs], in_=gf_.ap[:, cs], func=AF.Ln, bias=1.0, scale=1.0), reads=[gf_], writes=[gf_])
            S.op("dve", lambda e: e.tensor_scalar(out=gf_.ap[:, cs], in0=gf_.ap[:, cs], scalar1=-1.0, scalar2=None, op0=ALU.mult), reads=[gf_], writes=[gf_])
            if tt == 1:
                p.store(o_gi, gi_)
                p.store(o_gf, gf_)
        linear(p, D, [[(wgt, 0, 8)]], self.hT.v, 2, ev_g)

        def lhs(k, tb):
            return View(self.hT.t[:, k, tb * 128:(tb + 1) * 128], [self.hT.b[k][tb // 4]])

        def ev_k(tb, c0, ps):
            t_ = p.scr("st_v", [128, 512], BF16, 3)
            S.op("act", lambda e: e.activation(out=t_.ap, in_=ps.ap, func=AF.Copy, scale=1.0 / 16), reads=[ps], writes=[t_])
            p.store(o_ktm[tb * 128:(tb + 1) * 128, c0:c0 + 512], t_)

        def ev_v(tb, c0, ps):
            t_ = p.scr("st_v", [128, 512], BF16, 3)
            S.op("act", lambda e: e.activation(out=t_.ap, in_=ps.ap, func=AF.Copy), reads=[ps], writes=[t_])
            p.store(o_vtm[tb * 128:(tb + 1) * 128, c0:c0 + 512], t_)
        linear_tm(p, D, wk, 1024, lhs, 8, ev_k)
        linear_tm(p, D, wv, 2048, lhs, 8, ev_v)

    def mlstm_post(self):
        p, S = self, self.S
        layer = 2
        base = layer * 96
        g = View(self.modT.ap[:, base + 32:base + 48], self.modT.bufs)
        ih = self.inp("mhT", [D, T], BF16)
        iso = self.inp("sigoT_in", [D, T], BF16)
        wout = self.inp("m_wout", [D, D], F32)
        hng = S.sb("hng", [128, 16], F32)
        self.load(hng, self.inp("hngT", [128, 16], F32))
        def hsrc(hd):
            def src(c, tt):
                t_ = p.scr("mh_s", [128, 512], BF16, 3)
                cc = hd * 4 + c
                self.load(t_, ih[cc * 128:(cc + 1) * 128, tt * 512:(tt + 1) * 512])
                return t_
            return src
        for hd in range(4):
            hv = View(hng.ap[:, hd * 4:(hd + 1) * 4], hng.bufs)
            norm_fm(p, hsrc(hd), 4, 2, hv, None, (lambda hd: (lambda c, tt: self.hT.v(hd * 4 + c, tt)))(hd), 1.0 / 512)
        for c in range(16):
            so = p.scr("so_in", [128, T], BF16, 2)
            self.load(so, iso[c * 128:(c + 1) * 128, :])
            hv = self.hT.vc(c)
            S.op("pool", lambda e, hv=hv, so=so: e.tensor_tensor(out=hv.ap, in0=hv.ap, in1=so.ap, op=ALU.mult), reads=[hv, so], writes=[hv])
        linear(p, D, [[(wout, n * 512, 512)] for n in range(4)], self.hT.v, 2, self.resid_evac(g))


def build_mods():
    p = Prog(wslots=2, slot_elems=16 * 512)
    S = p.S
    ic = p.inp("c_col", [128, 16], F32)
    iw = p.inp("mod_w_s", [4, D, 1536], F32)
    ib = p.inp("mod_b_s", [1, 4 * 1536], F32)
    oo = p.out("modT_s", [128, 48], F32)
    cc = S.sb("cc", [128, 16], F32)
    p.load(cc, ic)
    cb = S.sb("cb", [128, 16], BF16)
    S.op("act", lambda e: e.activation(out=cb.ap, in_=cc.ap, func=AF.Silu), reads=[cc], writes=[cb])
    mb = S.sb("mb", [1, 4 * 1536], F32)
    p.load(mb, ib)
    row = S.sb("row", [1, 4 * 1536], F32)
    one1 = S.sb("one1", [1, 1], F32)
    S.op("dve", lambda e: e.memset(one1.ap, 1.0), writes=[one1])
    res = S.sb("res", [128, 48], F32)
    pst = p.banks[7]
    for l in range(4):
        for g in range(3):
            slot = p.wslot()
            sv3 = slot.ap.rearrange("p (k n) -> p k n", n=512)
            S.dma("pool", View(sv3, slot.bufs), View(iw[l, :, g * 512:(g + 1) * 512].rearrange("(k p) n -> p k n", p=128), []))
            ps = p.bank()
            for k in range(16):
                S.op("pe", lambda e, ps=ps, k=k, sv3=sv3: e.matmul(ps.ap[0:1, :], lhsT=cb.ap[:, k:k + 1], rhs=sv3[:, k, :], start=(k == 0), stop=(k == 15)),
                     reads=[slot, cb], writes=[ps])
            c0 = l * 1536 + g * 512
            S.op("dve", lambda e, ps=ps, c0=c0: e.tensor_tensor(out=row.ap[:, c0:c0 + 512], in0=ps.ap[0:1, :], in1=mb.ap[:, c0:c0 + 512], op=ALU.add),
                 reads=[ps, mb], writes=[row])
    for j in range(48):
        S.op("pe", lambda e, j=j: e.matmul(pst.ap[:, j:j + 1], lhsT=row.ap[0:1, j * 128:(j + 1) * 128], rhs=one1.ap, start=True, stop=True),
             reads=[row, one1], writes=[pst])
    S.op("dve", lambda e: e.tensor_copy(out=res.ap, in_=pst.ap[:, 0:48]), reads=[pst], writes=[res])
    p.store(oo, res)
    return p


def build_attn():
    p = Prog(wslots=1, slot_elems=16)
    S = p.S
    scale = 192 ** -0.5
    iqn = p.inp("qn", [2, 128, SEQ], BF16)
    iqp = p.inp("qp", [2, 64, SEQ], BF16)
    ikn = p.inp("kn", [2, 128, SEQ], BF16)
    ikp = p.inp("kp", [64, SEQ], BF16)
    iv = p.inp("v", [SEQ, 256], BF16)
    imask = p.inp("cmask", [128, 4 * 512], BF16)
    oat = p.out("att", [256, SEQ], BF16)
    NT = SEQ // 512
    kn = [[S.sb(f"kn{h}_{t}", [128, 512], BF16) for t in range(NT)] for h in range(2)]
    kp = [S.sb(f"kp_{t}", [64, 512], BF16) for t in range(NT)]
    vv = [S.sb(f"v_{t}", [128, 4, 256], BF16) for t in range(NT)]
    mask = S.sb("cmask", [128, 4 * 512], BF16)
    p.load(mask, imask)
    for t in range(NT):
        cs = slice(t * 512, (t + 1) * 512)
        for h in range(2):
            p.load(kn[h][t], ikn[h, :, cs])
        p.load(kp[t], ikp[:, cs])
        p.load(vv[t], iv[cs, :].rearrange("(b p) d -> p b d", p=128))
    ones = p.ones
    bias = []
    for h in range(2):
        mx = {}
        for nm in ("k", "q"):
            m_ = S.sb(f"mx{nm}{h}", [128, 1], F32)
            S.op("dve", lambda e, m_=m_: e.memset(m_.ap, 0.0), writes=[m_])
            mx[nm] = m_
        for t in range(NT):
            cs = slice(t * 512, (t + 1) * 512)
            for nm in ("k", "q"):
                if nm == "k":
                    a_, b_ = kn[h][t], kp[t]
                else:
                    a_ = p.scr("qn_n", [128, 512], BF16, 2)
                    b_ = p.scr("qp_n", [64, 512], BF16, 2)
                    p.load(a_, iqn[h, :, cs])
                    p.load(b_, iqp[h, :, cs])
                s1 = p.scr("nsq1", [128, 512], BF16, 2)
                s2 = p.scr("nsq2", [64, 512], BF16, 2)
                S.op("act", lambda e, s1=s1, a_=a_: e.activation(out=s1.ap, in_=a_.ap, func=AF.Square), reads=[a_], writes=[s1])
                S.op("act", lambda e, s2=s2, b_=b_: e.activation(out=s2.ap, in_=b_.ap, func=AF.Square), reads=[b_], writes=[s2])
                ps = p.bank(6, 8)
                S.op("pe", lambda e, ps=ps, s1=s1: e.matmul(ps.ap, lhsT=ones.ap, rhs=s1.ap, start=True, stop=False), reads=[s1, ones], writes=[ps])
                S.op("pe", lambda e, ps=ps, s2=s2: e.matmul(ps.ap, lhsT=ones.ap[:64, :], rhs=s2.ap, start=False, stop=True), reads=[s2, ones], writes=[ps])
                tm = p.scr("nmx", [128, 1], F32, 2)
                S.op("dve", lambda e, tm=tm, ps=ps: e.reduce_max(out=tm.ap, in_=ps.ap, axis=mybir.AxisListType.X), reads=[ps], writes=[tm])
                m_ = mx[nm]
                S.op("dve", lambda e, tm=tm, m_=m_: e.tensor_tensor(out=m_.ap, in0=m_.ap, in1=tm.ap, op=ALU.max), reads=[tm, m_], writes=[m_])
        bb = S.sb(f"bias{h}", [128, 1], F32)
        S.op("dve", lambda e, bb=bb, mx=mx: e.tensor_tensor(out=bb.ap, in0=mx["k"].ap, in1=mx["q"].ap, op=ALU.mult), reads=[mx["k"], mx["q"]], writes=[bb])
        S.op("act", lambda e, bb=bb: e.activation(out=bb.ap, in_=bb.ap, func=AF.Sqrt), reads=[bb], writes=[bb])
        S.op("dve", lambda e, bb=bb: e.tensor_scalar(out=bb.ap, in0=bb.ap, scalar1=-scale, scalar2=None, op0=ALU.mult), reads=[bb], writes=[bb])
        bias.append(bb)
    onesf = S.sb("onesf", [128, 128], F32)
    S.op("dve", lambda e: e.memset(onesf.ap, 1.0), writes=[onesf])

    def qtile(h, qi):
        cs = slice(qi * 512, (qi + 1) * 512)
        qn = p.scr("qn_m", [128, 512], BF16, 2)
        qp = p.scr("qp_m", [64, 512], BF16, 2)
        p.load(qn, iqn[h, :, cs])
        p.load(qp, iqp[h, :, cs])
        po = p.bank(4, 6)
        psum_ = p.bank(6, 8)
        nkb = 4 * (qi + 1)
        bh = bias[h]
        def emit_s(kb):
            t, r = kb // 4, kb % 4
            ks = slice(r * 128, (r + 1) * 128)
            ps = p.bank(0, 4)
            knt, kpt = kn[h][t], kp[t]
            S.op("pe", lambda e, ps=ps, knt=knt, ks=ks: e.matmul(ps.ap, lhsT=knt.ap[:, ks], rhs=qn.ap, start=True, stop=False),
                 reads=[knt, qn], writes=[ps])
            S.op("pe", lambda e, ps=ps, kpt=kpt, ks=ks: e.matmul(ps.ap, lhsT=kpt.ap[:, ks], rhs=qp.ap, start=False, stop=True),
                 reads=[kpt, qp], writes=[ps])
            pt = p.scr("pT", [128, 512], BF16, 6)
            S.op("act", lambda e, pt=pt, ps=ps: e.activation(out=pt.ap, in_=ps.ap, func=AF.Exp, bias=bh.ap[:, 0:1], scale=scale),
                 reads=[ps, bh], writes=[pt])
            if t == qi:
                S.op("pool", lambda e, pt=pt, r=r: e.tensor_tensor(out=pt.ap, in0=pt.ap, in1=mask.ap[:, r * 512:(r + 1) * 512], op=ALU.mult),
                     reads=[pt, mask], writes=[pt])
            return pt

        LOOK = 3
        racc = p.scr("racc", [128, 512], F32, 2)
        pts = {}
        for kb in range(min(LOOK, nkb)):
            pts[kb] = emit_s(kb)
        for kb in range(nkb):
            if kb + LOOK < nkb:
                pts[kb + LOOK] = emit_s(kb + LOOK)
            pt = pts.pop(kb)
            t, r = kb // 4, kb % 4
            vt_ = vv[t]
            S.op("pe", lambda e, pt=pt, vt_=vt_, r=r, kb=kb: e.matmul(po.ap, lhsT=vt_.ap[:, r, h * 128:(h + 1) * 128], rhs=pt.ap, start=(kb == 0), stop=(kb == nkb - 1)),
                 reads=[vt_, pt], writes=[po])
            if kb == 0:
                S.op("pool", lambda e, pt=pt: e.tensor_copy(out=racc.ap, in_=pt.ap), reads=[pt], writes=[racc])
            else:
                S.op("pool", lambda e, pt=pt: e.tensor_tensor(out=racc.ap, in0=racc.ap, in1=pt.ap, op=ALU.add), reads=[pt, racc], writes=[racc])
        S.op("pe", lambda e: e.matmul(psum_.ap, lhsT=onesf.ap, rhs=racc.ap, start=True, stop=True), reads=[onesf, racc], writes=[psum_])
        rs = p.scr("rs", [128, 512], F32, 2)
        S.op("dve", lambda e: e.reciprocal(out=rs.ap, in_=psum_.ap), reads=[psum_], writes=[rs])
        ot = p.scr("ot", [128, 512], BF16, 2)
        S.op("dve", lambda e: e.tensor_tensor(out=ot.ap, in0=po.ap, in1=rs.ap, op=ALU.mult), reads=[po, rs], writes=[ot])
        p.store(oat[h * 128:(h + 1) * 128, cs], ot)

    for h in range(2):
        for qi in range(NT):
            qtile(h, qi)
    return p


def build_mlstm():
    p = Prog(wslots=1, slot_elems=16)
    S = p.S
    NCH = SEQ // 128
    iq = p.inp("qT", [256, SEQ], BF16)
    ik = p.inp("kT", [256, SEQ], BF16)
    iktm = p.inp("k_tm", [SEQ, 256], BF16)
    ivtm = p.inp("v_tm", [SEQ, 256], BF16)
    ia = p.inp("lf", [128, NCH], F32)
    ii = p.inp("ig", [128, NCH], F32)
    itri = p.inp("tri", [128, 128], F32)
    oh = p.out("h_tm", [SEQ, 256], BF16)
    qT = [S.sb(f"qT{d}", [128, SEQ], BF16) for d in range(2)]
    kT = [S.sb(f"kT{d}", [128, SEQ], BF16) for d in range(2)]
    G = 8
    ktm = [S.sb(f"ktm{g}", [128, G, 256], BF16) for g in range(NCH // G)]
    vtm = [S.sb(f"vtm{g}", [128, G, 257], BF16) for g in range(NCH // G)]
    for d in range(2):
        for hf in range(4):
            cs = slice(hf * 2048, (hf + 1) * 2048)
            S.dma("sp", View(qT[d].ap[:, cs], qT[d].bufs), View(iq[d * 128:(d + 1) * 128, cs], []))
            S.dma("sp", View(kT[d].ap[:, cs], kT[d].bufs), View(ik[d * 128:(d + 1) * 128, cs], []))
    for g in range(NCH // G):
        rs_ = slice(g * G * 128, (g + 1) * G * 128)
        p.load(ktm[g], iktm[rs_, :].rearrange("(b p) d -> p b d", p=128))
        S.op("pool", lambda e, g=g: e.memset(vtm[g].ap[:, :, 256:257], 1.0), writes=[vtm[g]])
        S.dma("sp", View(vtm[g].ap[:, :, 0:256], vtm[g].bufs), View(ivtm[rs_, :].rearrange("(b p) d -> p b d", p=128), []))
    a = S.sb("lf", [128, NCH], F32)
    ig = S.sb("ig", [128, NCH], F32)
    tri = S.sb("tri", [128, 128], F32)
    tri_b = S.sb("tri_b", [128, 128], BF16)
    onesf = S.sb("onesf", [128, 128], F32)
    p.load(a, ia)
    p.load(ig, ii)
    p.load(tri, itri)
    S.op("dve", lambda e: e.memset(onesf.ap, 1.0), writes=[onesf])
    F_ = S.sb("F", [128, NCH], F32)
    FL = S.sb("FL", [128, NCH], F32)
    ps = p.bank(6, 8)
    S.op("pe", lambda e: e.matmul(ps.ap[:, :NCH], lhsT=tri.ap, rhs=a.ap, start=True, stop=True), reads=[tri, a], writes=[ps])
    S.op("dve", lambda e: e.tensor_copy(out=F_.ap, in_=ps.ap[:, :NCH]), reads=[ps], writes=[F_])
    ps2 = p.bank(6, 8)
    S.op("pe", lambda e: e.matmul(ps2.ap[:, :NCH], lhsT=onesf.ap, rhs=a.ap, start=True, stop=True), reads=[onesf, a], writes=[ps2])
    S.op("dve", lambda e: e.tensor_copy(out=FL.ap, in_=ps2.ap[:, :NCH]), reads=[ps2], writes=[FL])
    imF = S.sb("imF", [128, NCH], F32)
    S.op("dve", lambda e: e.tensor_tensor(out=imF.ap, in0=ig.ap, in1=F_.ap, op=ALU.subtract), reads=[ig, F_], writes=[imF])
    w_ = S.sb("w_s", [128, NCH], F32)
    S.op("dve", lambda e: e.tensor_tensor(out=w_.ap, in0=imF.ap, in1=FL.ap, op=ALU.add), reads=[imF, FL], writes=[w_])
    S.op("act", lambda e: e.activation(out=w_.ap, in_=w_.ap, func=AF.Exp), reads=[w_], writes=[w_])
    dec = S.sb("decay", [128, NCH], F32)
    S.op("act", lambda e: e.activation(out=dec.ap, in_=FL.ap, func=AF.Exp), reads=[FL], writes=[dec])
    C = [S.sb(f"C{d}", [128, 257], F32) for d in range(2)]
    Cb = [S.sb(f"Cb{d}", [128, 257], BF16) for d in range(2)]
    for d in range(2):
        S.op("dve", lambda e, d=d: e.memset(C[d].ap, 0.0), writes=[C[d]])
        S.op("pool", lambda e, d=d: e.memset(Cb[d].ap, 0.0), writes=[Cb[d]])
    hout = None
    for c in range(NCH):
        cs = slice(c * 128, (c + 1) * 128)
        g, gl = c // G, c % G
        ta = p.scr("ta", [128, 128], F32, 2)
        S.op("pool", lambda e, ta=ta, c=c: e.tensor_scalar(out=ta.ap, in0=tri.ap, scalar1=a.ap[:, c:c + 1], scalar2=None, op0=ALU.mult), reads=[tri, a], writes=[ta])
        pf = p.bank(6, 8)
        S.op("pe", lambda e, pf=pf, ta=ta: e.matmul(pf.ap[:, :128], lhsT=onesf.ap, rhs=ta.ap, start=True, stop=True), reads=[onesf, ta], writes=[pf])
        z = p.scr("z", [128, 128], F32, 2)
        S.op("dve", lambda e, z=z, pf=pf, c=c: e.tensor_scalar(out=z.ap, in0=pf.ap[:, :128], scalar1=imF.ap[:, c:c + 1], scalar2=20.0, op0=ALU.add, op1=ALU.min),
             reads=[pf, imF], writes=[z])
        S.op("act", lambda e, z=z: e.activation(out=z.ap, in_=z.ap, func=AF.Exp), reads=[z], writes=[z])
        dm = p.scr("dm", [128, 128], F32, 2)
        S.op("pool", lambda e, dm=dm, z=z: e.tensor_tensor(out=dm.ap, in0=z.ap, in1=tri.ap, op=ALU.mult), reads=[z, tri], writes=[dm])
        ef = p.scr("ef", [128, 128], F32, 2)
        S.op("act", lambda e, ef=ef, pf=pf: e.activation(out=ef.ap, in_=pf.ap[:, :128], func=AF.Exp), reads=[pf], writes=[ef])
        qt = p.scr("qtil", [128, 2, 128], BF16, 2)
        for d in range(2):
            S.op("pool", lambda e, qt=qt, d=d, ef=ef, cs=cs: e.tensor_tensor(out=qt.ap[:, d, :], in0=qT[d].ap[:, cs], in1=ef.ap, op=ALU.mult), reads=[qT[d], ef], writes=[qt])
        kw = p.scr("kw", [128, 256], BF16, 2)
        S.op("pool", lambda e, kw=kw, g=g, gl=gl, c=c: e.tensor_scalar(out=kw.ap, in0=ktm[g].ap[:, gl, :], scalar1=w_.ap[:, c:c + 1], scalar2=None, op0=ALU.mult),
             reads=[ktm[g], w_], writes=[kw])
        pS = p.bank(0, 2)
        for d in range(2):
            S.op("pe", lambda e, pS=pS, d=d, cs=cs: e.matmul(pS.ap[:, :128], lhsT=kT[d].ap[:, cs], rhs=qT[d].ap[:, cs], start=(d == 0), stop=(d == 1)),
                 reads=[kT[d], qT[d]], writes=[pS])
        pt = p.scr("pTm", [128, 128], BF16, 2)
        S.op("dve", lambda e, pt=pt, pS=pS, dm=dm: e.tensor_tensor(out=pt.ap, in0=pS.ap[:, :128], in1=dm.ap, op=ALU.mult), reads=[pS, dm], writes=[pt])
        pn = p.bank(2, 4)
        S.op("pe", lambda e, pn=pn, pt=pt, g=g, gl=gl: e.matmul(pn.ap[:, :257], lhsT=pt.ap, rhs=vtm[g].ap[:, gl, :], start=True, stop=False), reads=[pt, vtm[g]], writes=[pn])
        for d in range(2):
            S.op("pe", lambda e, pn=pn, qt=qt, d=d: e.matmul(pn.ap[:, :257], lhsT=qt.ap[:, d, :], rhs=Cb[d].ap, start=False, stop=(d == 1)), reads=[qt, Cb[d]], writes=[pn])
        den = p.scr("den", [128, 1], F32, 2)
        S.op("act", lambda e, den=den, pn=pn: e.activation(out=den.ap, in_=pn.ap[:, 256:257], func=AF.Abs), reads=[pn], writes=[den])
        S.op("dve", lambda e, den=den: e.tensor_scalar(out=den.ap, in0=den.ap, scalar1=1.0, scalar2=None, op0=ALU.max), reads=[den], writes=[den])
        S.op("dve", lambda e, den=den: e.reciprocal(out=den.ap, in_=den.ap), reads=[den], writes=[den])
        if gl == 0:
            hout = p.scr("hout", [128, G, 256], BF16, 2)
        S.op("act", lambda e, hout=hout, gl=gl, pn=pn, den=den: e.activation(out=hout.ap[:, gl, :], in_=pn.ap[:, :256], func=AF.Identity, bias=0.0, scale=den.ap[:, 0:1]),
             reads=[pn, den], writes=[hout])
        if gl == G - 1:
            p.store(oh[g * G * 128:(g + 1) * G * 128, :].rearrange("(b p) d -> p b d", p=128), hout)
        if c < NCH - 1:
            for d in range(2):
                pc = p.bank(4, 6)
                S.op("pe", lambda e, pc=pc, kw=kw, d=d, g=g, gl=gl: e.matmul(pc.ap[:, :257], lhsT=kw.ap[:, d * 128:(d + 1) * 128], rhs=vtm[g].ap[:, gl, :], start=True, stop=True),
                     reads=[kw, vtm[g]], writes=[pc])
                S.op("dve", lambda e, pc=pc, d=d, c=c: e.scalar_tensor_tensor(out=C[d].ap, in0=C[d].ap, scalar=dec.ap[:, c:c + 1], in1=pc.ap[:, :257], op0=ALU.mult, op1=ALU.add),
                     reads=[C[d], dec, pc], writes=[C[d]])
                S.op("act", lambda e, d=d: e.activation(out=Cb[d].ap, in_=C[d].ap, func=AF.Copy), reads=[C[d]], writes=[Cb[d]])
    return p


_cache = {}


def _prog(name, fn):
    if name not in _cache:
        p = fn()
        p.finish()
        _cache[name] = p
    return _cache[name]


def _run(p, in_maps):
    maps = []
    for m in in_maps:
        maps.append({k: np.ascontiguousarray(m[k]) for k in p.in_names})
    res = run_bass_kernel_spmd(p.nc, maps, core_ids=list(range(NCORES)))
    return res.results


def _pp(v, nch):
    return np.ascontiguousarray(np.asarray(v, np.float32).reshape(nch, 128).T)


def _b_conv1(layer, widx, first):
    def f():
        p = TL()
        p.load_x("xT")
        p.conv1(layer, widx)
        return p
    return f


def _b_L1():
    p = TL()
    p.load_x("xT")
    p.conv2(0, 0)
    p.ffn(0)
    p.store_x("x1T")
    return p


def _b_L1b():
    p = TL(resident=False, wslots=2)
    p.load_x("xT")
    p.mla_pre()
    return p


def _b_L2():
    p = TL()
    p.load_x("xT")
    p.attn_out()
    p.ffn(1)
    p.store_x("x2T")
    return p


def _b_L2b():
    p = TL(resident=False, wslots=2)
    p.load_x("xT")
    p.mlstm_pre()
    return p


def _b_L3():
    p = TL()
    p.load_x("xT")
    p.mlstm_post()
    p.ffn(2)
    p.store_x("x3T")
    p.conv1(3, 1)
    return p


def _b_L4():
    p = TL()
    p.load_x("xT")
    p.conv2(3, 1)
    p.ffn(3)
    p.final()
    return p


def _tsplit(a):
    return [np.ascontiguousarray(a[:, r * T:(r + 1) * T]) for r in range(NCORES)]


def kernel(x, c, positions, mod_w, mod_b, norm1_g, norm2_g, ffn_w_gate, ffn_w_up, ffn_w_down,
           conv_w_in, conv_w, conv_w_out,
           mla_w_dq, mla_q_norm_g, mla_w_uq, mla_w_dkv, mla_kv_norm_g, mla_w_ukv, mla_w_o,
           mlstm_w_in, mlstm_b_gates, mlstm_head_norm_g, mlstm_w_out, final_norm_g, _debug=None):
    f32 = np.float32
    R = range(NCORES)
    dbg = _debug if _debug is not None else {}
    pm = _prog("mods", build_mods)
    c_col = _pp(np.asarray(c, f32).reshape(-1), 16)
    maps = []
    for r in R:
        cs = slice(r * 1536, (r + 1) * 1536)
        maps.append({"c_col": c_col, "mod_w_s": np.asarray(mod_w)[:, :, cs],
                     "mod_b_s": np.asarray(mod_b)[:, cs].reshape(1, -1)})
    res = _run(pm, maps)
    modT = np.zeros((128, 4, 96), f32)
    for r in R:
        modT[:, :, r * 12:(r + 1) * 12] = res[r]["modT_s"].reshape(128, 4, 12)
    modT = modT.reshape(128, 384)
    dbg["modT"] = modT
    if dbg.get("stop") == "mods":
        return None
    ng1T = np.concatenate([_pp(norm1_g[l], 16) for l in range(4)], axis=1)
    ng2T = np.concatenate([_pp(norm2_g[l], 16) for l in range(4)], axis=1)
    common = {"modT": modT, "ng1T": ng1T, "ng2T": ng2T}

    def ffnw(l):
        return {f"wg{l}": ffn_w_gate[l], f"wu{l}": ffn_w_up[l], f"wd{l}": ffn_w_down[l]}

    def convw(j):
        cwp = np.concatenate([_pp(conv_w[j][t], 16) for t in range(3)], axis=1)
        return {f"cw{j}": cwp, f"cw_out{j}": conv_w_out[j]}

    def halos(cu_list):
        hs = [np.zeros((D, 2), f32)]
        for r in range(1, NCORES):
            hs.append(np.ascontiguousarray(cu_list[r - 1][:, T - 2:T]))
        return hs

    xT = _tsplit(np.ascontiguousarray(np.asarray(x, f32)[0].T))
    p0 = _prog("L0", _b_conv1(0, 0, True))
    res = _run(p0, [dict(common, xT=xT[r], cw_in0=conv_w_in[0]) for r in R])
    cB = [res[r]["convB"] for r in R]
    cCU = [res[r]["convCU"] for r in R]
    dbg["convB0"] = cB
    dbg["convCU0"] = cCU
    if dbg.get("stop") == "L0":
        return None
    hl = halos(cCU)
    p1 = _prog("L1", _b_L1)
    w_uq = np.asarray(mla_w_uq[0]).reshape(768, 16, 192)
    w_uqn = np.ascontiguousarray(w_uq[:, :, :128].reshape(768, 2048))
    pe = w_uq[:, :, 128:]
    w_uqp = np.ascontiguousarray(pe.reshape(768, 1024))
    w_uqps = np.ascontiguousarray(np.concatenate([pe[:, :, 32:], pe[:, :, :32]], axis=2).reshape(768, 1024))
    w_dkv = np.asarray(mla_w_dkv[0])
    w_dkvc = np.ascontiguousarray(w_dkv[:, :512])
    w_dkvp = np.ascontiguousarray(w_dkv[:, 512:])
    w_dkvps = np.ascontiguousarray(np.concatenate([w_dkv[:, 544:], w_dkv[:, 512:544]], axis=1))
    w_ukv = np.asarray(mla_w_ukv[0]).reshape(512, 16, 256)
    w_uk = np.ascontiguousarray(w_ukv[:, :, :128].reshape(512, 2048))
    w_uv = np.ascontiguousarray(w_ukv[:, :, 128:].reshape(512, 2048))
    pidx = np.arange(128)
    invf = (10000.0 ** (-2.0 * (pidx % 32).astype(np.float64) / 64)).astype(f32)
    sgn = np.where((pidx % 64) < 32, 1.0, -1.0).astype(f32)
    ropec = np.stack([invf, sgn], axis=1).astype(f32)
    pos = np.asarray(positions).reshape(-1).astype(np.int32)
    mla_in = {"w_dq": mla_w_dq[0], "w_uqn": w_uqn, "w_uqp": w_uqp, "w_uqps": w_uqps, "w_dkvc": w_dkvc, "w_dkvp": w_dkvp,
              "w_dkvps": w_dkvps, "w_uk": w_uk, "w_uv": w_uv, "qngT": _pp(mla_q_norm_g[0], 6), "kvngT": _pp(mla_kv_norm_g[0], 4),
              "ropec": ropec}
    maps = []
    for r in R:
        m = dict(common, xT=xT[r], convB_in=cB[r], convCU_in=cCU[r], halo=hl[r])
        m.update(convw(0)); m.update(ffnw(0))
        maps.append(m)
    res = _run(p1, maps)
    x1T = [res[r]["x1T"] for r in R]
    dbg["x1T"] = x1T
    if dbg.get("stop") == "L1":
        return None
    p1b = _prog("L1b", _b_L1b)
    maps = []
    for r in R:
        m = dict(common, xT=x1T[r])
        m.update(mla_in)
        m["pos_rep"] = np.ascontiguousarray(np.broadcast_to(pos[r * T:(r + 1) * T][None, :], (128, T)))
        maps.append(m)
    res = _run(p1b, maps)
    qn = np.concatenate([res[r]["qnT"] for r in R], axis=1)
    qp = np.concatenate([res[r]["qpT"] for r in R], axis=1)
    kn = np.concatenate([res[r]["knT"] for r in R], axis=1)
    kp = np.concatenate([res[r]["kpT"] for r in R], axis=1)
    vt = np.concatenate([res[r]["v_tm"] for r in R], axis=0)
    dbg.update(qn=qn, qp=qp, kn=kn, kp=kp, vt=vt)
    if dbg.get("stop") == "L1b":
        return None
    pa = _prog("attn", build_attn)
    jj = np.arange(512)[None, :]
    pp_ = np.arange(128)[:, None]
    cmask = np.concatenate([(jj >= (128 * r_ + pp_)).astype(f32) for r_ in range(4)], axis=1).astype(NPBF)
    maps = []
    for r in R:
        maps.append({"qn": qn[r * 256:(r + 1) * 256].reshape(2, 128, SEQ), "qp": qp[r * 128:(r + 1) * 128].reshape(2, 64, SEQ),
                     "kn": kn[r * 256:(r + 1) * 256].reshape(2, 128, SEQ), "kp": kp, "v": vt[:, r * 256:(r + 1) * 256], "cmask": cmask})
    res = _run(pa, maps)
    att = np.concatenate([res[r]["att"] for r in R], axis=0)
    dbg["att"] = att
    if dbg.get("stop") == "attn":
        return None
    attT = _tsplit(att)
    p2 = _prog("L2", _b_L2)
    w_in = np.asarray(mlstm_w_in[0])
    ml_in = {"w_o": mla_w_o[0], "m_wq": np.ascontiguousarray(w_in[:, :1024]), "m_wk": np.ascontiguousarray(w_in[:, 1024:2048]),
             "m_wv": np.ascontiguousarray(w_in[:, 2048:4096]), "m_wo": np.ascontiguousarray(w_in[:, 4096:6144]),
             "m_wg": np.ascontiguousarray(w_in[:, 6144:6152]), "m_bg": np.asarray(mlstm_b_gates[0], f32).reshape(8, 1)}
    maps = []
    for r in R:
        m = dict(common, xT=x1T[r], attT=attT[r], w_o=ml_in["w_o"])
        m.update(ffnw(1))
        maps.append(m)
    res = _run(p2, maps)
    x2T = [res[r]["x2T"] for r in R]
    dbg["x2T"] = x2T
    if dbg.get("stop") == "L2":
        return None
    p2b = _prog("L2b", _b_L2b)
    res = _run(p2b, [dict(common, xT=x2T[r], **ml_in) for r in R])
    sigo = [res[r]["sigoT"] for r in R]
    mq = np.concatenate([res[r]["mqT"] for r in R], axis=1)
    mk = np.concatenate([res[r]["mkT"] for r in R], axis=1)
    mktm = np.concatenate([res[r]["mk_tm"] for r in R], axis=0)
    mvtm = np.concatenate([res[r]["mv_tm"] for r in R], axis=0)
    gi = np.concatenate([res[r]["gi"] for r in R], axis=1)
    gf = np.concatenate([res[r]["gf"] for r in R], axis=1)
    dbg.update(mq=mq, mk=mk, mktm=mktm, mvtm=mvtm, gi=gi, gf=gf, sigo=sigo)
    if dbg.get("stop") == "L2b":
        return None
    pl = _prog("mlstm", build_mlstm)
    tri = np.triu(np.ones((128, 128), f32))
    maps = []
    for r in R:
        hd, hf = r // 2, r % 2
        maps.append({"qT": mq[hd * 256:(hd + 1) * 256], "kT": mk[hd * 256:(hd + 1) * 256],
                     "k_tm": mktm[:, hd * 256:(hd + 1) * 256], "v_tm": mvtm[:, hd * 512 + hf * 256: hd * 512 + (hf + 1) * 256],
                     "lf": np.ascontiguousarray(gf[4 + hd].reshape(SEQ // 128, 128).T), "ig": np.ascontiguousarray(gi[hd].reshape(SEQ // 128, 128).T),
                     "tri": tri})
    res = _run(pl, maps)
    mh = np.concatenate([res[r]["h_tm"] for r in R], axis=1)
    dbg["mh"] = mh
    if dbg.get("stop") == "mlstm":
        return None
    mhT = _tsplit(np.ascontiguousarray(mh.T))
    p3 = _prog("L3", _b_L3)
    maps = []
    for r in R:
        m = dict(common, xT=x2T[r], mhT=mhT[r], sigoT_in=sigo[r], m_wout=mlstm_w_out[0], hngT=_pp(mlstm_head_norm_g[0], 16), cw_in1=conv_w_in[1])
        m.update(ffnw(2))
        maps.append(m)
    res = _run(p3, maps)
    x3T = [res[r]["x3T"] for r in R]
    cB = [res[r]["convB"] for r in R]
    cCU = [res[r]["convCU"] for r in R]
    dbg["x3T"] = x3T
    if dbg.get("stop") == "L3":
        return None
    hl = halos(cCU)
    p4 = _prog("L4", _b_L4)
    maps = []
    for r in R:
        m = dict(common, xT=x3T[r], convB_in=cB[r], convCU_in=cCU[r], halo=hl[r], fngT=_pp(final_norm_g, 16))
        m.update(convw(1)); m.update(ffnw(3))
        maps.append(m)
    res = _run(p4, maps)
    outT = np.concatenate([res[r]["outT"] for r in R], axis=1)
    return np.ascontiguousarray(outT.T)[None].astype(np.float32)
```

```python
import math
from contextlib import ExitStack
import numpy as np
import ml_dtypes
import concourse.bass as bass
import concourse.mybir as mybir
from concourse.bass_utils import run_bass_kernel_spmd

F32 = mybir.dt.float32
BF16 = mybir.dt.bfloat16
I32 = mybir.dt.int32
ALU = mybir.AluOpType
AF = mybir.ActivationFunctionType
NPBF = ml_dtypes.bfloat16

NCORES = 8
D = 2048
SEQ = 8192
T = SEQ // NCORES
DFF = 5632
EPS = 1e-6
COMPUTE = ("pe", "act", "dve", "pool")
ALLENG = ("pe", "act", "dve", "pool", "sp")


class Buf:
    __slots__ = ("name", "writer", "readers", "dma_sem", "dma_cnt", "dma_last")

    def __init__(self, name):
        self.name = name
        self.writer = None
        self.readers = []
        self.dma_sem = None
        self.dma_cnt = 0
        self.dma_last = None


class View:
    __slots__ = ("ap", "bufs")

    def __init__(self, ap, bufs):
        self.ap = ap
        self.bufs = list(bufs)

    def __getitem__(self, idx):
        return View(self.ap[idx], self.bufs)


class Op:
    __slots__ = ("eng", "fn", "waits", "signal", "idx", "is_dma", "dsem", "dval")

    def __init__(self, eng, fn):
        self.eng = eng
        self.fn = fn
        self.waits = []
        self.signal = False
        self.idx = None
        self.is_dma = False
        self.dsem = None
        self.dval = 0


class Sched:
    def __init__(self, nc, same_engine_sync=True):
        self.nc = nc
        self.es = ExitStack()
        self.ops = {e: [] for e in ALLENG}
        self.same_engine_sync = same_engine_sync
        self.nbuf = 0

    def buf(self, name=None):
        self.nbuf += 1
        return Buf(name or f"b{self.nbuf}")

    def sb(self, name, shape, dtype):
        t = self.es.enter_context(self.nc.sbuf_tensor("sb_" + name, list(shape), dtype))
        return View(t[:], [self.buf(name)])

    def ps(self, name, shape, dtype=F32):
        t = self.es.enter_context(self.nc.psum_tensor(name, list(shape), dtype))
        return View(t[:], [self.buf(name)])

    def _deps(self, op, reads, writes):
        deps = []
        for v in reads:
            for b in v.bufs:
                if b.writer is not None:
                    deps.append(b.writer)
        for v in writes:
            for b in v.bufs:
                if b.writer is not None:
                    deps.append(b.writer)
                deps.extend(b.readers)
        seen = set()
        for d in deps:
            if d is op or id(d) in seen:
                continue
            seen.add(id(d))
            if (not d.is_dma) and d.eng == op.eng:
                if d.eng == "pe" or not self.same_engine_sync:
                    continue
            op.waits.append(d)
        for v in reads:
            for b in v.bufs:
                b.readers.append(op)
        for v in writes:
            for b in v.bufs:
                b.writer = op
                b.readers = []

    def op(self, eng, fn, reads=(), writes=()):
        o = Op(eng, fn)
        self._deps(o, reads, writes)
        self.ops[eng].append(o)
        return o

    def dma(self, eng, out, in_, chan=None, extra=None):
        pairs = [(out, in_)] + list(extra or [])
        if chan is None:
            chan = out.bufs[0] if out.bufs else in_.bufs[0]
        if chan.dma_sem is None:
            chan.dma_sem = self.es.enter_context(self.nc.semaphore("d_" + chan.name))
        sem = chan.dma_sem

        def fn(e, pairs=pairs, sem=sem):
            for (o_, i_) in pairs:
                e.dma_start(out=o_.ap, in_=i_.ap).then_inc(sem, 16)
            return None

        o = Op(eng, fn)
        o.is_dma = True
        o.dsem = sem
        chan.dma_cnt += 16 * len(pairs)
        o.dval = chan.dma_cnt
        if chan.dma_last is not None:
            o.waits.append(chan.dma_last)
        chan.dma_last = o
        self._deps(o, [p[1] for p in pairs], [p[0] for p in pairs])
        self.ops[eng].append(o)
        return o

    def emit(self, final_waits=()):
        nc = self.nc
        sems = {e: self.es.enter_context(nc.semaphore("s_" + e)) for e in COMPUTE}
        for e in ALLENG:
            for o in self.ops[e]:
                for d in o.waits:
                    if not d.is_dma:
                        d.signal = True
        for fo in final_waits:
            if not fo.is_dma:
                fo.signal = True
        for e in ALLENG:
            c = 0
            for o in self.ops[e]:
                if o.signal and not o.is_dma:
                    c += 1
                    o.idx = c

        def run_stream(e, eng):
            seen = {}
            for o in self.ops[e]:
                for d in o.waits:
                    if d.is_dma:
                        key = ("d", id(d.dsem))
                        if seen.get(key, 0) >= d.dval:
                            continue
                        seen[key] = d.dval
                        eng.wait_ge(d.dsem, d.dval)
                    else:
                        key = ("c", d.eng)
                        if seen.get(key, 0) >= d.idx:
                            continue
                        seen[key] = d.idx
                        eng.wait_ge(sems[d.eng], d.idx)
                ins = o.fn(eng)
                if o.signal and not o.is_dma:
                    ins.then_inc(sems[e], 1)
            if e == "sp":
                for fo in final_waits:
                    if fo.is_dma:
                        eng.wait_ge(fo.dsem, fo.dval)
                    else:
                        eng.wait_ge(sems[fo.eng], fo.idx)

        with nc.Block() as block:
            @block.tensor
            def _(eng):
                run_stream("pe", eng)

            @block.scalar
            def _(eng):
                run_stream("act", eng)

            @block.vector
            def _(eng):
                run_stream("dve", eng)

            @block.gpsimd
            def _(eng):
                run_stream("pool", eng)

            @block.sync
            def _(eng):
                run_stream("sp", eng)
        self.es.close()


class FM:
    def __init__(self, p, name, nch, Tn, dt, tw=512):
        self.t = p.S.es.enter_context(p.nc.sbuf_tensor("fm_" + name, [128, nch, Tn], dt))
        self.tw = tw
        self.ntt = Tn // tw
        self.b = [[p.S.buf(f"{name}_{c}_{t}") for t in range(self.ntt)] for c in range(nch)]

    def v(self, c, tt, rows=128):
        return View(self.t[:rows, c, tt * self.tw:(tt + 1) * self.tw], [self.b[c][tt]])

    def vc(self, c, rows=128):
        return View(self.t[:rows, c, :], self.b[c])


class Prog:
    def __init__(self, wslots=3, slot_elems=8192):
        self.nc = bass.Bass("TRN2", target_bir_lowering=False)
        self.S = Sched(self.nc)
        self.in_names = []
        self.out_names = []
        self.out_ops = []
        self.banks = [self.S.ps(f"psb{i}", [128, 512], F32) for i in range(8)]
        self.rr = {}
        self.scr_pool = {}
        self.ones = self.S.sb("ones_bf", [128, 128], BF16)
        o = self.ones
        self.S.op("pool", lambda e: e.memset(o.ap, 1.0), writes=[o])
        self.slots = [self.S.sb(f"wslot{i}", [128, slot_elems], BF16) for i in range(wslots)]
        self.slot_elems = slot_elems
        self.slot_i = 0

    def inp(self, name, shape, dt):
        self.in_names.append(name)
        return self.nc.dram_tensor(name, list(shape), dt, kind="ExternalInput").ap()

    def out(self, name, shape, dt):
        self.out_names.append(name)
        return self.nc.dram_tensor(name, list(shape), dt, kind="ExternalOutput").ap()

    def load(self, sbv, dram_ap, eng="sp"):
        return self.S.dma(eng, sbv, View(dram_ap, []))

    def store(self, dram_ap, sbv, eng="sp"):
        o = self.S.dma(eng, View(dram_ap, []), sbv, chan=sbv.bufs[0])
        self.out_ops.append(o)
        return o

    def bank(self, lo=0, hi=6):
        k = (lo, hi)
        i = self.rr.get(k, 0)
        self.rr[k] = (i + 1) % (hi - lo)
        return self.banks[lo + i]

    def scr(self, name, shape, dt, n=2):
        if name not in self.scr_pool:
            self.scr_pool[name] = ([self.S.sb(f"{name}{i}", shape, dt) for i in range(n)], [0])
        lst, ctr = self.scr_pool[name]
        v = lst[ctr[0] % len(lst)]
        ctr[0] += 1
        return v

    def wslot(self):
        s = self.slots[self.slot_i % len(self.slots)]
        self.slot_i += 1
        return s

    def finish(self):
        self.S.emit(final_waits=self.out_ops)
        return self.nc


def linear(p, K, groups, rhs, ntt, evac, interleave=False, tw=512):
    S = p.S
    KC = K // 128
    for gi, segs in enumerate(groups):
        slot = p.wslot()
        tot = sum(s[2] for s in segs)
        assert KC * tot <= p.slot_elems, (KC, tot)
        sv3 = slot.ap[:, :KC * tot].rearrange("p (k n) -> p k n", n=tot)
        pairs = []
        offs = []
        off = 0
        for (w, c0, n) in segs:
            pairs.append((View(sv3[:, :, off:off + n], slot.bufs),
                          View(w[:, c0:c0 + n].rearrange("(k p) n -> p k n", p=128), [])))
            offs.append(off)
            off += n
        S.dma("pool", pairs[0][0], pairs[0][1], extra=pairs[1:])
        chunks = []
        for si, (w, c0, n) in enumerate(segs):
            for ci in range(0, n, 128):
                chunks.append((ci // 128, si, offs[si] + ci, min(128, n - ci)))
        if interleave:
            chunks.sort(key=lambda t: (t[0], t[1]))
        for (ci, si, o0, m) in chunks:
            for tt in range(ntt):
                ps = p.bank()
                for k in range(KC):
                    r = rhs(k, tt)
                    S.op("pe", lambda e, ps=ps, k=k, r=r, o0=o0, m=m, sv3=sv3: e.matmul(
                        ps.ap[:m, :tw], lhsT=sv3[:, k, o0:o0 + m], rhs=r.ap, start=(k == 0), stop=(k == KC - 1)),
                        reads=[slot, r], writes=[ps])
                evac(gi, si, ci, m, tt, ps)


def linear_tm(p, K, w, ncols, lhs, ntb, evac, cw=512):
    S = p.S
    KC = K // 128
    gcols = (p.slot_elems // KC) // cw * cw
    for g0 in range(0, ncols, gcols):
        gn = min(gcols, ncols - g0)
        slot = p.wslot()
        sv3 = slot.ap[:, :KC * gn].rearrange("p (k n) -> p k n", n=gn)
        S.dma("pool", View(sv3, slot.bufs), View(w[:, g0:g0 + gn].rearrange("(k p) n -> p k n", p=128), []))
        for tb in range(ntb):
            for c0 in range(0, gn, cw):
                ps = p.bank()
                for k in range(KC):
                    l = lhs(k, tb)
                    S.op("pe", lambda e, ps=ps, k=k, l=l, c0=c0, sv3=sv3: e.matmul(
                        ps.ap[:, :cw], lhsT=l.ap, rhs=sv3[:, k, c0:c0 + cw], start=(k == 0), stop=(k == KC - 1)),
                        reads=[slot, l], writes=[ps])
                evac(tb, g0 + c0, ps)


def norm_fm(p, src, nch, ntt, a, b, dst, inv_n):
    S = p.S
    for tt in range(ntt):
        ps = p.bank(6, 8)
        for c in range(nch):
            sq = p.scr("sq", [128, 512], BF16, 3)
            s_ = src(c, tt)
            S.op("act", lambda e, o=sq, i=s_: e.activation(out=o.ap, in_=i.ap, func=AF.Square), reads=[s_], writes=[sq])
            S.op("pe", lambda e, ps=ps, sq=sq, c=c: e.matmul(ps.ap, lhsT=p.ones.ap, rhs=sq.ap, start=(c == 0), stop=(c == nch - 1)),
                 reads=[sq, p.ones], writes=[ps])
        r = p.scr("rstd", [128, 512], F32, 2)
        S.op("act", lambda e, r=r, ps=ps: e.activation(out=r.ap, in_=ps.ap, func=AF.Sqrt, bias=EPS, scale=inv_n), reads=[ps], writes=[r])
        S.op("dve", lambda e, r=r: e.reciprocal(out=r.ap, in_=r.ap), reads=[r], writes=[r])
        for c in range(nch):
            tmp = p.scr("ntmp", [128, 512], F32, 3)
            s_ = src(c, tt)
            d_ = dst(c, tt)
            S.op("dve", lambda e, tmp=tmp, s_=s_, r=r: e.tensor_tensor(out=tmp.ap, in0=s_.ap, in1=r.ap, op=ALU.mult), reads=[s_, r], writes=[tmp])
            if b is not None:
                S.op("act", lambda e, d_=d_, tmp=tmp, c=c: e.activation(out=d_.ap, in_=tmp.ap, func=AF.Identity, bias=b.ap[:, c:c + 1], scale=a.ap[:, c:c + 1]),
                     reads=[tmp, a, b], writes=[d_])
            else:
                S.op("act", lambda e, d_=d_, tmp=tmp, c=c: e.activation(out=d_.ap, in_=tmp.ap, func=AF.Identity, bias=0.0, scale=a.ap[:, c:c + 1]),
                     reads=[tmp, a], writes=[d_])


class TL(Prog):
    def __init__(self, resident=True, wslots=3):
        super().__init__(wslots=wslots)
        p = self
        self.resident = resident
        if resident:
            self.xT = FM(p, "xT", 16, T, F32)
        self.hT = FM(p, "hT", 16, T, BF16)
        self.modT = self.S.sb("modT", [128, 384], F32)
        self.ng1 = self.S.sb("ng1", [128, 64], F32)
        self.ng2 = self.S.sb("ng2", [128, 64], F32)
        self.load(self.modT, self.inp("modT", [128, 384], F32))
        self.load(self.ng1, self.inp("ng1T", [128, 64], F32))
        self.load(self.ng2, self.inp("ng2T", [128, 64], F32))

    def load_x(self, name):
        xin = self.inp(name, [D, T], F32)
        if not self.resident:
            def src(c, tt):
                t_ = self.scr("xs", [128, 512], F32, 3)
                self.load(t_, xin[c * 128:(c + 1) * 128, tt * 512:(tt + 1) * 512])
                return t_
            self.x_src = src
            return
        self.x_src = self.xT.v
        for c in range(16):
            self.load(self.xT.vc(c), xin[c * 128:(c + 1) * 128, :])

    def store_x(self, name):
        xo = self.out(name, [D, T], F32)
        for c in range(16):
            self.store(xo[c * 128:(c + 1) * 128, :], self.xT.vc(c))

    def adaln(self, layer, which):
        S = self.S
        base = layer * 96 + which * 48
        ng = self.ng1 if which == 0 else self.ng2
        a = S.sb(f"ada_{layer}_{which}", [128, 16], F32)
        S.op("dve", lambda e: e.scalar_tensor_tensor(out=a.ap, in0=self.modT.ap[:, base + 16:base + 32], scalar=1.0,
                                                      in1=ng.ap[:, layer * 16:(layer + 1) * 16], op0=ALU.add, op1=ALU.mult),
             reads=[self.modT, ng], writes=[a])
        sh = View(self.modT.ap[:, base:base + 16], self.modT.bufs)
        g = View(self.modT.ap[:, base + 32:base + 48], self.modT.bufs)
        return a, sh, g

    def resid_evac(self, g, cpg=4):
        S = self.S

        def ev(gi, si, ci, m, tt, ps, g=g):
            c = gi * cpg + ci
            xv = self.xT.v(c, tt)
            S.op("dve", lambda e: e.scalar_tensor_tensor(out=xv.ap, in0=ps.ap, scalar=g.ap[:, c:c + 1], in1=xv.ap, op0=ALU.mult, op1=ALU.add),
                 reads=[ps, g, xv], writes=[xv])
        return ev

    def ffn(self, layer):
        p, S = self, self.S
        a, sh, g = self.adaln(layer, 1)
        norm_fm(p, self.x_src, 16, 2, a, sh, self.hT.v, 1.0 / D)
        wg = self.inp(f"wg{layer}", [D, DFF], F32)
        wu = self.inp(f"wu{layer}", [D, DFF], F32)
        wd = self.inp(f"wd{layer}", [DFF, D], F32)
        if not hasattr(self, "aT"):
            self.aT = FM(p, "aT", 6, T, BF16)
        aT = self.aT
        parts = [6, 6, 6, 6, 5, 5, 5, 5]
        for q in range(8):
            j0 = sum(parts[:q])
            nj = parts[q]
            sizes = [2, 2, 2] if nj == 6 else [2, 2, 1]
            groups = []
            jj = j0
            gstart = []
            for sz in sizes:
                groups.append([(wg, jj * 128, sz * 128), (wu, jj * 128, sz * 128)])
                gstart.append(jj - j0)
                jj += sz
            sgs = {}

            def ev(gi, si, ci, m, tt, ps):
                jl = gstart[gi] + ci
                if si == 0:
                    sg = p.scr("sg", [128, 512], F32, 5)
                    sgs[(jl, tt)] = sg
                    S.op("act", lambda e: e.activation(out=sg.ap, in_=ps.ap, func=AF.Silu), reads=[ps], writes=[sg])
                else:
                    sg = sgs[(jl, tt)]
                    av = aT.v(jl, tt)
                    S.op("dve", lambda e: e.tensor_tensor(out=av.ap, in0=sg.ap, in1=ps.ap, op=ALU.mult), reads=[sg, ps], writes=[av])
            linear(p, D, groups, self.hT.v, 2, ev, interleave=True)
            wdq = wd[j0 * 128:(j0 + nj) * 128, :]
            linear(p, nj * 128, [[(wdq, n * 1024, 1024)] for n in range(2)], aT.v, 2, self.resid_evac(g, 8))

    def conv1(self, layer, widx):
        p, S = self, self.S
        a, sh, g = self.adaln(layer, 0)
        norm_fm(p, self.x_src, 16, 2, a, sh, self.hT.v, 1.0 / D)
        w = self.inp(f"cw_in{widx}", [D, 3 * D], F32)
        oB = self.out("convB", [D, T], BF16)
        oCU = self.out("convCU", [D, T], F32)
        groups = [[(w, n * 128, 128), (w, D + n * 128, 128), (w, 2 * D + n * 128, 128)] for n in range(16)]
        st = {}

        def ev(gi, si, ci, m, tt, ps):
            cs = slice(tt * 512, (tt + 1) * 512)
            if si == 0:
                if tt == 0:
                    st["B"] = p.scr("stB", [128, T], BF16, 2)
                b_ = st["B"]
                S.op("act", lambda e: e.activation(out=b_.ap[:, cs], in_=ps.ap, func=AF.Copy), reads=[ps], writes=[b_])
                if tt == 1:
                    p.store(oB[gi * 128:(gi + 1) * 128, :], b_)
            elif si == 1:
                c_ = p.scr("cC", [128, 512], F32, 3)
                st[("C", tt)] = c_
                S.op("act", lambda e: e.activation(out=c_.ap, in_=ps.ap, func=AF.Copy), reads=[ps], writes=[c_])
            else:
                if tt == 0:
                    st["CU"] = p.scr("stCU", [128, T], F32, 2)
                cu = st["CU"]
                c_ = st[("C", tt)]
                S.op("dve", lambda e: e.tensor_tensor(out=cu.ap[:, cs], in0=c_.ap, in1=ps.ap, op=ALU.mult), reads=[c_, ps], writes=[cu])
                if tt == 1:
                    p.store(oCU[gi * 128:(gi + 1) * 128, :], cu)
        linear(p, D, groups, self.hT.v, 2, ev)

    def conv2(self, layer, widx):
        p, S = self, self.S
        base = layer * 96
        g = View(self.modT.ap[:, base + 32:base + 48], self.modT.bufs)
        iB = self.inp("convB_in", [D, T], BF16)
        iCU = self.inp("convCU_in", [D, T], F32)
        ihalo = self.inp("halo", [D, 2], F32)
        icw = self.inp(f"cw{widx}", [128, 48], F32)
        wout = self.inp(f"cw_out{widx}", [D, D], F32)
        cw = S.sb("convw", [128, 48], F32)
        self.load(cw, icw)
        for c in range(16):
            cu = p.scr("cu_in", [128, T + 2], F32, 2)
            bc = p.scr("b_in", [128, T], BF16, 2)
            S.dma("sp", View(cu.ap[:, 0:2], cu.bufs), View(ihalo[c * 128:(c + 1) * 128, :], []),
                  extra=[(View(cu.ap[:, 2:], cu.bufs), View(iCU[c * 128:(c + 1) * 128, :], []))])
            self.load(bc, iB[c * 128:(c + 1) * 128, :])
            z = p.scr("convz", [128, T], F32, 2)
            S.op("dve", lambda e, z=z, cu=cu, c=c: e.tensor_scalar(out=z.ap, in0=cu.ap[:, 2:2 + T], scalar1=cw.ap[:, 32 + c:33 + c], scalar2=None, op0=ALU.mult),
                 reads=[cu, cw], writes=[z])
            S.op("dve", lambda e, z=z, cu=cu, c=c: e.scalar_tensor_tensor(out=z.ap, in0=cu.ap[:, 1:1 + T], scalar=cw.ap[:, 16 + c:17 + c], in1=z.ap, op0=ALU.mult, op1=ALU.add),
                 reads=[cu, cw, z], writes=[z])
            S.op("dve", lambda e, z=z, cu=cu, c=c: e.scalar_tensor_tensor(out=z.ap, in0=cu.ap[:, 0:T], scalar=cw.ap[:, c:c + 1], in1=z.ap, op0=ALU.mult, op1=ALU.add),
                 reads=[cu, cw, z], writes=[z])
            hv = self.hT.vc(c)
            S.op("pool", lambda e, z=z, bc=bc, hv=hv: e.tensor_tensor(out=hv.ap, in0=z.ap, in1=bc.ap, op=ALU.mult), reads=[z, bc], writes=[hv])
        linear(p, D, [[(wout, n * 512, 512)] for n in range(4)], self.hT.v, 2, self.resid_evac(g))

    def final(self):
        p, S = self, self.S
        fg = S.sb("fng", [128, 16], F32)
        self.load(fg, self.inp("fngT", [128, 16], F32))
        oo = self.out("outT", [D, T], F32)
        for tt in range(2):
            ps = p.bank(6, 8)
            for c in range(16):
                sq = p.scr("sq", [128, 512], BF16, 3)
                s_ = self.xT.v(c, tt)
                S.op("act", lambda e, o=sq, i=s_: e.activation(out=o.ap, in_=i.ap, func=AF.Square), reads=[s_], writes=[sq])
                S.op("pe", lambda e, ps=ps, sq=sq, c=c: e.matmul(ps.ap, lhsT=p.ones.ap, rhs=sq.ap, start=(c == 0), stop=(c == 15)),
                     reads=[sq, p.ones], writes=[ps])
            r = p.scr("rstd", [128, 512], F32, 2)
            S.op("act", lambda e, r=r, ps=ps: e.activation(out=r.ap, in_=ps.ap, func=AF.Sqrt, bias=EPS, scale=1.0 / D), reads=[ps], writes=[r])
            S.op("dve", lambda e, r=r: e.reciprocal(out=r.ap, in_=r.ap), reads=[r], writes=[r])
            for c in range(16):
                s_ = self.xT.v(c, tt)
                d_ = p.scr("ntmp", [128, 512], F32, 3)
                S.op("dve", lambda e, d_=d_, s_=s_, r=r, c=c: e.scalar_tensor_tensor(out=d_.ap, in0=s_.ap, scalar=fg.ap[:, c:c + 1], in1=r.ap, op0=ALU.mult, op1=ALU.mult),
                     reads=[s_, r, fg], writes=[d_])
                p.store(oo[c * 128:(c + 1) * 128, tt * 512:(tt + 1) * 512], d_)

    def evac_store(self, name, out_ap, dt, row_of, scale=None, eng="act"):
        p, S = self, self.S
        st = {}

        def ev(gi, si, ci, m, tt, ps):
            cs = slice(tt * 512, (tt + 1) * 512)
            if tt == 0:
                st["t"] = p.scr("st_" + name, [128, T], dt, 2)
            t_ = st["t"]
            if scale is None:
                S.op("act", lambda e: e.activation(out=t_.ap[:m, cs], in_=ps.ap[:m, :], func=AF.Copy), reads=[ps], writes=[t_])
            else:
                S.op("act", lambda e: e.activation(out=t_.ap[:m, cs], in_=ps.ap[:m, :], func=AF.Copy, scale=scale), reads=[ps], writes=[t_])
            if tt == 1:
                r0 = row_of(gi, si, ci)
                p.store(out_ap[r0:r0 + m, :], View(t_.ap[:m, :], t_.bufs))
        return ev

    def rope_tables(self):
        p, S = self, self.S
        posi = S.sb("posi", [128, T], I32)
        self.load(posi, self.inp("pos_rep", [128, T], I32))
        rc = S.sb("ropec", [128, 2], F32)
        self.load(rc, self.inp("ropec", [128, 2], F32))
        ang = S.sb("ang", [128, T], F32)
        S.op("dve", lambda e: e.tensor_copy(out=ang.ap, in_=posi.ap), reads=[posi], writes=[ang])
        S.op("dve", lambda e: e.tensor_scalar(out=ang.ap, in0=ang.ap, scalar1=rc.ap[:, 0:1], scalar2=None, op0=ALU.mult), reads=[ang, rc], writes=[ang])
        C1 = 6.28125
        C2 = 2 * math.pi - C1
        outs = []
        for name, shift in (("sin", 0.0), ("cos", math.pi / 2)):
            y = S.sb("rt_" + name, [128, T], F32)
            ni = p.scr("rt_ni", [128, T], I32, 1)
            nf = p.scr("rt_nf", [128, T], F32, 1)
            S.op("dve", lambda e, y=y, shift=shift: e.tensor_scalar(out=y.ap, in0=ang.ap, scalar1=shift, scalar2=None, op0=ALU.add), reads=[ang], writes=[y])
            S.op("dve", lambda e, y=y, nf=nf: e.tensor_scalar(out=nf.ap, in0=y.ap, scalar1=1.0 / (2 * math.pi), scalar2=None, op0=ALU.mult), reads=[y], writes=[nf])
            S.op("dve", lambda e, ni=ni, nf=nf: e.tensor_copy(out=ni.ap, in_=nf.ap), reads=[nf], writes=[ni])
            S.op("dve", lambda e, ni=ni, nf=nf: e.tensor_copy(out=nf.ap, in_=ni.ap), reads=[ni], writes=[nf])
            S.op("dve", lambda e, y=y, nf=nf: e.scalar_tensor_tensor(out=y.ap, in0=nf.ap, scalar=-C1, in1=y.ap, op0=ALU.mult, op1=ALU.add), reads=[nf, y], writes=[y])
            S.op("dve", lambda e, y=y, nf=nf: e.scalar_tensor_tensor(out=y.ap, in0=nf.ap, scalar=-C2, in1=y.ap, op0=ALU.mult, op1=ALU.add), reads=[nf, y], writes=[y])
            S.op("dve", lambda e, y=y: e.tensor_scalar(out=y.ap, in0=y.ap, scalar1=math.pi, scalar2=-math.pi, op0=ALU.min, op1=ALU.max), reads=[y], writes=[y])
            S.op("act", lambda e, y=y: e.activation(out=y.ap, in_=y.ap, func=AF.Sin), reads=[y], writes=[y])
            outs.append(y)
        sin, cos = outs
        S.op("dve", lambda e: e.tensor_scalar(out=sin.ap, in0=sin.ap, scalar1=rc.ap[:, 1:2], scalar2=-1.0, op0=ALU.mult, op1=ALU.mult), reads=[sin, rc], writes=[sin])
        self.cos2, self.sin2s = cos, sin

    def rope_evac(self, name, out_ap, row_of):
        p, S = self, self.S
        st = {}

        def ev(gi, si, ci, m, tt, ps):
            cs = slice(tt * 512, (tt + 1) * 512)
            if si == 0:
                t1 = p.scr("rp_t1", [128, 512], F32, 4)
                st[(ci, tt)] = t1
                S.op("dve", lambda e: e.tensor_tensor(out=t1.ap[:m, :], in0=ps.ap[:m, :], in1=self.cos2.ap[:m, cs], op=ALU.mult), reads=[ps, self.cos2], writes=[t1])
            else:
                t1 = st[(ci, tt)]
                t2 = p.scr("rp_t2", [128, 512], F32, 2)
                S.op("dve", lambda e: e.tensor_tensor(out=t2.ap[:m, :], in0=ps.ap[:m, :], in1=self.sin2s.ap[:m, cs], op=ALU.mult), reads=[ps, self.sin2s], writes=[t2])
                if tt == 0:
                    st["o"] = p.scr("st_" + name, [128, T], BF16, 2)
                o_ = st["o"]
                S.op("pool", lambda e: e.tensor_tensor(out=o_.ap[:m, cs], in0=t1.ap[:m, :], in1=t2.ap[:m, :], op=ALU.add), reads=[t1, t2], writes=[o_])
                if tt == 1:
                    r0 = row_of(gi, ci)
                    p.store(out_ap[r0:r0 + m, :], View(o_.ap[:m, :], o_.bufs))
        return ev

    def mla_pre(self):
        p, S = self, self.S
        layer = 1
        a, sh, g = self.adaln(layer, 0)
        norm_fm(p, self.x_src, 16, 2, a, sh, self.hT.v, 1.0 / D)
        self.rope_tables()
        w_dq = self.inp("w_dq", [D, 768], F32)
        w_uqn = self.inp("w_uqn", [768, 2048], F32)
        w_uqp = self.inp("w_uqp", [768, 1024], F32)
        w_uqps = self.inp("w_uqps", [768, 1024], F32)
        w_dkvc = self.inp("w_dkvc", [D, 512], F32)
        w_dkvp = self.inp("w_dkvp", [D, 64], F32)
        w_dkvps = self.inp("w_dkvps", [D, 64], F32)
        w_uk = self.inp("w_uk", [512, 2048], F32)
        w_uv = self.inp("w_uv", [512, 2048], F32)
        qng = S.sb("qng", [128, 6], F32)
        kvng = S.sb("kvng", [128, 4], F32)
        self.load(qng, self.inp("qngT", [128, 6], F32))
        self.load(kvng, self.inp("kvngT", [128, 4], F32))
        o_qn = self.out("qnT", [2048, T], BF16)
        o_qp = self.out("qpT", [1024, T], BF16)
        o_kn = self.out("knT", [2048, T], BF16)
        o_kp = self.out("kpT", [64, T], BF16)
        o_v = self.out("v_tm", [T, 2048], BF16)
        cqpre = FM(p, "cqpre", 6, T, F32)
        cq = FM(p, "cq", 6, T, BF16)

        def ev_pre(dstfm, cbase):
            def ev(gi, si, ci, m, tt, ps):
                d_ = dstfm.v(cbase(gi) + ci, tt)
                S.op("act", lambda e: e.activation(out=d_.ap, in_=ps.ap, func=AF.Copy), reads=[ps], writes=[d_])
            return ev
        linear(p, D, [[(w_dq, 0, 512)], [(w_dq, 512, 256)]], self.hT.v, 2, ev_pre(cqpre, lambda gi: gi * 4))
        norm_fm(p, cqpre.v, 6, 2, qng, None, cq.v, 1.0 / 768)
        linear(p, 768, [[(w_uqn, n * 1024, 1024)] for n in range(2)], cq.v, 2,
               self.evac_store("qn", o_qn, BF16, lambda gi, si, ci: gi * 1024 + ci * 128))
        linear(p, 768, [[(w_uqp, n * 512, 512), (w_uqps, n * 512, 512)] for n in range(2)], cq.v, 2,
               self.rope_evac("qp", o_qp, lambda gi, ci: gi * 512 + ci * 128), interleave=True)
        linear(p, D, [[(w_dkvc, 0, 512)]], self.hT.v, 2, ev_pre(cqpre, lambda gi: 0))
        norm_fm(p, cqpre.v, 4, 2, kvng, None, cq.v, 1.0 / 512)
        linear(p, D, [[(w_dkvp, 0, 64), (w_dkvps, 0, 64)]], self.hT.v, 2,
               self.rope_evac("kp", o_kp, lambda gi, ci: 0), interleave=True)
        linear(p, 512, [[(w_uk, 0, 2048)]], cq.v, 2,
               self.evac_store("kn", o_kn, BF16, lambda gi, si, ci: ci * 128))

        def ev_v(tb, c0, ps):
            t_ = p.scr("st_v", [128, 512], BF16, 3)
            S.op("act", lambda e: e.activation(out=t_.ap, in_=ps.ap, func=AF.Copy), reads=[ps], writes=[t_])
            p.store(o_v[tb * 128:(tb + 1) * 128, c0:c0 + 512], t_)
        linear_tm(p, 512, w_uv, 2048, lambda k, tb: View(cq.t[:, k, tb * 128:(tb + 1) * 128], [cq.b[k][tb // 4]]), 8, ev_v)

    def attn_out(self):
        p, S = self, self.S
        layer = 1
        base = layer * 96
        g = View(self.modT.ap[:, base + 32:base + 48], self.modT.bufs)
        ia = self.inp("attT", [D, T], BF16)
        wo = self.inp("w_o", [D, D], F32)
        for c in range(16):
            self.load(self.hT.vc(c), ia[c * 128:(c + 1) * 128, :])
        linear(p, D, [[(wo, n * 512, 512)] for n in range(4)], self.hT.v, 2, self.resid_evac(g))

    def mlstm_pre(self):
        p, S = self, self.S
        layer = 2
        a, sh, g = self.adaln(layer, 0)
        norm_fm(p, self.x_src, 16, 2, a, sh, self.hT.v, 1.0 / D)
        wq = self.inp("m_wq", [D, 1024], F32)
        wk = self.inp("m_wk", [D, 1024], F32)
        wv = self.inp("m_wv", [D, 2048], F32)
        wo = self.inp("m_wo", [D, 2048], F32)
        wgt = self.inp("m_wg", [D, 8], F32)
        bg = S.sb("m_bg", [8, 1], F32)
        self.load(bg, self.inp("m_bg", [8, 1], F32))
        o_q = self.out("mqT", [1024, T], BF16)
        o_k = self.out("mkT", [1024, T], BF16)
        o_ktm = self.out("mk_tm", [T, 1024], BF16)
        o_vtm = self.out("mv_tm", [T, 2048], BF16)
        o_gi = self.out("gi", [8, T], F32)
        o_gf = self.out("gf", [8, T], F32)
        o_so = self.out("sigoT", [D, T], BF16)
        linear(p, D, [[(wq, n * 512, 512)] for n in range(2)], self.hT.v, 2,
               self.evac_store("mq", o_q, BF16, lambda gi, si, ci: gi * 512 + ci * 128))
        linear(p, D, [[(wk, n * 512, 512)] for n in range(2)], self.hT.v, 2,
               self.evac_store("mk", o_k, BF16, lambda gi, si, ci: gi * 512 + ci * 128, scale=1.0 / 16))
        st = {}

        def ev_so(gi, si, ci, m, tt, ps):
            cs = slice(tt * 512, (tt + 1) * 512)
            if tt == 0:
                st["t"] = p.scr("st_so", [128, T], BF16, 2)
            t_ = st["t"]
            S.op("act", lambda e: e.activation(out=t_.ap[:, cs], in_=ps.ap, func=AF.Sigmoid), reads=[ps], writes=[t_])
            if tt == 1:
                r0 = gi * 512 + ci * 128
                p.store(o_so[r0:r0 + 128, :], t_)
        linear(p, D, [[(wo, n * 512, 512)] for n in range(4)], self.hT.v, 2, ev_so)
        gst = {}

        def ev_g(gi, si, ci, m, tt, ps):
            cs = slice(tt * 512, (tt + 1) * 512)
            if tt == 0:
                gst["i"] = S.sb("g_i", [8, T], F32)
                gst["f"] = S.sb("g_f", [8, T], F32)
            gi_, gf_ = gst["i"], gst["f"]
            S.op("act", lambda e: e.activation(out=gi_.ap[:, cs], in_=ps.ap[:8, :], func=AF.Identity, bias=bg.ap[:, 0:1], scale=1.0), reads=[ps, bg], writes=[gi_])
            S.op("act", lambda e: e.activation(out=gi_.ap[:, cs], in_=gi_.ap[:, cs], func=AF.Tanh, scale=1.0 / 15), reads=[gi_], writes=[gi_])
            S.op("dve", lambda e: e.tensor_scalar(out=gi_.ap[:, cs], in0=gi_.ap[:, cs], scalar1=15.0, scalar2=None, op0=ALU.mult), reads=[gi_], writes=[gi_])
            S.op("act", lambda e: e.activation(out=gf_.ap[:, cs], in_=gi_.ap[:, cs], func=AF.Exp, scale=-1.0), reads=[gi_], writes=[gf_])
            S.op("act", lambda e: e.activation(out=gf_.ap[:, cs], in_=gf_.ap[:, cs], func=AF.Ln, bias=1.0, scale=1.0), reads=[gf_], writes=[gf_])
            S.op("dve", lambda e: e.tensor_scalar(out=gf_.ap[:, cs], in0=gf_.ap[:, cs], scalar1=-1.0, scalar2=None, op0=ALU.mult), reads=[gf_], writes=[gf_])
            if tt == 1:
                p.store(o_gi, gi_)
                p.store(o_gf, gf_)
        linear(p, D, [[(wgt, 0, 8)]], self.hT.v, 2, ev_g)

        def lhs(k, tb):
            return View(self.hT.t[:, k, tb * 128:(tb + 1) * 128], [self.hT.b[k][tb // 4]])

        def ev_k(tb, c0, ps):
            t_ = p.scr("st_v", [128, 512], BF16, 3)
            S.op("act", lambda e: e.activation(out=t_.ap, in_=ps.ap, func=AF.Copy, scale=1.0 / 16), reads=[ps], writes=[t_])
            p.store(o_ktm[tb * 128:(tb + 1) * 128, c0:c0 + 512], t_)

        def ev_v(tb, c0, ps):
            t_ = p.scr("st_v", [128, 512], BF16, 3)
            S.op("act", lambda e: e.activation(out=t_.ap, in_=ps.ap, func=AF.Copy), reads=[ps], writes=[t_])
            p.store(o_vtm[tb * 128:(tb + 1) * 128, c0:c0 + 512], t_)
        linear_tm(p, D, wk, 1024, lhs, 8, ev_k)
        linear_tm(p, D, wv, 2048, lhs, 8, ev_v)

    def mlstm_post(self):
        p, S = self, self.S
        layer = 2
        base = layer * 96
        g = View(self.modT.ap[:, base + 32:base + 48], self.modT.bufs)
        ih = self.inp("mhT", [D, T], BF16)
        iso = self.inp("sigoT_in", [D, T], BF16)
        wout = self.inp("m_wout", [D, D], F32)
        hng = S.sb("hng", [128, 16], F32)
        self.load(hng, self.inp("hngT", [128, 16], F32))
        def hsrc(hd):
            def src(c, tt):
                t_ = p.scr("mh_s", [128, 512], BF16, 3)
                cc = hd * 4 + c
                self.load(t_, ih[cc * 128:(cc + 1) * 128, tt * 512:(tt + 1) * 512])
                return t_
            return src
        for hd in range(4):
            hv = View(hng.ap[:, hd * 4:(hd + 1) * 4], hng.bufs)
            norm_fm(p, hsrc(hd), 4, 2, hv, None, (lambda hd: (lambda c, tt: self.hT.v(hd * 4 + c, tt)))(hd), 1.0 / 512)
        for c in range(16):
            so = p.scr("so_in", [128, T], BF16, 2)
            self.load(so, iso[c * 128:(c + 1) * 128, :])
            hv = self.hT.vc(c)
            S.op("pool", lambda e, hv=hv, so=so: e.tensor_tensor(out=hv.ap, in0=hv.ap, in1=so.ap, op=ALU.mult), reads=[hv, so], writes=[hv])
        linear(p, D, [[(wout, n * 512, 512)] for n in range(4)], self.hT.v, 2, self.resid_evac(g))


def build_mods():
    p = Prog(wslots=2, slot_elems=16 * 512)
    S = p.S
    ic = p.inp("c_col", [128, 16], F32)
    iw = p.inp("mod_w_s", [4, D, 1536], F32)
    ib = p.inp("mod_b_s", [1, 4 * 1536], F32)
    oo = p.out("modT_s", [128, 48], F32)
    cc = S.sb("cc", [128, 16], F32)
    p.load(cc, ic)
    cb = S.sb("cb", [128, 16], BF16)
    S.op("act", lambda e: e.activation(out=cb.ap, in_=cc.ap, func=AF.Silu), reads=[cc], writes=[cb])
    mb = S.sb("mb", [1, 4 * 1536], F32)
    p.load(mb, ib)
    row = S.sb("row", [1, 4 * 1536], F32)
    one1 = S.sb("one1", [1, 1], F32)
    S.op("dve", lambda e: e.memset(one1.ap, 1.0), writes=[one1])
    res = S.sb("res", [128, 48], F32)
    pst = p.banks[7]
    for l in range(4):
        for g in range(3):
            slot = p.wslot()
            sv3 = slot.ap.rearrange("p (k n) -> p k n", n=512)
            S.dma("pool", View(sv3, slot.bufs), View(iw[l, :, g * 512:(g + 1) * 512].rearrange("(k p) n -> p k n", p=128), []))
            ps = p.bank()
            for k in range(16):
                S.op("pe", lambda e, ps=ps, k=k, sv3=sv3: e.matmul(ps.ap[0:1, :], lhsT=cb.ap[:, k:k + 1], rhs=sv3[:, k, :], start=(k == 0), stop=(k == 15)),
                     reads=[slot, cb], writes=[ps])
            c0 = l * 1536 + g * 512
            S.op("dve", lambda e, ps=ps, c0=c0: e.tensor_tensor(out=row.ap[:, c0:c0 + 512], in0=ps.ap[0:1, :], in1=mb.ap[:, c0:c0 + 512], op=ALU.add),
                 reads=[ps, mb], writes=[row])
    for j in range(48):
        S.op("pe", lambda e, j=j: e.matmul(pst.ap[:, j:j + 1], lhsT=row.ap[0:1, j * 128:(j + 1) * 128], rhs=one1.ap, start=True, stop=True),
             reads=[row, one1], writes=[pst])
    S.op("dve", lambda e: e.tensor_copy(out=res.ap, in_=pst.ap[:, 0:48]), reads=[pst], writes=[res])
    p.store(oo, res)
    return p


def build_attn():
    p = Prog(wslots=1, slot_elems=16)
    S = p.S
    scale = 192 ** -0.5
    iqn = p.inp("qn", [2, 128, SEQ], BF16)
    iqp = p.inp("qp", [2, 64, SEQ], BF16)
    ikn = p.inp("kn", [2, 128, SEQ], BF16)
    ikp = p.inp("kp", [64, SEQ], BF16)
    iv = p.inp("v", [SEQ, 256], BF16)
    imask = p.inp("cmask", [128, 4 * 512], BF16)
    oat = p.out("att", [256, SEQ], BF16)
    NT = SEQ // 512
    kn = [[S.sb(f"kn{h}_{t}", [128, 512], BF16) for t in range(NT)] for h in range(2)]
    kp = [S.sb(f"kp_{t}", [64, 512], BF16) for t in range(NT)]
    vv = [S.sb(f"v_{t}", [128, 4, 256], BF16) for t in range(NT)]
    mask = S.sb("cmask", [128, 4 * 512], BF16)
    p.load(mask, imask)
    for t in range(NT):
        cs = slice(t * 512, (t + 1) * 512)
        for h in range(2):
            p.load(kn[h][t], ikn[h, :, cs])
        p.load(kp[t], ikp[:, cs])
        p.load(vv[t], iv[cs, :].rearrange("(b p) d -> p b d", p=128))
    ones = p.ones
    bias = []
    for h in range(2):
        mx = {}
        for nm in ("k", "q"):
            m_ = S.sb(f"mx{nm}{h}", [128, 1], F32)
            S.op("dve", lambda e, m_=m_: e.memset(m_.ap, 0.0), writes=[m_])
            mx[nm] = m_
        for t in range(NT):
            cs = slice(t * 512, (t + 1) * 512)
            for nm in ("k", "q"):
                if nm == "k":
                    a_, b_ = kn[h][t], kp[t]
                else:
                    a_ = p.scr("qn_n", [128, 512], BF16, 2)
                    b_ = p.scr("qp_n", [64, 512], BF16, 2)
                    p.load(a_, iqn[h, :, cs])
                    p.load(b_, iqp[h, :, cs])
                s1 = p.scr("nsq1", [128, 512], BF16, 2)
                s2 = p.scr("nsq2", [64, 512], BF16, 2)
                S.op("act", lambda e, s1=s1, a_=a_: e.activation(out=s1.ap, in_=a_.ap, func=AF.Square), reads=[a_], writes=[s1])
                S.op("act", lambda e, s2=s2, b_=b_: e.activation(out=s2.ap, in_=b_.ap, func=AF.Square), reads=[b_], writes=[s2])
                ps = p.bank(6, 8)
                S.op("pe", lambda e, ps=ps, s1=s1: e.matmul(ps.ap, lhsT=ones.ap, rhs=s1.ap, start=True, stop=False), reads=[s1, ones], writes=[ps])
                S.op("pe", lambda e, ps=ps, s2=s2: e.matmul(ps.ap, lhsT=ones.ap[:64, :], rhs=s2.ap, start=False, stop=True), reads=[s2, ones], writes=[ps])
                tm = p.scr("nmx", [128, 1], F32, 2)
                S.op("dve", lambda e, tm=tm, ps=ps: e.reduce_max(out=tm.ap, in_=ps.ap, axis=mybir.AxisListType.X), reads=[ps], writes=[tm])
                m_ = mx[nm]
                S.op("dve", lambda e, tm=tm, m_=m_: e.tensor_tensor(out=m_.ap, in0=m_.ap, in1=tm.ap, op=ALU.max), reads=[tm, m_], writes=[m_])
        bb = S.sb(f"bias{h}", [128, 1], F32)
        S.op("dve", lambda e, bb=bb, mx=mx: e.tensor_tensor(out=bb.ap, in0=mx["k"].ap, in1=mx["q"].ap, op=ALU.mult), reads=[mx["k"], mx["q"]], writes=[bb])
        S.op("act", lambda e, bb=bb: e.activation(out=bb.ap, in_=bb.ap, func=AF.Sqrt), reads=[bb], writes=[bb])
        S.op("dve", lambda e, bb=bb: e.tensor_scalar(out=bb.ap, in0=bb.ap, scalar1=-scale, scalar2=None, op0=ALU.mult), reads=[bb], writes=[bb])
        bias.append(bb)
    def qtile(h, qi):
        cs = slice(qi * 512, (qi + 1) * 512)
        qn = p.scr("qn_m", [128, 512], BF16, 2)
        qp = p.scr("qp_m", [64, 512], BF16, 2)
        p.load(qn, iqn[h, :, cs])
        p.load(qp, iqp[h, :, cs])
        po = p.bank(4, 6)
        psum_ = p.bank(6, 8)
        nkb = 4 * (qi + 1)
        bh = bias[h]
        def emit_s(kb):
            t, r = kb // 4, kb % 4
            ks = slice(r * 128, (r + 1) * 128)
            ps = p.bank(0, 4)
            knt, kpt = kn[h][t], kp[t]
            S.op("pe", lambda e, ps=ps, knt=knt, ks=ks: e.matmul(ps.ap, lhsT=knt.ap[:, ks], rhs=qn.ap, start=True, stop=False),
                 reads=[knt, qn], writes=[ps])
            S.op("pe", lambda e, ps=ps, kpt=kpt, ks=ks: e.matmul(ps.ap, lhsT=kpt.ap[:, ks], rhs=qp.ap, start=False, stop=True),
                 reads=[kpt, qp], writes=[ps])
            pt = p.scr("pT", [128, 512], BF16, 5)
            S.op("act", lambda e, pt=pt, ps=ps: e.activation(out=pt.ap, in_=ps.ap, func=AF.Exp, bias=bh.ap[:, 0:1], scale=scale),
                 reads=[ps, bh], writes=[pt])
            if t == qi:
                S.op("pool", lambda e, pt=pt, r=r: e.tensor_tensor(out=pt.ap, in0=pt.ap, in1=mask.ap[:, r * 512:(r + 1) * 512], op=ALU.mult),
                     reads=[pt, mask], writes=[pt])
            return pt

        LOOK = 2
        pts = {}
        for kb in range(min(LOOK, nkb)):
            pts[kb] = emit_s(kb)
        for kb in range(nkb):
            if kb + LOOK < nkb:
                pts[kb + LOOK] = emit_s(kb + LOOK)
            pt = pts.pop(kb)
            t, r = kb // 4, kb % 4
            vt_ = vv[t]
            S.op("pe", lambda e, pt=pt, vt_=vt_, r=r, kb=kb: e.matmul(po.ap, lhsT=vt_.ap[:, r, h * 128:(h + 1) * 128], rhs=pt.ap, start=(kb == 0), stop=(kb == nkb - 1)),
                 reads=[vt_, pt], writes=[po])
            S.op("pe", lambda e, pt=pt, kb=kb: e.matmul(psum_.ap, lhsT=ones.ap, rhs=pt.ap, start=(kb == 0), stop=(kb == nkb - 1)),
                 reads=[ones, pt], writes=[psum_])
        rs = p.scr("rs", [128, 512], F32, 2)
        S.op("dve", lambda e: e.reciprocal(out=rs.ap, in_=psum_.ap), reads=[psum_], writes=[rs])
        ot = p.scr("ot", [128, 512], BF16, 2)
        S.op("dve", lambda e: e.tensor_tensor(out=ot.ap, in0=po.ap, in1=rs.ap, op=ALU.mult), reads=[po, rs], writes=[ot])
        p.store(oat[h * 128:(h + 1) * 128, cs], ot)

    for h in range(2):
        for qi in range(NT):
            qtile(h, qi)
    return p


def build_mlstm():
    p = Prog(wslots=1, slot_elems=16)
    S = p.S
    NCH = SEQ // 128
    iq = p.inp("qT", [256, SEQ], BF16)
    ik = p.inp("kT", [256, SEQ], BF16)
    iktm = p.inp("k_tm", [SEQ, 256], BF16)
    ivtm = p.inp("v_tm", [SEQ, 256], BF16)
    ia = p.inp("lf", [128, NCH], F32)
    ii = p.inp("ig", [128, NCH], F32)
    itri = p.inp("tri", [128, 128], F32)
    oh = p.out("h_tm", [SEQ, 256], BF16)
    qT = [S.sb(f"qT{d}", [128, SEQ], BF16) for d in range(2)]
    kT = [S.sb(f"kT{d}", [128, SEQ], BF16) for d in range(2)]
    G = 8
    ktm = [S.sb(f"ktm{g}", [128, G, 256], BF16) for g in range(NCH // G)]
    vtm = [S.sb(f"vtm{g}", [128, G, 257], BF16) for g in range(NCH // G)]
    for d in range(2):
        for hf in range(4):
            cs = slice(hf * 2048, (hf + 1) * 2048)
            S.dma("sp", View(qT[d].ap[:, cs], qT[d].bufs), View(iq[d * 128:(d + 1) * 128, cs], []))
            S.dma("sp", View(kT[d].ap[:, cs], kT[d].bufs), View(ik[d * 128:(d + 1) * 128, cs], []))
    for g in range(NCH // G):
        rs_ = slice(g * G * 128, (g + 1) * G * 128)
        p.load(ktm[g], iktm[rs_, :].rearrange("(b p) d -> p b d", p=128))
        S.op("pool", lambda e, g=g: e.memset(vtm[g].ap[:, :, 256:257], 1.0), writes=[vtm[g]])
        S.dma("sp", View(vtm[g].ap[:, :, 0:256], vtm[g].bufs), View(ivtm[rs_, :].rearrange("(b p) d -> p b d", p=128), []))
    a = S.sb("lf", [128, NCH], F32)
    ig = S.sb("ig", [128, NCH], F32)
    tri = S.sb("tri", [128, 128], F32)
    tri_b = S.sb("tri_b", [128, 128], BF16)
    onesf = S.sb("onesf", [128, 128], F32)
    p.load(a, ia)
    p.load(ig, ii)
    p.load(tri, itri)
    S.op("dve", lambda e: e.memset(onesf.ap, 1.0), writes=[onesf])
    F_ = S.sb("F", [128, NCH], F32)
    FL = S.sb("FL", [128, NCH], F32)
    ps = p.bank(6, 8)
    S.op("pe", lambda e: e.matmul(ps.ap[:, :NCH], lhsT=tri.ap, rhs=a.ap, start=True, stop=True), reads=[tri, a], writes=[ps])
    S.op("dve", lambda e: e.tensor_copy(out=F_.ap, in_=ps.ap[:, :NCH]), reads=[ps], writes=[F_])
    ps2 = p.bank(6, 8)
    S.op("pe", lambda e: e.matmul(ps2.ap[:, :NCH], lhsT=onesf.ap, rhs=a.ap, start=True, stop=True), reads=[onesf, a], writes=[ps2])
    S.op("dve", lambda e: e.tensor_copy(out=FL.ap, in_=ps2.ap[:, :NCH]), reads=[ps2], writes=[FL])
    imF = S.sb("imF", [128, NCH], F32)
    S.op("dve", lambda e: e.tensor_tensor(out=imF.ap, in0=ig.ap, in1=F_.ap, op=ALU.subtract), reads=[ig, F_], writes=[imF])
    w_ = S.sb("w_s", [128, NCH], F32)
    S.op("dve", lambda e: e.tensor_tensor(out=w_.ap, in0=imF.ap, in1=FL.ap, op=ALU.add), reads=[imF, FL], writes=[w_])
    S.op("act", lambda e: e.activation(out=w_.ap, in_=w_.ap, func=AF.Exp), reads=[w_], writes=[w_])
    dec = S.sb("decay", [128, NCH], F32)
    S.op("act", lambda e: e.activation(out=dec.ap, in_=FL.ap, func=AF.Exp), reads=[FL], writes=[dec])
    C = [S.sb(f"C{d}", [128, 257], F32) for d in range(2)]
    Cb = [[S.sb(f"Cb{d}_{par}", [128, 257], BF16) for d in range(2)] for par in range(2)]
    for d in range(2):
        S.op("dve", lambda e, d=d: e.memset(C[d].ap, 0.0), writes=[C[d]])
        for par in range(2):
            S.op("pool", lambda e, d=d, par=par: e.memset(Cb[par][d].ap, 0.0), writes=[Cb[par][d]])
    st = {}

    def prep(c):
        cs = slice(c * 128, (c + 1) * 128)
        g, gl = c // G, c % G
        ta = p.scr("ta", [128, 128], F32, 3)
        S.op("act", lambda e: e.activation(out=ta.ap, in_=tri.ap, func=AF.Identity, bias=0.0, scale=a.ap[:, c:c + 1]), reads=[tri, a], writes=[ta])
        pf = p.bank(6, 8)
        S.op("pe", lambda e: e.matmul(pf.ap[:, :128], lhsT=onesf.ap, rhs=ta.ap, start=True, stop=True), reads=[onesf, ta], writes=[pf])
        z = p.scr("z", [128, 128], F32, 3)
        S.op("dve", lambda e: e.tensor_scalar(out=z.ap, in0=pf.ap[:, :128], scalar1=imF.ap[:, c:c + 1], scalar2=20.0, op0=ALU.add, op1=ALU.min),
             reads=[pf, imF], writes=[z])
        S.op("act", lambda e: e.activation(out=z.ap, in_=z.ap, func=AF.Exp), reads=[z], writes=[z])
        dm = p.scr("dm", [128, 128], F32, 3)
        S.op("dve", lambda e: e.tensor_tensor(out=dm.ap, in0=z.ap, in1=tri.ap, op=ALU.mult), reads=[z, tri], writes=[dm])
        ef = p.scr("ef", [128, 128], F32, 3)
        S.op("act", lambda e: e.activation(out=ef.ap, in_=pf.ap[:, :128], func=AF.Exp), reads=[pf], writes=[ef])
        qt = p.scr("qtil", [128, 2, 128], BF16, 3)
        for d in range(2):
            S.op("dve" if d == 0 else "pool", lambda e, d=d: e.tensor_tensor(out=qt.ap[:, d, :], in0=qT[d].ap[:, cs], in1=ef.ap, op=ALU.mult), reads=[qT[d], ef], writes=[qt])
        kw = p.scr("kw", [128, 256], BF16, 3)
        S.op("act", lambda e: e.activation(out=kw.ap, in_=ktm[g].ap[:, gl, :], func=AF.Identity, bias=0.0, scale=w_.ap[:, c:c + 1]),
             reads=[ktm[g], w_], writes=[kw])
        pS = p.bank(0, 2)
        for d in range(2):
            S.op("pe", lambda e, d=d: e.matmul(pS.ap[:, :128], lhsT=kT[d].ap[:, cs], rhs=qT[d].ap[:, cs], start=(d == 0), stop=(d == 1)),
                 reads=[kT[d], qT[d]], writes=[pS])
        st[c] = (dm, qt, kw, pS)

    def update(c):
        g, gl = c // G, c % G
        dm, qt, kw, pS = st[c]
        par = c % 2
        for d in range(2):
            pc = p.bank(4, 6)
            S.op("pe", lambda e, pc=pc, d=d: e.matmul(pc.ap[:, :257], lhsT=kw.ap[:, d * 128:(d + 1) * 128], rhs=vtm[g].ap[:, gl, :], start=True, stop=True),
                 reads=[kw, vtm[g]], writes=[pc])
            S.op("dve", lambda e, pc=pc, d=d: e.scalar_tensor_tensor(out=C[d].ap, in0=C[d].ap, scalar=dec.ap[:, c:c + 1], in1=pc.ap[:, :257], op0=ALU.mult, op1=ALU.add),
                 reads=[C[d], dec, pc], writes=[C[d]])
            S.op("act", lambda e, d=d: e.activation(out=Cb[par][d].ap, in_=C[d].ap, func=AF.Copy), reads=[C[d]], writes=[Cb[par][d]])

    def output(c):
        g, gl = c // G, c % G
        dm, qt, kw, pS = st.pop(c)
        prev = Cb[(c - 1) % 2]
        pt = p.scr("pTm", [128, 128], BF16, 3)
        S.op("dve", lambda e: e.tensor_tensor(out=pt.ap, in0=pS.ap[:, :128], in1=dm.ap, op=ALU.mult), reads=[pS, dm], writes=[pt])
        pn = p.bank(2, 4)
        S.op("pe", lambda e: e.matmul(pn.ap[:, :257], lhsT=pt.ap, rhs=vtm[g].ap[:, gl, :], start=True, stop=False), reads=[pt, vtm[g]], writes=[pn])
        for d in range(2):
            S.op("pe", lambda e, d=d: e.matmul(pn.ap[:, :257], lhsT=qt.ap[:, d, :], rhs=prev[d].ap, start=False, stop=(d == 1)), reads=[qt, prev[d]], writes=[pn])
        den = p.scr("den", [128, 1], F32, 3)
        S.op("act", lambda e: e.activation(out=den.ap, in_=pn.ap[:, 256:257], func=AF.Abs), reads=[pn], writes=[den])
        S.op("dve", lambda e: e.tensor_scalar(out=den.ap, in0=den.ap, scalar1=1.0, scalar2=None, op0=ALU.max), reads=[den], writes=[den])
        S.op("dve", lambda e: e.reciprocal(out=den.ap, in_=den.ap), reads=[den], writes=[den])
        if gl == 0:
            st["hout"] = p.scr("hout", [128, G, 256], BF16, 2)
        hout = st["hout"]
        S.op("act", lambda e: e.activation(out=hout.ap[:, gl, :], in_=pn.ap[:, :256], func=AF.Identity, bias=0.0, scale=den.ap[:, 0:1]),
             reads=[pn, den], writes=[hout])
        if gl == G - 1:
            p.store(oh[g * G * 128:(g + 1) * G * 128, :].rearrange("(b p) d -> p b d", p=128), hout)

    prep(0)
    for c in range(NCH):
        if c < NCH - 1:
            update(c)
        output(c)
        if c + 1 < NCH:
            prep(c + 1)
    return p


_cache = {}


def _prog(name, fn):
    if name not in _cache:
        p = fn()
        p.finish()
        _cache[name] = p
    return _cache[name]


def _run(p, in_maps):
    maps = []
    for m in in_maps:
        maps.append({k: np.ascontiguousarray(m[k]) for k in p.in_names})
    res = run_bass_kernel_spmd(p.nc, maps, core_ids=list(range(NCORES)))
    return res.results


def _pp(v, nch):
    return np.ascontiguousarray(np.asarray(v, np.float32).reshape(nch, 128).T)


def _b_conv1(layer, widx, first):
    def f():
        p = TL()
        p.load_x("xT")
        p.conv1(layer, widx)
        return p
    return f


def _b_L1():
    p = TL()
    p.load_x("xT")
    p.conv2(0, 0)
    p.ffn(0)
    p.store_x("x1T")
    return p


def _b_L1b():
    p = TL(resident=False, wslots=2)
    p.load_x("xT")
    p.mla_pre()
    return p


def _b_L2():
    p = TL()
    p.load_x("xT")
    p.attn_out()
    p.ffn(1)
    p.store_x("x2T")
    return p


def _b_L2b():
    p = TL(resident=False, wslots=2)
    p.load_x("xT")
    p.mlstm_pre()
    return p


def _b_L3():
    p = TL()
    p.load_x("xT")
    p.mlstm_post()
    p.ffn(2)
    p.store_x("x3T")
    p.conv1(3, 1)
    return p


def _b_L4():
    p = TL()
    p.load_x("xT")
    p.conv2(3, 1)
    p.ffn(3)
    p.final()
    return p


def _tsplit(a):
    return [np.ascontiguousarray(a[:, r * T:(r + 1) * T]) for r in range(NCORES)]


def kernel(x, c, positions, mod_w, mod_b, norm1_g, norm2_g, ffn_w_gate, ffn_w_up, ffn_w_down,
           conv_w_in, conv_w, conv_w_out,
           mla_w_dq, mla_q_norm_g, mla_w_uq, mla_w_dkv, mla_kv_norm_g, mla_w_ukv, mla_w_o,
           mlstm_w_in, mlstm_b_gates, mlstm_head_norm_g, mlstm_w_out, final_norm_g, _debug=None):
    f32 = np.float32
    R = range(NCORES)
    dbg = _debug if _debug is not None else {}
    pm = _prog("mods", build_mods)
    c_col = _pp(np.asarray(c, f32).reshape(-1), 16)
    maps = []
    for r in R:
        cs = slice(r * 1536, (r + 1) * 1536)
        maps.append({"c_col": c_col, "mod_w_s": np.asarray(mod_w)[:, :, cs],
                     "mod_b_s": np.asarray(mod_b)[:, cs].reshape(1, -1)})
    res = _run(pm, maps)
    modT = np.zeros((128, 4, 96), f32)
    for r in R:
        modT[:, :, r * 12:(r + 1) * 12] = res[r]["modT_s"].reshape(128, 4, 12)
    modT = modT.reshape(128, 384)
    dbg["modT"] = modT
    if dbg.get("stop") == "mods":
        return None
    ng1T = np.concatenate([_pp(norm1_g[l], 16) for l in range(4)], axis=1)
    ng2T = np.concatenate([_pp(norm2_g[l], 16) for l in range(4)], axis=1)
    common = {"modT": modT, "ng1T": ng1T, "ng2T": ng2T}

    def ffnw(l):
        return {f"wg{l}": ffn_w_gate[l], f"wu{l}": ffn_w_up[l], f"wd{l}": ffn_w_down[l]}

    def convw(j):
        cwp = np.concatenate([_pp(conv_w[j][t], 16) for t in range(3)], axis=1)
        return {f"cw{j}": cwp, f"cw_out{j}": conv_w_out[j]}

    def halos(cu_list):
        hs = [np.zeros((D, 2), f32)]
        for r in range(1, NCORES):
            hs.append(np.ascontiguousarray(cu_list[r - 1][:, T - 2:T]))
        return hs

    xT = _tsplit(np.ascontiguousarray(np.asarray(x, f32)[0].T))
    p0 = _prog("L0", _b_conv1(0, 0, True))
    res = _run(p0, [dict(common, xT=xT[r], cw_in0=conv_w_in[0]) for r in R])
    cB = [res[r]["convB"] for r in R]
    cCU = [res[r]["convCU"] for r in R]
    dbg["convB0"] = cB
    dbg["convCU0"] = cCU
    if dbg.get("stop") == "L0":
        return None
    hl = halos(cCU)
    p1 = _prog("L1", _b_L1)
    w_uq = np.asarray(mla_w_uq[0]).reshape(768, 16, 192)
    w_uqn = np.ascontiguousarray(w_uq[:, :, :128].reshape(768, 2048))
    pe = w_uq[:, :, 128:]
    w_uqp = np.ascontiguousarray(pe.reshape(768, 1024))
    w_uqps = np.ascontiguousarray(np.concatenate([pe[:, :, 32:], pe[:, :, :32]], axis=2).reshape(768, 1024))
    w_dkv = np.asarray(mla_w_dkv[0])
    w_dkvc = np.ascontiguousarray(w_dkv[:, :512])
    w_dkvp = np.ascontiguousarray(w_dkv[:, 512:])
    w_dkvps = np.ascontiguousarray(np.concatenate([w_dkv[:, 544:], w_dkv[:, 512:544]], axis=1))
    w_ukv = np.asarray(mla_w_ukv[0]).reshape(512, 16, 256)
    w_uk = np.ascontiguousarray(w_ukv[:, :, :128].reshape(512, 2048))
    w_uv = np.ascontiguousarray(w_ukv[:, :, 128:].reshape(512, 2048))
    pidx = np.arange(128)
    invf = (10000.0 ** (-2.0 * (pidx % 32).astype(np.float64) / 64)).astype(f32)
    sgn = np.where((pidx % 64) < 32, 1.0, -1.0).astype(f32)
    ropec = np.stack([invf, sgn], axis=1).astype(f32)
    pos = np.asarray(positions).reshape(-1).astype(np.int32)
    mla_in = {"w_dq": mla_w_dq[0], "w_uqn": w_uqn, "w_uqp": w_uqp, "w_uqps": w_uqps, "w_dkvc": w_dkvc, "w_dkvp": w_dkvp,
              "w_dkvps": w_dkvps, "w_uk": w_uk, "w_uv": w_uv, "qngT": _pp(mla_q_norm_g[0], 6), "kvngT": _pp(mla_kv_norm_g[0], 4),
              "ropec": ropec}
    maps = []
    for r in R:
        m = dict(common, xT=xT[r], convB_in=cB[r], convCU_in=cCU[r], halo=hl[r])
        m.update(convw(0)); m.update(ffnw(0))
        maps.append(m)
    res = _run(p1, maps)
    x1T = [res[r]["x1T"] for r in R]
    dbg["x1T"] = x1T
    if dbg.get("stop") == "L1":
        return None
    p1b = _prog("L1b", _b_L1b)
    maps = []
    for r in R:
        m = dict(common, xT=x1T[r])
        m.update(mla_in)
        m["pos_rep"] = np.ascontiguousarray(np.broadcast_to(pos[r * T:(r + 1) * T][None, :], (128, T)))
        maps.append(m)
    res = _run(p1b, maps)
    qn = np.concatenate([res[r]["qnT"] for r in R], axis=1)
    qp = np.concatenate([res[r]["qpT"] for r in R], axis=1)
    kn = np.concatenate([res[r]["knT"] for r in R], axis=1)
    kp = np.concatenate([res[r]["kpT"] for r in R], axis=1)
    vt = np.concatenate([res[r]["v_tm"] for r in R], axis=0)
    dbg.update(qn=qn, qp=qp, kn=kn, kp=kp, vt=vt)
    if dbg.get("stop") == "L1b":
        return None
    pa = _prog("attn", build_attn)
    jj = np.arange(512)[None, :]
    pp_ = np.arange(128)[:, None]
    cmask = np.concatenate([(jj >= (128 * r_ + pp_)).astype(f32) for r_ in range(4)], axis=1).astype(NPBF)
    maps = []
    for r in R:
        maps.append({"qn": qn[r * 256:(r + 1) * 256].reshape(2, 128, SEQ), "qp": qp[r * 128:(r + 1) * 128].reshape(2, 64, SEQ),
                     "kn": kn[r * 256:(r + 1) * 256].reshape(2, 128, SEQ), "kp": kp, "v": vt[:, r * 256:(r + 1) * 256], "cmask": cmask})
    res = _run(pa, maps)
    att = np.concatenate([res[r]["att"] for r in R], axis=0)
    dbg["att"] = att
    if dbg.get("stop") == "attn":
        return None
    attT = _tsplit(att)
    p2 = _prog("L2", _b_L2)
    w_in = np.asarray(mlstm_w_in[0])
    ml_in = {"w_o": mla_w_o[0], "m_wq": np.ascontiguousarray(w_in[:, :1024]), "m_wk": np.ascontiguousarray(w_in[:, 1024:2048]),
             "m_wv": np.ascontiguousarray(w_in[:, 2048:4096]), "m_wo": np.ascontiguousarray(w_in[:, 4096:6144]),
             "m_wg": np.ascontiguousarray(w_in[:, 6144:6152]), "m_bg": np.asarray(mlstm_b_gates[0], f32).reshape(8, 1)}
    maps = []
    for r in R:
        m = dict(common, xT=x1T[r], attT=attT[r], w_o=ml_in["w_o"])
        m.update(ffnw(1))
        maps.append(m)
    res = _run(p2, maps)
    x2T = [res[r]["x2T"] for r in R]
    dbg["x2T"] = x2T
    if dbg.get("stop") == "L2":
        return None
    p2b = _prog("L2b", _b_L2b)
    res = _run(p2b, [dict(common, xT=x2T[r], **ml_in) for r in R])
    sigo = [res[r]["sigoT"] for r in R]
    mq = np.concatenate([res[r]["mqT"] for r in R], axis=1)
    mk = np.concatenate([res[r]["mkT"] for r in R], axis=1)
    mktm = np.concatenate([res[r]["mk_tm"] for r in R], axis=0)
    mvtm = np.concatenate([res[r]["mv_tm"] for r in R], axis=0)
    gi = np.concatenate([res[r]["gi"] for r in R], axis=1)
    gf = np.concatenate([res[r]["gf"] for r in R], axis=1)
    dbg.update(mq=mq, mk=mk, mktm=mktm, mvtm=mvtm, gi=gi, gf=gf, sigo=sigo)
    if dbg.get("stop") == "L2b":
        return None
    pl = _prog("mlstm", build_mlstm)
    tri = np.triu(np.ones((128, 128), f32))
    maps = []
    for r in R:
        hd, hf = r // 2, r % 2
        maps.append({"qT": mq[hd * 256:(hd + 1) * 256], "kT": mk[hd * 256:(hd + 1) * 256],
                     "k_tm": mktm[:, hd * 256:(hd + 1) * 256], "v_tm": mvtm[:, hd * 512 + hf * 256: hd * 512 + (hf + 1) * 256],
                     "lf": np.ascontiguousarray(gf[4 + hd].reshape(SEQ // 128, 128).T), "ig": np.ascontiguousarray(gi[hd].reshape(SEQ // 128, 128).T),
                     "tri": tri})
    res = _run(pl, maps)
    mh = np.concatenate([res[r]["h_tm"] for r in R], axis=1)
    dbg["mh"] = mh
    if dbg.get("stop") == "mlstm":
        return None
    mhT = _tsplit(np.ascontiguousarray(mh.T))
    p3 = _prog("L3", _b_L3)
    maps = []
    for r in R:
        m = dict(common, xT=x2T[r], mhT=mhT[r], sigoT_in=sigo[r], m_wout=mlstm_w_out[0], hngT=_pp(mlstm_head_norm_g[0], 16), cw_in1=conv_w_in[1])
        m.update(ffnw(2))
        maps.append(m)
    res = _run(p3, maps)
    x3T = [res[r]["x3T"] for r in R]
    cB = [res[r]["convB"] for r in R]
    cCU = [res[r]["convCU"] for r in R]
    dbg["x3T"] = x3T
    if dbg.get("stop") == "L3":
        return None
    hl = halos(cCU)
    p4 = _prog("L4", _b_L4)
    maps = []
    for r in R:
        m = dict(common, xT=x3T[r], convB_in=cB[r], convCU_in=cCU[r], halo=hl[r], fngT=_pp(final_norm_g, 16))
        m.update(convw(1)); m.update(ffnw(3))
        maps.append(m)
    res = _run(p4, maps)
    outT = np.concatenate([res[r]["outT"] for r in R], axis=1)
    return np.ascontiguousarray(outT.T)[None].astype(np.float32)
```

```python
import math
from contextlib import ExitStack
import numpy as np
import ml_dtypes
import concourse.bass as bass
import concourse.mybir as mybir
from concourse.bass_utils import run_bass_kernel_spmd

F32 = mybir.dt.float32
BF16 = mybir.dt.bfloat16
I32 = mybir.dt.int32
ALU = mybir.AluOpType
AF = mybir.ActivationFunctionType
NPBF = ml_dtypes.bfloat16

NCORES = 8
D = 2048
SEQ = 8192
T = SEQ // NCORES
DFF = 5632
EPS = 1e-6
COMPUTE = ("pe", "act", "dve", "pool")
ALLENG = ("pe", "act", "dve", "pool", "sp")


class Buf:
    __slots__ = ("name", "writer", "readers", "dma_sem", "dma_cnt", "dma_last")

    def __init__(self, name):
        self.name = name
        self.writer = None
        self.readers = []
        self.dma_sem = None
        self.dma_cnt = 0
        self.dma_last = None


class View:
    __slots__ = ("ap", "bufs")

    def __init__(self, ap, bufs):
        self.ap = ap
        self.bufs = list(bufs)

    def __getitem__(self, idx):
        return View(self.ap[idx], self.bufs)


class Op:
    __slots__ = ("eng", "fn", "waits", "signal", "idx", "is_dma", "dsem", "dval")

    def __init__(self, eng, fn):
        self.eng = eng
        self.fn = fn
        self.waits = []
        self.signal = False
        self.idx = None
        self.is_dma = False
        self.dsem = None
        self.dval = 0


class Sched:
    def __init__(self, nc, same_engine_sync=True):
        self.nc = nc
        self.es = ExitStack()
        self.ops = {e: [] for e in ALLENG}
        self.same_engine_sync = same_engine_sync
        self.nbuf = 0

    def buf(self, name=None):
        self.nbuf += 1
        return Buf(name or f"b{self.nbuf}")

    def sb(self, name, shape, dtype):
        t = self.es.enter_context(self.nc.sbuf_tensor("sb_" + name, list(shape), dtype))
        return View(t[:], [self.buf(name)])

    def ps(self, name, shape, dtype=F32):
        t = self.es.enter_context(self.nc.psum_tensor(name, list(shape), dtype))
        return View(t[:], [self.buf(name)])

    def _deps(self, op, reads, writes):
        deps = []
        for v in reads:
            for b in v.bufs:
                if b.writer is not None:
                    deps.append(b.writer)
        for v in writes:
            for b in v.bufs:
                if b.writer is not None:
                    deps.append(b.writer)
                deps.extend(b.readers)
        seen = set()
        for d in deps:
            if d is op or id(d) in seen:
                continue
            seen.add(id(d))
            if (not d.is_dma) and d.eng == op.eng:
                if d.eng == "pe" or not self.same_engine_sync:
                    continue
            op.waits.append(d)
        for v in reads:
            for b in v.bufs:
                b.readers.append(op)
        for v in writes:
            for b in v.bufs:
                b.writer = op
                b.readers = []

    def op(self, eng, fn, reads=(), writes=()):
        o = Op(eng, fn)
        self._deps(o, reads, writes)
        self.ops[eng].append(o)
        return o

    def dma(self, eng, out, in_, chan=None, extra=None):
        pairs = [(out, in_)] + list(extra or [])
        if chan is None:
            chan = out.bufs[0] if out.bufs else in_.bufs[0]
        if chan.dma_sem is None:
            chan.dma_sem = self.es.enter_context(self.nc.semaphore("d_" + chan.name))
        sem = chan.dma_sem

        def fn(e, pairs=pairs, sem=sem):
            for (o_, i_) in pairs:
                e.dma_start(out=o_.ap, in_=i_.ap).then_inc(sem, 16)
            return None

        o = Op(eng, fn)
        o.is_dma = True
        o.dsem = sem
        chan.dma_cnt += 16 * len(pairs)
        o.dval = chan.dma_cnt
        if chan.dma_last is not None:
            o.waits.append(chan.dma_last)
        chan.dma_last = o
        self._deps(o, [p[1] for p in pairs], [p[0] for p in pairs])
        self.ops[eng].append(o)
        return o

    def emit(self, final_waits=()):
        nc = self.nc
        sems = {e: self.es.enter_context(nc.semaphore("s_" + e)) for e in COMPUTE}
        for e in ALLENG:
            for o in self.ops[e]:
                for d in o.waits:
                    if not d.is_dma:
                        d.signal = True
        for fo in final_waits:
            if not fo.is_dma:
                fo.signal = True
        for e in ALLENG:
            c = 0
            for o in self.ops[e]:
                if o.signal and not o.is_dma:
                    c += 1
                    o.idx = c

        def run_stream(e, eng):
            seen = {}
            for o in self.ops[e]:
                for d in o.waits:
                    if d.is_dma:
                        key = ("d", id(d.dsem))
                        if seen.get(key, 0) >= d.dval:
                            continue
                        seen[key] = d.dval
                        eng.wait_ge(d.dsem, d.dval)
                    else:
                        key = ("c", d.eng)
                        if seen.get(key, 0) >= d.idx:
                            continue
                        seen[key] = d.idx
                        eng.wait_ge(sems[d.eng], d.idx)
                ins = o.fn(eng)
                if o.signal and not o.is_dma:
                    ins.then_inc(sems[e], 1)
            if e == "sp":
                for fo in final_waits:
                    if fo.is_dma:
                        eng.wait_ge(fo.dsem, fo.dval)
                    else:
                        eng.wait_ge(sems[fo.eng], fo.idx)

        with nc.Block() as block:
            @block.tensor
            def _(eng):
                run_stream("pe", eng)

            @block.scalar
            def _(eng):
                run_stream("act", eng)

            @block.vector
            def _(eng):
                run_stream("dve", eng)

            @block.gpsimd
            def _(eng):
                run_stream("pool", eng)

            @block.sync
            def _(eng):
                run_stream("sp", eng)
        self.es.close()


class FM:
    def __init__(self, p, name, nch, Tn, dt, tw=512):
        self.t = p.S.es.enter_context(p.nc.sbuf_tensor("fm_" + name, [128, nch, Tn], dt))
        self.tw = tw
        self.ntt = Tn // tw
        self.b = [[p.S.buf(f"{name}_{c}_{t}") for t in range(self.ntt)] for c in range(nch)]

    def v(self, c, tt, rows=128):
        return View(self.t[:rows, c, tt * self.tw:(tt + 1) * self.tw], [self.b[c][tt]])

    def vc(self, c, rows=128):
        return View(self.t[:rows, c, :], self.b[c])


class Prog:
    def __init__(self, wslots=3, slot_elems=8192):
        self.nc = bass.Bass("TRN2", target_bir_lowering=False)
        self.S = Sched(self.nc)
        self.in_names = []
        self.out_names = []
        self.out_ops = []
        self.banks = [self.S.ps(f"psb{i}", [128, 512], F32) for i in range(8)]
        self.rr = {}
        self.scr_pool = {}
        self.ones = self.S.sb("ones_bf", [128, 128], BF16)
        o = self.ones
        self.S.op("pool", lambda e: e.memset(o.ap, 1.0), writes=[o])
        self.slots = [self.S.sb(f"wslot{i}", [128, slot_elems], BF16) for i in range(wslots)]
        self.slot_elems = slot_elems
        self.slot_i = 0

    def inp(self, name, shape, dt):
        self.in_names.append(name)
        return self.nc.dram_tensor(name, list(shape), dt, kind="ExternalInput").ap()

    def out(self, name, shape, dt):
        self.out_names.append(name)
        return self.nc.dram_tensor(name, list(shape), dt, kind="ExternalOutput").ap()

    def load(self, sbv, dram_ap, eng="sp"):
        return self.S.dma(eng, sbv, View(dram_ap, []))

    def store(self, dram_ap, sbv, eng="sp"):
        o = self.S.dma(eng, View(dram_ap, []), sbv, chan=sbv.bufs[0])
        self.out_ops.append(o)
        return o

    def bank(self, lo=0, hi=6):
        k = (lo, hi)
        i = self.rr.get(k, 0)
        self.rr[k] = (i + 1) % (hi - lo)
        return self.banks[lo + i]

    def scr(self, name, shape, dt, n=2):
        if name not in self.scr_pool:
            self.scr_pool[name] = ([self.S.sb(f"{name}{i}", shape, dt) for i in range(n)], [0])
        lst, ctr = self.scr_pool[name]
        v = lst[ctr[0] % len(lst)]
        ctr[0] += 1
        return v

    def wslot(self):
        s = self.slots[self.slot_i % len(self.slots)]
        self.slot_i += 1
        return s

    def finish(self):
        self.S.emit(final_waits=self.out_ops)
        return self.nc


def linear(p, K, groups, rhs, ntt, evac, interleave=False, tw=512):
    S = p.S
    KC = K // 128
    for gi, segs in enumerate(groups):
        slot = p.wslot()
        tot = sum(s[2] for s in segs)
        assert KC * tot <= p.slot_elems, (KC, tot)
        sv3 = slot.ap[:, :KC * tot].rearrange("p (k n) -> p k n", n=tot)
        pairs = []
        offs = []
        off = 0
        for (w, c0, n) in segs:
            pairs.append((View(sv3[:, :, off:off + n], slot.bufs),
                          View(w[:, c0:c0 + n].rearrange("(k p) n -> p k n", p=128), [])))
            offs.append(off)
            off += n
        S.dma("pool", pairs[0][0], pairs[0][1], extra=pairs[1:])
        chunks = []
        for si, (w, c0, n) in enumerate(segs):
            for ci in range(0, n, 128):
                chunks.append((ci // 128, si, offs[si] + ci, min(128, n - ci)))
        if interleave:
            chunks.sort(key=lambda t: (t[0], t[1]))
        for (ci, si, o0, m) in chunks:
            for tt in range(ntt):
                ps = p.bank()
                for k in range(KC):
                    r = rhs(k, tt)
                    S.op("pe", lambda e, ps=ps, k=k, r=r, o0=o0, m=m, sv3=sv3: e.matmul(
                        ps.ap[:m, :tw], lhsT=sv3[:, k, o0:o0 + m], rhs=r.ap, start=(k == 0), stop=(k == KC - 1)),
                        reads=[slot, r], writes=[ps])
                evac(gi, si, ci, m, tt, ps)


def linear_tm(p, K, w, ncols, lhs, ntb, evac, cw=512):
    S = p.S
    KC = K // 128
    gcols = (p.slot_elems // KC) // cw * cw
    for g0 in range(0, ncols, gcols):
        gn = min(gcols, ncols - g0)
        slot = p.wslot()
        sv3 = slot.ap[:, :KC * gn].rearrange("p (k n) -> p k n", n=gn)
        S.dma("pool", View(sv3, slot.bufs), View(w[:, g0:g0 + gn].rearrange("(k p) n -> p k n", p=128), []))
        for tb in range(ntb):
            for c0 in range(0, gn, cw):
                ps = p.bank()
                for k in range(KC):
                    l = lhs(k, tb)
                    S.op("pe", lambda e, ps=ps, k=k, l=l, c0=c0, sv3=sv3: e.matmul(
                        ps.ap[:, :cw], lhsT=l.ap, rhs=sv3[:, k, c0:c0 + cw], start=(k == 0), stop=(k == KC - 1)),
                        reads=[slot, l], writes=[ps])
                evac(tb, g0 + c0, ps)


def norm_fm(p, src, nch, ntt, a, b, dst, inv_n):
    S = p.S
    for tt in range(ntt):
        ps = p.bank(6, 8)
        for c in range(nch):
            sq = p.scr("sq", [128, 512], BF16, 3)
            s_ = src(c, tt)
            S.op("act", lambda e, o=sq, i=s_: e.activation(out=o.ap, in_=i.ap, func=AF.Square), reads=[s_], writes=[sq])
            S.op("pe", lambda e, ps=ps, sq=sq, c=c: e.matmul(ps.ap, lhsT=p.ones.ap, rhs=sq.ap, start=(c == 0), stop=(c == nch - 1)),
                 reads=[sq, p.ones], writes=[ps])
        r = p.scr("rstd", [128, 512], F32, 2)
        S.op("act", lambda e, r=r, ps=ps: e.activation(out=r.ap, in_=ps.ap, func=AF.Sqrt, bias=EPS, scale=inv_n), reads=[ps], writes=[r])
        S.op("dve", lambda e, r=r: e.reciprocal(out=r.ap, in_=r.ap), reads=[r], writes=[r])
        for c in range(nch):
            tmp = p.scr("ntmp", [128, 512], F32, 3)
            s_ = src(c, tt)
            d_ = dst(c, tt)
            S.op("dve", lambda e, tmp=tmp, s_=s_, r=r: e.tensor_tensor(out=tmp.ap, in0=s_.ap, in1=r.ap, op=ALU.mult), reads=[s_, r], writes=[tmp])
            if b is not None:
                S.op("act", lambda e, d_=d_, tmp=tmp, c=c: e.activation(out=d_.ap, in_=tmp.ap, func=AF.Identity, bias=b.ap[:, c:c + 1], scale=a.ap[:, c:c + 1]),
                     reads=[tmp, a, b], writes=[d_])
            else:
                S.op("act", lambda e, d_=d_, tmp=tmp, c=c: e.activation(out=d_.ap, in_=tmp.ap, func=AF.Identity, bias=0.0, scale=a.ap[:, c:c + 1]),
                     reads=[tmp, a], writes=[d_])


class TL(Prog):
    def __init__(self, resident=True, wslots=3):
        super().__init__(wslots=wslots)
        p = self
        self.resident = resident
        if resident:
            self.xT = FM(p, "xT", 16, T, F32)
        self.hT = FM(p, "hT", 16, T, BF16)
        self.modT = self.S.sb("modT", [128, 384], F32)
        self.ng1 = self.S.sb("ng1", [128, 64], F32)
        self.ng2 = self.S.sb("ng2", [128, 64], F32)
        self.load(self.modT, self.inp("modT", [128, 384], F32))
        self.load(self.ng1, self.inp("ng1T", [128, 64], F32))
        self.load(self.ng2, self.inp("ng2T", [128, 64], F32))

    def load_x(self, name):
        xin = self.inp(name, [D, T], F32)
        if not self.resident:
            def src(c, tt):
                t_ = self.scr("xs", [128, 512], F32, 3)
                self.load(t_, xin[c * 128:(c + 1) * 128, tt * 512:(tt + 1) * 512])
                return t_
            self.x_src = src
            return
        self.x_src = self.xT.v
        for c in range(16):
            self.load(self.xT.vc(c), xin[c * 128:(c + 1) * 128, :])

    def store_x(self, name):
        xo = self.out(name, [D, T], F32)
        for c in range(16):
            self.store(xo[c * 128:(c + 1) * 128, :], self.xT.vc(c))

    def adaln(self, layer, which):
        S = self.S
        base = layer * 96 + which * 48
        ng = self.ng1 if which == 0 else self.ng2
        a = S.sb(f"ada_{layer}_{which}", [128, 16], F32)
        S.op("dve", lambda e: e.scalar_tensor_tensor(out=a.ap, in0=self.modT.ap[:, base + 16:base + 32], scalar=1.0,
                                                      in1=ng.ap[:, layer * 16:(layer + 1) * 16], op0=ALU.add, op1=ALU.mult),
             reads=[self.modT, ng], writes=[a])
        sh = View(self.modT.ap[:, base:base + 16], self.modT.bufs)
        g = View(self.modT.ap[:, base + 32:base + 48], self.modT.bufs)
        return a, sh, g

    def resid_evac(self, g, cpg=4):
        S = self.S

        def ev(gi, si, ci, m, tt, ps, g=g):
            c = gi * cpg + ci
            xv = self.xT.v(c, tt)
            S.op("dve", lambda e: e.scalar_tensor_tensor(out=xv.ap, in0=ps.ap, scalar=g.ap[:, c:c + 1], in1=xv.ap, op0=ALU.mult, op1=ALU.add),
                 reads=[ps, g, xv], writes=[xv])
        return ev

    def ffn(self, layer):
        p, S = self, self.S
        a, sh, g = self.adaln(layer, 1)
        norm_fm(p, self.x_src, 16, 2, a, sh, self.hT.v, 1.0 / D)
        wg = self.inp(f"wg{layer}", [D, DFF], F32)
        wu = self.inp(f"wu{layer}", [D, DFF], F32)
        wd = self.inp(f"wd{layer}", [DFF, D], F32)
        if not hasattr(self, "aT"):
            self.aT = FM(p, "aT", 6, T, BF16)
        aT = self.aT
        parts = [6, 6, 6, 6, 5, 5, 5, 5]
        for q in range(8):
            j0 = sum(parts[:q])
            nj = parts[q]
            sizes = [2, 2, 2] if nj == 6 else [2, 2, 1]
            groups = []
            jj = j0
            gstart = []
            for sz in sizes:
                groups.append([(wg, jj * 128, sz * 128), (wu, jj * 128, sz * 128)])
                gstart.append(jj - j0)
                jj += sz
            sgs = {}

            def ev(gi, si, ci, m, tt, ps):
                jl = gstart[gi] + ci
                if si == 0:
                    sg = p.scr("sg", [128, 512], F32, 5)
                    sgs[(jl, tt)] = sg
                    S.op("act", lambda e: e.activation(out=sg.ap, in_=ps.ap, func=AF.Silu), reads=[ps], writes=[sg])
                else:
                    sg = sgs[(jl, tt)]
                    av = aT.v(jl, tt)
                    S.op("dve", lambda e: e.tensor_tensor(out=av.ap, in0=sg.ap, in1=ps.ap, op=ALU.mult), reads=[sg, ps], writes=[av])
            linear(p, D, groups, self.hT.v, 2, ev, interleave=True)
            wdq = wd[j0 * 128:(j0 + nj) * 128, :]
            linear(p, nj * 128, [[(wdq, n * 1024, 1024)] for n in range(2)], aT.v, 2, self.resid_evac(g, 8))

    def conv1(self, layer, widx):
        p, S = self, self.S
        a, sh, g = self.adaln(layer, 0)
        norm_fm(p, self.x_src, 16, 2, a, sh, self.hT.v, 1.0 / D)
        w = self.inp(f"cw_in{widx}", [D, 3 * D], F32)
        oB = self.out("convB", [D, T], BF16)
        oCU = self.out("convCU", [D, T], F32)
        groups = [[(w, n * 128, 128), (w, D + n * 128, 128), (w, 2 * D + n * 128, 128)] for n in range(16)]
        st = {}

        def ev(gi, si, ci, m, tt, ps):
            cs = slice(tt * 512, (tt + 1) * 512)
            if si == 0:
                if tt == 0:
                    st["B"] = p.scr("stB", [128, T], BF16, 2)
                b_ = st["B"]
                S.op("act", lambda e: e.activation(out=b_.ap[:, cs], in_=ps.ap, func=AF.Copy), reads=[ps], writes=[b_])
                if tt == 1:
                    p.store(oB[gi * 128:(gi + 1) * 128, :], b_)
            elif si == 1:
                c_ = p.scr("cC", [128, 512], F32, 3)
                st[("C", tt)] = c_
                S.op("act", lambda e: e.activation(out=c_.ap, in_=ps.ap, func=AF.Copy), reads=[ps], writes=[c_])
            else:
                if tt == 0:
                    st["CU"] = p.scr("stCU", [128, T], F32, 2)
                cu = st["CU"]
                c_ = st[("C", tt)]
                S.op("dve", lambda e: e.tensor_tensor(out=cu.ap[:, cs], in0=c_.ap, in1=ps.ap, op=ALU.mult), reads=[c_, ps], writes=[cu])
                if tt == 1:
                    p.store(oCU[gi * 128:(gi + 1) * 128, :], cu)
        linear(p, D, groups, self.hT.v, 2, ev)

    def conv2(self, layer, widx):
        p, S = self, self.S
        base = layer * 96
        g = View(self.modT.ap[:, base + 32:base + 48], self.modT.bufs)
        iB = self.inp("convB_in", [D, T], BF16)
        iCU = self.inp("convCU_in", [D, T], F32)
        ihalo = self.inp("halo", [D, 2], F32)
        icw = self.inp(f"cw{widx}", [128, 48], F32)
        wout = self.inp(f"cw_out{widx}", [D, D], F32)
        cw = S.sb("convw", [128, 48], F32)
        self.load(cw, icw)
        for c in range(16):
            cu = p.scr("cu_in", [128, T + 2], F32, 2)
            bc = p.scr("b_in", [128, T], BF16, 2)
            S.dma("sp", View(cu.ap[:, 0:2], cu.bufs), View(ihalo[c * 128:(c + 1) * 128, :], []),
                  extra=[(View(cu.ap[:, 2:], cu.bufs), View(iCU[c * 128:(c + 1) * 128, :], []))])
            self.load(bc, iB[c * 128:(c + 1) * 128, :])
            z = p.scr("convz", [128, T], F32, 2)
            S.op("dve", lambda e, z=z, cu=cu, c=c: e.tensor_scalar(out=z.ap, in0=cu.ap[:, 2:2 + T], scalar1=cw.ap[:, 32 + c:33 + c], scalar2=None, op0=ALU.mult),
                 reads=[cu, cw], writes=[z])
            S.op("dve", lambda e, z=z, cu=cu, c=c: e.scalar_tensor_tensor(out=z.ap, in0=cu.ap[:, 1:1 + T], scalar=cw.ap[:, 16 + c:17 + c], in1=z.ap, op0=ALU.mult, op1=ALU.add),
                 reads=[cu, cw, z], writes=[z])
            S.op("dve", lambda e, z=z, cu=cu, c=c: e.scalar_tensor_tensor(out=z.ap, in0=cu.ap[:, 0:T], scalar=cw.ap[:, c:c + 1], in1=z.ap, op0=ALU.mult, op1=ALU.add),
                 reads=[cu, cw, z], writes=[z])
            hv = self.hT.vc(c)
            S.op("pool", lambda e, z=z, bc=bc, hv=hv: e.tensor_tensor(out=hv.ap, in0=z.ap, in1=bc.ap, op=ALU.mult), reads=[z, bc], writes=[hv])
        linear(p, D, [[(wout, n * 512, 512)] for n in range(4)], self.hT.v, 2, self.resid_evac(g))

    def final(self):
        p, S = self, self.S
        fg = S.sb("fng", [128, 16], F32)
        self.load(fg, self.inp("fngT", [128, 16], F32))
        oo = self.out("outT", [D, T], F32)
        for tt in range(2):
            ps = p.bank(6, 8)
            for c in range(16):
                sq = p.scr("sq", [128, 512], BF16, 3)
                s_ = self.xT.v(c, tt)
                S.op("act", lambda e, o=sq, i=s_: e.activation(out=o.ap, in_=i.ap, func=AF.Square), reads=[s_], writes=[sq])
                S.op("pe", lambda e, ps=ps, sq=sq, c=c: e.matmul(ps.ap, lhsT=p.ones.ap, rhs=sq.ap, start=(c == 0), stop=(c == 15)),
                     reads=[sq, p.ones], writes=[ps])
            r = p.scr("rstd", [128, 512], F32, 2)
            S.op("act", lambda e, r=r, ps=ps: e.activation(out=r.ap, in_=ps.ap, func=AF.Sqrt, bias=EPS, scale=1.0 / D), reads=[ps], writes=[r])
            S.op("dve", lambda e, r=r: e.reciprocal(out=r.ap, in_=r.ap), reads=[r], writes=[r])
            for c in range(16):
                s_ = self.xT.v(c, tt)
                d_ = p.scr("ntmp", [128, 512], F32, 3)
                S.op("dve", lambda e, d_=d_, s_=s_, r=r, c=c: e.scalar_tensor_tensor(out=d_.ap, in0=s_.ap, scalar=fg.ap[:, c:c + 1], in1=r.ap, op0=ALU.mult, op1=ALU.mult),
                     reads=[s_, r, fg], writes=[d_])
                p.store(oo[c * 128:(c + 1) * 128, tt * 512:(tt + 1) * 512], d_)

    def evac_store(self, name, out_ap, dt, row_of, scale=None, eng="act"):
        p, S = self, self.S
        st = {}

        def ev(gi, si, ci, m, tt, ps):
            cs = slice(tt * 512, (tt + 1) * 512)
            if tt == 0:
                st["t"] = p.scr("st_" + name, [128, T], dt, 2)
            t_ = st["t"]
            if scale is None:
                S.op("act", lambda e: e.activation(out=t_.ap[:m, cs], in_=ps.ap[:m, :], func=AF.Copy), reads=[ps], writes=[t_])
            else:
                S.op("act", lambda e: e.activation(out=t_.ap[:m, cs], in_=ps.ap[:m, :], func=AF.Copy, scale=scale), reads=[ps], writes=[t_])
            if tt == 1:
                r0 = row_of(gi, si, ci)
                p.store(out_ap[r0:r0 + m, :], View(t_.ap[:m, :], t_.bufs))
        return ev

    def rope_tables(self):
        p, S = self, self.S
        posi = S.sb("posi", [128, T], I32)
        self.load(posi, self.inp("pos_rep", [128, T], I32))
        rc = S.sb("ropec", [128, 2], F32)
        self.load(rc, self.inp("ropec", [128, 2], F32))
        ang = S.sb("ang", [128, T], F32)
        S.op("dve", lambda e: e.tensor_copy(out=ang.ap, in_=posi.ap), reads=[posi], writes=[ang])
        S.op("dve", lambda e: e.tensor_scalar(out=ang.ap, in0=ang.ap, scalar1=rc.ap[:, 0:1], scalar2=None, op0=ALU.mult), reads=[ang, rc], writes=[ang])
        C1 = 6.28125
        C2 = 2 * math.pi - C1
        outs = []
        for name, shift in (("sin", 0.0), ("cos", math.pi / 2)):
            y = S.sb("rt_" + name, [128, T], F32)
            ni = p.scr("rt_ni", [128, T], I32, 1)
            nf = p.scr("rt_nf", [128, T], F32, 1)
            S.op("dve", lambda e, y=y, shift=shift: e.tensor_scalar(out=y.ap, in0=ang.ap, scalar1=shift, scalar2=None, op0=ALU.add), reads=[ang], writes=[y])
            S.op("dve", lambda e, y=y, nf=nf: e.tensor_scalar(out=nf.ap, in0=y.ap, scalar1=1.0 / (2 * math.pi), scalar2=None, op0=ALU.mult), reads=[y], writes=[nf])
            S.op("dve", lambda e, ni=ni, nf=nf: e.tensor_copy(out=ni.ap, in_=nf.ap), reads=[nf], writes=[ni])
            S.op("dve", lambda e, ni=ni, nf=nf: e.tensor_copy(out=nf.ap, in_=ni.ap), reads=[ni], writes=[nf])
            S.op("dve", lambda e, y=y, nf=nf: e.scalar_tensor_tensor(out=y.ap, in0=nf.ap, scalar=-C1, in1=y.ap, op0=ALU.mult, op1=ALU.add), reads=[nf, y], writes=[y])
            S.op("dve", lambda e, y=y, nf=nf: e.scalar_tensor_tensor(out=y.ap, in0=nf.ap, scalar=-C2, in1=y.ap, op0=ALU.mult, op1=ALU.add), reads=[nf, y], writes=[y])
            S.op("dve", lambda e, y=y: e.tensor_scalar(out=y.ap, in0=y.ap, scalar1=math.pi, scalar2=-math.pi, op0=ALU.min, op1=ALU.max), reads=[y], writes=[y])
            S.op("act", lambda e, y=y: e.activation(out=y.ap, in_=y.ap, func=AF.Sin), reads=[y], writes=[y])
            outs.append(y)
        sin, cos = outs
        S.op("dve", lambda e: e.tensor_scalar(out=sin.ap, in0=sin.ap, scalar1=rc.ap[:, 1:2], scalar2=-1.0, op0=ALU.mult, op1=ALU.mult), reads=[sin, rc], writes=[sin])
        self.cos2, self.sin2s = cos, sin

    def rope_evac(self, name, out_ap, row_of):
        p, S = self, self.S
        st = {}

        def ev(gi, si, ci, m, tt, ps):
            cs = slice(tt * 512, (tt + 1) * 512)
            if si == 0:
                t1 = p.scr("rp_t1", [128, 512], F32, 4)
                st[(ci, tt)] = t1
                S.op("dve", lambda e: e.tensor_tensor(out=t1.ap[:m, :], in0=ps.ap[:m, :], in1=self.cos2.ap[:m, cs], op=ALU.mult), reads=[ps, self.cos2], writes=[t1])
            else:
                t1 = st[(ci, tt)]
                t2 = p.scr("rp_t2", [128, 512], F32, 2)
                S.op("dve", lambda e: e.tensor_tensor(out=t2.ap[:m, :], in0=ps.ap[:m, :], in1=self.sin2s.ap[:m, cs], op=ALU.mult), reads=[ps, self.sin2s], writes=[t2])
                if tt == 0:
                    st["o"] = p.scr("st_" + name, [128, T], BF16, 2)
                o_ = st["o"]
                S.op("pool", lambda e: e.tensor_tensor(out=o_.ap[:m, cs], in0=t1.ap[:m, :], in1=t2.ap[:m, :], op=ALU.add), reads=[t1, t2], writes=[o_])
                if tt == 1:
                    r0 = row_of(gi, ci)
                    p.store(out_ap[r0:r0 + m, :], View(o_.ap[:m, :], o_.bufs))
        return ev

    def mla_pre(self):
        p, S = self, self.S
        layer = 1
        a, sh, g = self.adaln(layer, 0)
        norm_fm(p, self.x_src, 16, 2, a, sh, self.hT.v, 1.0 / D)
        self.rope_tables()
        w_dq = self.inp("w_dq", [D, 768], F32)
        w_uqn = self.inp("w_uqn", [768, 2048], F32)
        w_uqp = self.inp("w_uqp", [768, 1024], F32)
        w_uqps = self.inp("w_uqps", [768, 1024], F32)
        w_dkvc = self.inp("w_dkvc", [D, 512], F32)
        w_dkvp = self.inp("w_dkvp", [D, 64], F32)
        w_dkvps = self.inp("w_dkvps", [D, 64], F32)
        w_uk = self.inp("w_uk", [512, 2048], F32)
        w_uv = self.inp("w_uv", [512, 2048], F32)
        qng = S.sb("qng", [128, 6], F32)
        kvng = S.sb("kvng", [128, 4], F32)
        self.load(qng, self.inp("qngT", [128, 6], F32))
        self.load(kvng, self.inp("kvngT", [128, 4], F32))
        o_qn = self.out("qnT", [2048, T], BF16)
        o_qp = self.out("qpT", [1024, T], BF16)
        o_kn = self.out("knT", [2048, T], BF16)
        o_kp = self.out("kpT", [64, T], BF16)
        o_v = self.out("v_tm", [T, 2048], BF16)
        cqpre = FM(p, "cqpre", 6, T, F32)
        cq = FM(p, "cq", 6, T, BF16)

        def ev_pre(dstfm, cbase):
            def ev(gi, si, ci, m, tt, ps):
                d_ = dstfm.v(cbase(gi) + ci, tt)
                S.op("act", lambda e: e.activation(out=d_.ap, in_=ps.ap, func=AF.Copy), reads=[ps], writes=[d_])
            return ev
        linear(p, D, [[(w_dq, 0, 512)], [(w_dq, 512, 256)]], self.hT.v, 2, ev_pre(cqpre, lambda gi: gi * 4))
        norm_fm(p, cqpre.v, 6, 2, qng, None, cq.v, 1.0 / 768)
        linear(p, 768, [[(w_uqn, n * 1024, 1024)] for n in range(2)], cq.v, 2,
               self.evac_store("qn", o_qn, BF16, lambda gi, si, ci: gi * 1024 + ci * 128))
        linear(p, 768, [[(w_uqp, n * 512, 512), (w_uqps, n * 512, 512)] for n in range(2)], cq.v, 2,
               self.rope_evac("qp", o_qp, lambda gi, ci: gi * 512 + ci * 128), interleave=True)
        linear(p, D, [[(w_dkvc, 0, 512)]], self.hT.v, 2, ev_pre(cqpre, lambda gi: 0))
        norm_fm(p, cqpre.v, 4, 2, kvng, None, cq.v, 1.0 / 512)
        linear(p, D, [[(w_dkvp, 0, 64), (w_dkvps, 0, 64)]], self.hT.v, 2,
               self.rope_evac("kp", o_kp, lambda gi, ci: 0), interleave=True)
        linear(p, 512, [[(w_uk, 0, 2048)]], cq.v, 2,
               self.evac_store("kn", o_kn, BF16, lambda gi, si, ci: ci * 128))

        def ev_v(tb, c0, ps):
            t_ = p.scr("st_v", [128, 512], BF16, 3)
            S.op("act", lambda e: e.activation(out=t_.ap, in_=ps.ap, func=AF.Copy), reads=[ps], writes=[t_])
            p.store(o_v[tb * 128:(tb + 1) * 128, c0:c0 + 512], t_)
        linear_tm(p, 512, w_uv, 2048, lambda k, tb: View(cq.t[:, k, tb * 128:(tb + 1) * 128], [cq.b[k][tb // 4]]), 8, ev_v)

    def attn_out(self):
        p, S = self, self.S
        layer = 1
        base = layer * 96
        g = View(self.modT.ap[:, base + 32:base + 48], self.modT.bufs)
        ia = self.inp("attT", [D, T], BF16)
        wo = self.inp("w_o", [D, D], F32)
        for c in range(16):
            self.load(self.hT.vc(c), ia[c * 128:(c + 1) * 128, :])
        linear(p, D, [[(wo, n * 512, 512)] for n in range(4)], self.hT.v, 2, self.resid_evac(g))

    def mlstm_pre(self):
        p, S = self, self.S
        layer = 2
        a, sh, g = self.adaln(layer, 0)
        norm_fm(p, self.x_src, 16, 2, a, sh, self.hT.v, 1.0 / D)
        wq = self.inp("m_wq", [D, 1024], F32)
        wk = self.inp("m_wk", [D, 1024], F32)
        wv = self.inp("m_wv", [D, 2048], F32)
        wo = self.inp("m_wo", [D, 2048], F32)
        wgt = self.inp("m_wg", [D, 8], F32)
        bg = S.sb("m_bg", [8, 1], F32)
        self.load(bg, self.inp("m_bg", [8, 1], F32))
        o_q = self.out("mqT", [1024, T], BF16)
        o_k = self.out("mkT", [1024, T], BF16)
        o_ktm = self.out("mk_tm", [T, 1024], BF16)
        o_vtm = self.out("mv_tm", [T, 2048], BF16)
        o_gi = self.out("gi", [8, T], F32)
        o_gf = self.out("gf", [8, T], F32)
        o_so = self.out("sigoT", [D, T], BF16)
        linear(p, D, [[(wq, n * 512, 512)] for n in range(2)], self.hT.v, 2,
               self.evac_store("mq", o_q, BF16, lambda gi, si, ci: gi * 512 + ci * 128))
        linear(p, D, [[(wk, n * 512, 512)] for n in range(2)], self.hT.v, 2,
               self.evac_store("mk", o_k, BF16, lambda gi, si, ci: gi * 512 + ci * 128, scale=1.0 / 16))
        st = {}

        def ev_so(gi, si, ci, m, tt, ps):
            cs = slice(tt * 512, (tt + 1) * 512)
            if tt == 0:
                st["t"] = p.scr("st_so", [128, T], BF16, 2)
            t_ = st["t"]
            S.op("act", lambda e: e.activation(out=t_.ap[:, cs], in_=ps.ap, func=AF.Sigmoid), reads=[ps], writes=[t_])
            if tt == 1:
                r0 = gi * 512 + ci * 128
                p.store(o_so[r0:r0 + 128, :], t_)
        linear(p, D, [[(wo, n * 512, 512)] for n in range(4)], self.hT.v, 2, ev_so)
        gst = {}

        def ev_g(gi, si, ci, m, tt, ps):
            cs = slice(tt * 512, (tt + 1) * 512)
            if tt == 0:
                gst["i"] = S.sb("g_i", [8, T], F32)
                gst["f"] = S.sb("g_f", [8, T], F32)
            gi_, gf_ = gst["i"], gst["f"]
            S.op("act", lambda e: e.activation(out=gi_.ap[:, cs], in_=ps.ap[:8, :], func=AF.Identity, bias=bg.ap[:, 0:1], scale=1.0), reads=[ps, bg], writes=[gi_])
            S.op("act", lambda e: e.activation(out=gi_.ap[:, cs], in_=gi_.ap[:, cs], func=AF.Tanh, scale=1.0 / 15), reads=[gi_], writes=[gi_])
            S.op("dve", lambda e: e.tensor_scalar(out=gi_.ap[:, cs], in0=gi_.ap[:, cs], scalar1=15.0, scalar2=None, op0=ALU.mult), reads=[gi_], writes=[gi_])
            S.op("act", lambda e: e.activation(out=gf_.ap[:, cs], in_=gi_.ap[:, cs], func=AF.Exp, scale=-1.0), reads=[gi_], writes=[gf_])
            S.op("act", lambda e: e.activation(out=gf_.ap[:, cs], in_=gf_.ap[:, cs], func=AF.Ln, bias=1.0, scale=1.0), reads=[gf_], writes=[gf_])
            S.op("dve", lambda e: e.tensor_scalar(out=gf_.ap[:, cs], in0=gf_.ap[:, cs], scalar1=-1.0, scalar2=None, op0=ALU.mult), reads=[gf_], writes=[gf_])
            if tt == 1:
                p.store(o_gi, gi_)
                p.store(o_gf, gf_)
        linear(p, D, [[(wgt, 0, 8)]], self.hT.v, 2, ev_g)

        def lhs(k, tb):
            return View(self.hT.t[:, k, tb * 128:(tb + 1) * 128], [self.hT.b[k][tb // 4]])

        def ev_k(tb, c0, ps):
            t_ = p.scr("st_v", [128, 512], BF16, 3)
            S.op("act", lambda e: e.activation(out=t_.ap, in_=ps.ap, func=AF.Copy, scale=1.0 / 16), reads=[ps], writes=[t_])
            p.store(o_ktm[tb * 128:(tb + 1) * 128, c0:c0 + 512], t_)

        def ev_v(tb, c0, ps):
            t_ = p.scr("st_v", [128, 512], BF16, 3)
            S.op("act", lambda e: e.activation(out=t_.ap, in_=ps.ap, func=AF.Copy), reads=[ps], writes=[t_])
            p.store(o_vtm[tb * 128:(tb + 1) * 128, c0:c0 + 512], t_)
        linear_tm(p, D, wk, 1024, lhs, 8, ev_k)
        linear_tm(p, D, wv, 2048, lhs, 8, ev_v)

    def mlstm_post(self):
        p, S = self, self.S
        layer = 2
        base = layer * 96
        g = View(self.modT.ap[:, base + 32:base + 48], self.modT.bufs)
        ih = self.inp("mhT", [D, T], BF16)
        iso = self.inp("sigoT_in", [D, T], BF16)
        wout = self.inp("m_wout", [D, D], F32)
        hng = S.sb("hng", [128, 16], F32)
        self.load(hng, self.inp("hngT", [128, 16], F32))
        def hsrc(hd):
            def src(c, tt):
                t_ = p.scr("mh_s", [128, 512], BF16, 3)
                cc = hd * 4 + c
                self.load(t_, ih[cc * 128:(cc + 1) * 128, tt * 512:(tt + 1) * 512])
                return t_
            return src
        for hd in range(4):
            hv = View(hng.ap[:, hd * 4:(hd + 1) * 4], hng.bufs)
            norm_fm(p, hsrc(hd), 4, 2, hv, None, (lambda hd: (lambda c, tt: self.hT.v(hd * 4 + c, tt)))(hd), 1.0 / 512)
        for c in range(16):
            so = p.scr("so_in", [128, T], BF16, 2)
            self.load(so, iso[c * 128:(c + 1) * 128, :])
            hv = self.hT.vc(c)
            S.op("dve", lambda e, hv=hv, so=so: e.tensor_tensor(out=hv.ap, in0=hv.ap, in1=so.ap, op=ALU.mult), reads=[hv, so], writes=[hv])
        linear(p, D, [[(wout, n * 512, 512)] for n in range(4)], self.hT.v, 2, self.resid_evac(g))


def build_mods():
    p = Prog(wslots=2, slot_elems=16 * 512)
    S = p.S
    ic = p.inp("c_col", [128, 16], F32)
    iw = p.inp("mod_w_s", [4, D, 1536], F32)
    ib = p.inp("mod_b_s", [1, 4 * 1536], F32)
    oo = p.out("modT_s", [128, 48], F32)
    cc = S.sb("cc", [128, 16], F32)
    p.load(cc, ic)
    cb = S.sb("cb", [128, 16], BF16)
    S.op("act", lambda e: e.activation(out=cb.ap, in_=cc.ap, func=AF.Silu), reads=[cc], writes=[cb])
    mb = S.sb("mb", [1, 4 * 1536], F32)
    p.load(mb, ib)
    row = S.sb("row", [1, 4 * 1536], F32)
    one1 = S.sb("one1", [1, 1], F32)
    S.op("dve", lambda e: e.memset(one1.ap, 1.0), writes=[one1])
    res = S.sb("res", [128, 48], F32)
    pst = p.banks[7]
    for l in range(4):
        for g in range(3):
            slot = p.wslot()
            sv3 = slot.ap.rearrange("p (k n) -> p k n", n=512)
            S.dma("pool", View(sv3, slot.bufs), View(iw[l, :, g * 512:(g + 1) * 512].rearrange("(k p) n -> p k n", p=128), []))
            ps = p.bank()
            for k in range(16):
                S.op("pe", lambda e, ps=ps, k=k, sv3=sv3: e.matmul(ps.ap[0:1, :], lhsT=cb.ap[:, k:k + 1], rhs=sv3[:, k, :], start=(k == 0), stop=(k == 15)),
                     reads=[slot, cb], writes=[ps])
            c0 = l * 1536 + g * 512
            S.op("dve", lambda e, ps=ps, c0=c0: e.tensor_tensor(out=row.ap[:, c0:c0 + 512], in0=ps.ap[0:1, :], in1=mb.ap[:, c0:c0 + 512], op=ALU.add),
                 reads=[ps, mb], writes=[row])
    for j in range(48):
        S.op("pe", lambda e, j=j: e.matmul(pst.ap[:, j:j + 1], lhsT=row.ap[0:1, j * 128:(j + 1) * 128], rhs=one1.ap, start=True, stop=True),
             reads=[row, one1], writes=[pst])
    S.op("dve", lambda e: e.tensor_copy(out=res.ap, in_=pst.ap[:, 0:48]), reads=[pst], writes=[res])
    p.store(oo, res)
    return p


def build_attn():
    p = Prog(wslots=1, slot_elems=16)
    S = p.S
    scale = 192 ** -0.5
    iqn = p.inp("qn", [2, 128, SEQ], BF16)
    iqp = p.inp("qp", [2, 64, SEQ], BF16)
    ikn = p.inp("kn", [2, 128, SEQ], BF16)
    ikp = p.inp("kp", [64, SEQ], BF16)
    iv = p.inp("v", [SEQ, 256], BF16)
    imask = p.inp("cmask", [128, 4 * 512], BF16)
    oat = p.out("att", [256, SEQ], BF16)
    NT = SEQ // 512
    kn = [[S.sb(f"kn{h}_{t}", [128, 512], BF16) for t in range(NT)] for h in range(2)]
    kp = [S.sb(f"kp_{t}", [64, 512], BF16) for t in range(NT)]
    vv = [S.sb(f"v_{t}", [128, 4, 256], BF16) for t in range(NT)]
    mask = S.sb("cmask", [128, 4 * 512], BF16)
    p.load(mask, imask)
    for t in range(NT):
        cs = slice(t * 512, (t + 1) * 512)
        for h in range(2):
            p.load(kn[h][t], ikn[h, :, cs])
        p.load(kp[t], ikp[:, cs])
        p.load(vv[t], iv[cs, :].rearrange("(b p) d -> p b d", p=128))
    ones = p.ones
    bias = []
    for h in range(2):
        mx = {}
        for nm in ("k", "q"):
            m_ = S.sb(f"mx{nm}{h}", [128, 1], F32)
            S.op("dve", lambda e, m_=m_: e.memset(m_.ap, 0.0), writes=[m_])
            mx[nm] = m_
        for t in range(NT):
            cs = slice(t * 512, (t + 1) * 512)
            for nm in ("k", "q"):
                if nm == "k":
                    a_, b_ = kn[h][t], kp[t]
                else:
                    a_ = p.scr("qn_n", [128, 512], BF16, 2)
                    b_ = p.scr("qp_n", [64, 512], BF16, 2)
                    p.load(a_, iqn[h, :, cs])
                    p.load(b_, iqp[h, :, cs])
                s1 = p.scr("nsq1", [128, 512], BF16, 2)
                s2 = p.scr("nsq2", [64, 512], BF16, 2)
                S.op("act", lambda e, s1=s1, a_=a_: e.activation(out=s1.ap, in_=a_.ap, func=AF.Square), reads=[a_], writes=[s1])
                S.op("act", lambda e, s2=s2, b_=b_: e.activation(out=s2.ap, in_=b_.ap, func=AF.Square), reads=[b_], writes=[s2])
                ps = p.bank(6, 8)
                S.op("pe", lambda e, ps=ps, s1=s1: e.matmul(ps.ap, lhsT=ones.ap, rhs=s1.ap, start=True, stop=False), reads=[s1, ones], writes=[ps])
                S.op("pe", lambda e, ps=ps, s2=s2: e.matmul(ps.ap, lhsT=ones.ap[:64, :], rhs=s2.ap, start=False, stop=True), reads=[s2, ones], writes=[ps])
                tm = p.scr("nmx", [128, 1], F32, 2)
                S.op("dve", lambda e, tm=tm, ps=ps: e.reduce_max(out=tm.ap, in_=ps.ap, axis=mybir.AxisListType.X), reads=[ps], writes=[tm])
                m_ = mx[nm]
                S.op("dve", lambda e, tm=tm, m_=m_: e.tensor_tensor(out=m_.ap, in0=m_.ap, in1=tm.ap, op=ALU.max), reads=[tm, m_], writes=[m_])
        bb = S.sb(f"bias{h}", [128, 1], F32)
        S.op("dve", lambda e, bb=bb, mx=mx: e.tensor_tensor(out=bb.ap, in0=mx["k"].ap, in1=mx["q"].ap, op=ALU.mult), reads=[mx["k"], mx["q"]], writes=[bb])
        S.op("act", lambda e, bb=bb: e.activation(out=bb.ap, in_=bb.ap, func=AF.Sqrt), reads=[bb], writes=[bb])
        S.op("dve", lambda e, bb=bb: e.tensor_scalar(out=bb.ap, in0=bb.ap, scalar1=-scale, scalar2=None, op0=ALU.mult), reads=[bb], writes=[bb])
        bias.append(bb)
    def qtile(h, qi):
        cs = slice(qi * 512, (qi + 1) * 512)
        qn = p.scr("qn_m", [128, 512], BF16, 2)
        qp = p.scr("qp_m", [64, 512], BF16, 2)
        p.load(qn, iqn[h, :, cs])
        p.load(qp, iqp[h, :, cs])
        po = p.bank(4, 6)
        psum_ = p.bank(6, 8)
        nkb = 4 * (qi + 1)
        bh = bias[h]
        def emit_s(kb):
            t, r = kb // 4, kb % 4
            ks = slice(r * 128, (r + 1) * 128)
            ps = p.bank(0, 4)
            knt, kpt = kn[h][t], kp[t]
            S.op("pe", lambda e, ps=ps, knt=knt, ks=ks: e.matmul(ps.ap, lhsT=knt.ap[:, ks], rhs=qn.ap, start=True, stop=False),
                 reads=[knt, qn], writes=[ps])
            S.op("pe", lambda e, ps=ps, kpt=kpt, ks=ks: e.matmul(ps.ap, lhsT=kpt.ap[:, ks], rhs=qp.ap, start=False, stop=True),
                 reads=[kpt, qp], writes=[ps])
            pt = p.scr("pT", [128, 512], BF16, 5)
            S.op("act", lambda e, pt=pt, ps=ps: e.activation(out=pt.ap, in_=ps.ap, func=AF.Exp, bias=bh.ap[:, 0:1], scale=scale),
                 reads=[ps, bh], writes=[pt])
            if t == qi:
                S.op("dve", lambda e, pt=pt, r=r: e.tensor_tensor(out=pt.ap, in0=pt.ap, in1=mask.ap[:, r * 512:(r + 1) * 512], op=ALU.mult),
                     reads=[pt, mask], writes=[pt])
            return pt

        LOOK = 2
        pts = {}
        for kb in range(min(LOOK, nkb)):
            pts[kb] = emit_s(kb)
        for kb in range(nkb):
            if kb + LOOK < nkb:
                pts[kb + LOOK] = emit_s(kb + LOOK)
            pt = pts.pop(kb)
            t, r = kb // 4, kb % 4
            vt_ = vv[t]
            S.op("pe", lambda e, pt=pt, vt_=vt_, r=r, kb=kb: e.matmul(po.ap, lhsT=vt_.ap[:, r, h * 128:(h + 1) * 128], rhs=pt.ap, start=(kb == 0), stop=(kb == nkb - 1)),
                 reads=[vt_, pt], writes=[po])
            S.op("pe", lambda e, pt=pt, kb=kb: e.matmul(psum_.ap, lhsT=ones.ap, rhs=pt.ap, start=(kb == 0), stop=(kb == nkb - 1)),
                 reads=[ones, pt], writes=[psum_])
        rs = p.scr("rs", [128, 512], F32, 2)
        S.op("dve", lambda e: e.reciprocal(out=rs.ap, in_=psum_.ap), reads=[psum_], writes=[rs])
        ot = p.scr("ot", [128, 512], BF16, 2)
        S.op("dve", lambda e: e.tensor_tensor(out=ot.ap, in0=po.ap, in1=rs.ap, op=ALU.mult), reads=[po, rs], writes=[ot])
        p.store(oat[h * 128:(h + 1) * 128, cs], ot)

    for h in range(2):
        for qi in range(NT):
            qtile(h, qi)
    return p


def build_mlstm():
    p = Prog(wslots=1, slot_elems=16)
    S = p.S
    NCH = SEQ // 128
    iq = p.inp("qT", [256, SEQ], BF16)
    ik = p.inp("kT", [256, SEQ], BF16)
    iktm = p.inp("k_tm", [SEQ, 256], BF16)
    ivtm = p.inp("v_tm", [SEQ, 256], BF16)
    ia = p.inp("lf", [128, NCH], F32)
    ii = p.inp("ig", [128, NCH], F32)
    itri = p.inp("tri", [128, 128], F32)
    oh = p.out("h_tm", [SEQ, 256], BF16)
    qT = [S.sb(f"qT{d}", [128, SEQ], BF16) for d in range(2)]
    kT = [S.sb(f"kT{d}", [128, SEQ], BF16) for d in range(2)]
    G = 8
    ktm = [S.sb(f"ktm{g}", [128, G, 256], BF16) for g in range(NCH // G)]
    vtm = [S.sb(f"vtm{g}", [128, G, 257], BF16) for g in range(NCH // G)]
    for d in range(2):
        for hf in range(4):
            cs = slice(hf * 2048, (hf + 1) * 2048)
            S.dma("sp", View(qT[d].ap[:, cs], qT[d].bufs), View(iq[d * 128:(d + 1) * 128, cs], []))
            S.dma("sp", View(kT[d].ap[:, cs], kT[d].bufs), View(ik[d * 128:(d + 1) * 128, cs], []))
    for g in range(NCH // G):
        rs_ = slice(g * G * 128, (g + 1) * G * 128)
        p.load(ktm[g], iktm[rs_, :].rearrange("(b p) d -> p b d", p=128))
        S.op("pool", lambda e, g=g: e.memset(vtm[g].ap[:, :, 256:257], 1.0), writes=[vtm[g]])
        S.dma("sp", View(vtm[g].ap[:, :, 0:256], vtm[g].bufs), View(ivtm[rs_, :].rearrange("(b p) d -> p b d", p=128), []))
    a = S.sb("lf", [128, NCH], F32)
    ig = S.sb("ig", [128, NCH], F32)
    tri = S.sb("tri", [128, 128], F32)
    tri_b = S.sb("tri_b", [128, 128], BF16)
    onesf = S.sb("onesf", [128, 128], F32)
    p.load(a, ia)
    p.load(ig, ii)
    p.load(tri, itri)
    S.op("dve", lambda e: e.memset(onesf.ap, 1.0), writes=[onesf])
    F_ = S.sb("F", [128, NCH], F32)
    FL = S.sb("FL", [128, NCH], F32)
    ps = p.bank(6, 8)
    S.op("pe", lambda e: e.matmul(ps.ap[:, :NCH], lhsT=tri.ap, rhs=a.ap, start=True, stop=True), reads=[tri, a], writes=[ps])
    S.op("dve", lambda e: e.tensor_copy(out=F_.ap, in_=ps.ap[:, :NCH]), reads=[ps], writes=[F_])
    ps2 = p.bank(6, 8)
    S.op("pe", lambda e: e.matmul(ps2.ap[:, :NCH], lhsT=onesf.ap, rhs=a.ap, start=True, stop=True), reads=[onesf, a], writes=[ps2])
    S.op("dve", lambda e: e.tensor_copy(out=FL.ap, in_=ps2.ap[:, :NCH]), reads=[ps2], writes=[FL])
    imF = S.sb("imF", [128, NCH], F32)
    S.op("dve", lambda e: e.tensor_tensor(out=imF.ap, in0=ig.ap, in1=F_.ap, op=ALU.subtract), reads=[ig, F_], writes=[imF])
    w_ = S.sb("w_s", [128, NCH], F32)
    S.op("dve", lambda e: e.tensor_tensor(out=w_.ap, in0=imF.ap, in1=FL.ap, op=ALU.add), reads=[imF, FL], writes=[w_])
    S.op("act", lambda e: e.activation(out=w_.ap, in_=w_.ap, func=AF.Exp), reads=[w_], writes=[w_])
    dec = S.sb("decay", [128, NCH], F32)
    S.op("act", lambda e: e.activation(out=dec.ap, in_=FL.ap, func=AF.Exp), reads=[FL], writes=[dec])
    C = [S.sb(f"C{d}", [128, 257], F32) for d in range(2)]
    Cb = [[S.sb(f"Cb{d}_{par}", [128, 257], BF16) for d in range(2)] for par in range(2)]
    for d in range(2):
        S.op("dve", lambda e, d=d: e.memset(C[d].ap, 0.0), writes=[C[d]])
        for par in range(2):
            S.op("pool", lambda e, d=d, par=par: e.memset(Cb[par][d].ap, 0.0), writes=[Cb[par][d]])
    st = {}

    def prep(c):
        cs = slice(c * 128, (c + 1) * 128)
        g, gl = c // G, c % G
        ta = p.scr("ta", [128, 128], F32, 3)
        S.op("act", lambda e: e.activation(out=ta.ap, in_=tri.ap, func=AF.Identity, bias=0.0, scale=a.ap[:, c:c + 1]), reads=[tri, a], writes=[ta])
        pf = p.bank(6, 8)
        S.op("pe", lambda e: e.matmul(pf.ap[:, :128], lhsT=onesf.ap, rhs=ta.ap, start=True, stop=True), reads=[onesf, ta], writes=[pf])
        z = p.scr("z", [128, 128], F32, 3)
        S.op("dve", lambda e: e.tensor_scalar(out=z.ap, in0=pf.ap[:, :128], scalar1=imF.ap[:, c:c + 1], scalar2=20.0, op0=ALU.add, op1=ALU.min),
             reads=[pf, imF], writes=[z])
        S.op("act", lambda e: e.activation(out=z.ap, in_=z.ap, func=AF.Exp), reads=[z], writes=[z])
        dm = p.scr("dm", [128, 128], F32, 3)
        S.op("dve", lambda e: e.tensor_tensor(out=dm.ap, in0=z.ap, in1=tri.ap, op=ALU.mult), reads=[z, tri], writes=[dm])
        ef = p.scr("ef", [128, 128], F32, 3)
        S.op("act", lambda e: e.activation(out=ef.ap, in_=pf.ap[:, :128], func=AF.Exp), reads=[pf], writes=[ef])
        qt = p.scr("qtil", [128, 2, 128], BF16, 3)
        for d in range(2):
            S.op("dve" if d == 0 else "pool", lambda e, d=d: e.tensor_tensor(out=qt.ap[:, d, :], in0=qT[d].ap[:, cs], in1=ef.ap, op=ALU.mult), reads=[qT[d], ef], writes=[qt])
        kw = p.scr("kw", [128, 256], BF16, 3)
        S.op("act", lambda e: e.activation(out=kw.ap, in_=ktm[g].ap[:, gl, :], func=AF.Identity, bias=0.0, scale=w_.ap[:, c:c + 1]),
             reads=[ktm[g], w_], writes=[kw])
        pS = p.bank(0, 2)
        for d in range(2):
            S.op("pe", lambda e, d=d: e.matmul(pS.ap[:, :128], lhsT=kT[d].ap[:, cs], rhs=qT[d].ap[:, cs], start=(d == 0), stop=(d == 1)),
                 reads=[kT[d], qT[d]], writes=[pS])
        st[c] = (dm, qt, kw, pS)

    def update(c):
        g, gl = c // G, c % G
        dm, qt, kw, pS = st[c]
        par = c % 2
        for d in range(2):
            pc = p.bank(4, 6)
            S.op("pe", lambda e, pc=pc, d=d: e.matmul(pc.ap[:, :257], lhsT=kw.ap[:, d * 128:(d + 1) * 128], rhs=vtm[g].ap[:, gl, :], start=True, stop=True),
                 reads=[kw, vtm[g]], writes=[pc])
            S.op("dve", lambda e, pc=pc, d=d: e.scalar_tensor_tensor(out=C[d].ap, in0=C[d].ap, scalar=dec.ap[:, c:c + 1], in1=pc.ap[:, :257], op0=ALU.mult, op1=ALU.add),
                 reads=[C[d], dec, pc], writes=[C[d]])
            S.op("act", lambda e, d=d: e.activation(out=Cb[par][d].ap, in_=C[d].ap, func=AF.Copy), reads=[C[d]], writes=[Cb[par][d]])

    def output(c):
        g, gl = c // G, c % G
        dm, qt, kw, pS = st.pop(c)
        prev = Cb[(c - 1) % 2]
        pt = p.scr("pTm", [128, 128], BF16, 3)
        S.op("dve", lambda e: e.tensor_tensor(out=pt.ap, in0=pS.ap[:, :128], in1=dm.ap, op=ALU.mult), reads=[pS, dm], writes=[pt])
        pn = p.bank(2, 4)
        S.op("pe", lambda e: e.matmul(pn.ap[:, :257], lhsT=pt.ap, rhs=vtm[g].ap[:, gl, :], start=True, stop=False), reads=[pt, vtm[g]], writes=[pn])
        for d in range(2):
            S.op("pe", lambda e, d=d: e.matmul(pn.ap[:, :257], lhsT=qt.ap[:, d, :], rhs=prev[d].ap, start=False, stop=(d == 1)), reads=[qt, prev[d]], writes=[pn])
        den = p.scr("den", [128, 1], F32, 3)
        S.op("act", lambda e: e.activation(out=den.ap, in_=pn.ap[:, 256:257], func=AF.Abs), reads=[pn], writes=[den])
        S.op("dve", lambda e: e.tensor_scalar(out=den.ap, in0=den.ap, scalar1=1.0, scalar2=None, op0=ALU.max), reads=[den], writes=[den])
        S.op("dve", lambda e: e.reciprocal(out=den.ap, in_=den.ap), reads=[den], writes=[den])
        if gl == 0:
            st["hout"] = p.scr("hout", [128, G, 256], BF16, 2)
        hout = st["hout"]
        S.op("act", lambda e: e.activation(out=hout.ap[:, gl, :], in_=pn.ap[:, :256], func=AF.Identity, bias=0.0, scale=den.ap[:, 0:1]),
             reads=[pn, den], writes=[hout])
        if gl == G - 1:
            p.store(oh[g * G * 128:(g + 1) * G * 128, :].rearrange("(b p) d -> p b d", p=128), hout)

    prep(0)
    for c in range(NCH):
        if c < NCH - 1:
            update(c)
        output(c)
        if c + 1 < NCH:
            prep(c + 1)
    return p


_cache = {}


def _prog(name, fn):
    if name not in _cache:
        p = fn()
        p.finish()
        _cache[name] = p
    return _cache[name]


def _run(p, in_maps):
    maps = []
    for m in in_maps:
        maps.append({k: np.ascontiguousarray(m[k]) for k in p.in_names})
    res = run_bass_kernel_spmd(p.nc, maps, core_ids=list(range(NCORES)))
    return res.results


def _pp(v, nch):
    return np.ascontiguousarray(np.asarray(v, np.float32).reshape(nch, 128).T)


def _b_conv1(layer, widx, first):
    def f():
        p = TL()
        p.load_x("xT")
        p.conv1(layer, widx)
        return p
    return f


def _b_L1():
    p = TL()
    p.load_x("xT")
    p.conv2(0, 0)
    p.ffn(0)
    p.store_x("x1T")
    return p


def _b_L1b():
    p = TL(resident=False, wslots=2)
    p.load_x("xT")
    p.mla_pre()
    return p


def _b_L2():
    p = TL()
    p.load_x("xT")
    p.attn_out()
    p.ffn(1)
    p.store_x("x2T")
    return p


def _b_L2b():
    p = TL(resident=False, wslots=2)
    p.load_x("xT")
    p.mlstm_pre()
    return p


def _b_L3():
    p = TL()
    p.load_x("xT")
    p.mlstm_post()
    p.ffn(2)
    p.store_x("x3T")
    p.conv1(3, 1)
    return p


def _b_L4():
    p = TL()
    p.load_x("xT")
    p.conv2(3, 1)
    p.ffn(3)
    p.final()
    return p


def _tsplit(a):
    return [np.ascontiguousarray(a[:, r * T:(r + 1) * T]) for r in range(NCORES)]


def kernel(x, c, positions, mod_w, mod_b, norm1_g, norm2_g, ffn_w_gate, ffn_w_up, ffn_w_down,
           conv_w_in, conv_w, conv_w_out,
           mla_w_dq, mla_q_norm_g, mla_w_uq, mla_w_dkv, mla_kv_norm_g, mla_w_ukv, mla_w_o,
           mlstm_w_in, mlstm_b_gates, mlstm_head_norm_g, mlstm_w_out, final_norm_g, _debug=None):
    f32 = np.float32
    R = range(NCORES)
    dbg = _debug if _debug is not None else {}
    pm = _prog("mods", build_mods)
    c_col = _pp(np.asarray(c, f32).reshape(-1), 16)
    maps = []
    for r in R:
        cs = slice(r * 1536, (r + 1) * 1536)
        maps.append({"c_col": c_col, "mod_w_s": np.asarray(mod_w)[:, :, cs],
                     "mod_b_s": np.asarray(mod_b)[:, cs].reshape(1, -1)})
    res = _run(pm, maps)
    modT = np.zeros((128, 4, 96), f32)
    for r in R:
        modT[:, :, r * 12:(r + 1) * 12] = res[r]["modT_s"].reshape(128, 4, 12)
    modT = modT.reshape(128, 384)
    dbg["modT"] = modT
    if dbg.get("stop") == "mods":
        return None
    ng1T = np.concatenate([_pp(norm1_g[l], 16) for l in range(4)], axis=1)
    ng2T = np.concatenate([_pp(norm2_g[l], 16) for l in range(4)], axis=1)
    common = {"modT": modT, "ng1T": ng1T, "ng2T": ng2T}

    def ffnw(l):
        return {f"wg{l}": ffn_w_gate[l], f"wu{l}": ffn_w_up[l], f"wd{l}": ffn_w_down[l]}

    def convw(j):
        cwp = np.concatenate([_pp(conv_w[j][t], 16) for t in range(3)], axis=1)
        return {f"cw{j}": cwp, f"cw_out{j}": conv_w_out[j]}

    def halos(cu_list):
        hs = [np.zeros((D, 2), f32)]
        for r in range(1, NCORES):
            hs.append(np.ascontiguousarray(cu_list[r - 1][:, T - 2:T]))
        return hs

    xT = _tsplit(np.ascontiguousarray(np.asarray(x, f32)[0].T))
    p0 = _prog("L0", _b_conv1(0, 0, True))
    res = _run(p0, [dict(common, xT=xT[r], cw_in0=conv_w_in[0]) for r in R])
    cB = [res[r]["convB"] for r in R]
    cCU = [res[r]["convCU"] for r in R]
    dbg["convB0"] = cB
    dbg["convCU0"] = cCU
    if dbg.get("stop") == "L0":
        return None
    hl = halos(cCU)
    p1 = _prog("L1", _b_L1)
    w_uq = np.asarray(mla_w_uq[0]).reshape(768, 16, 192)
    w_uqn = np.ascontiguousarray(w_uq[:, :, :128].reshape(768, 2048))
    pe = w_uq[:, :, 128:]
    w_uqp = np.ascontiguousarray(pe.reshape(768, 1024))
    w_uqps = np.ascontiguousarray(np.concatenate([pe[:, :, 32:], pe[:, :, :32]], axis=2).reshape(768, 1024))
    w_dkv = np.asarray(mla_w_dkv[0])
    w_dkvc = np.ascontiguousarray(w_dkv[:, :512])
    w_dkvp = np.ascontiguousarray(w_dkv[:, 512:])
    w_dkvps = np.ascontiguousarray(np.concatenate([w_dkv[:, 544:], w_dkv[:, 512:544]], axis=1))
    w_ukv = np.asarray(mla_w_ukv[0]).reshape(512, 16, 256)
    w_uk = np.ascontiguousarray(w_ukv[:, :, :128].reshape(512, 2048))
    w_uv = np.ascontiguousarray(w_ukv[:, :, 128:].reshape(512, 2048))
    pidx = np.arange(128)
    invf = (10000.0 ** (-2.0 * (pidx % 32).astype(np.float64) / 64)).astype(f32)
    sgn = np.where((pidx % 64) < 32, 1.0, -1.0).astype(f32)
    ropec = np.stack([invf, sgn], axis=1).astype(f32)
    pos = np.asarray(positions).reshape(-1).astype(np.int32)
    mla_in = {"w_dq": mla_w_dq[0], "w_uqn": w_uqn, "w_uqp": w_uqp, "w_uqps": w_uqps, "w_dkvc": w_dkvc, "w_dkvp": w_dkvp,
              "w_dkvps": w_dkvps, "w_uk": w_uk, "w_uv": w_uv, "qngT": _pp(mla_q_norm_g[0], 6), "kvngT": _pp(mla_kv_norm_g[0], 4),
              "ropec": ropec}
    maps = []
    for r in R:
        m = dict(common, xT=xT[r], convB_in=cB[r], convCU_in=cCU[r], halo=hl[r])
        m.update(convw(0)); m.update(ffnw(0))
        maps.append(m)
    res = _run(p1, maps)
    x1T = [res[r]["x1T"] for r in R]
    dbg["x1T"] = x1T
    if dbg.get("stop") == "L1":
        return None
    p1b = _prog("L1b", _b_L1b)
    maps = []
    for r in R:
        m = dict(common, xT=x1T[r])
        m.update(mla_in)
        m["pos_rep"] = np.ascontiguousarray(np.broadcast_to(pos[r * T:(r + 1) * T][None, :], (128, T)))
        maps.append(m)
    res = _run(p1b, maps)
    qn = np.concatenate([res[r]["qnT"] for r in R], axis=1)
    qp = np.concatenate([res[r]["qpT"] for r in R], axis=1)
    kn = np.concatenate([res[r]["knT"] for r in R], axis=1)
    kp = np.concatenate([res[r]["kpT"] for r in R], axis=1)
    vt = np.concatenate([res[r]["v_tm"] for r in R], axis=0)
    dbg.update(qn=qn, qp=qp, kn=kn, kp=kp, vt=vt)
    if dbg.get("stop") == "L1b":
        return None
    pa = _prog("attn", build_attn)
    jj = np.arange(512)[None, :]
    pp_ = np.arange(128)[:, None]
    cmask = np.concatenate([(jj >= (128 * r_ + pp_)).astype(f32) for r_ in range(4)], axis=1).astype(NPBF)
    maps = []
    for r in R:
        maps.append({"qn": qn[r * 256:(r + 1) * 256].reshape(2, 128, SEQ), "qp": qp[r * 128:(r + 1) * 128].reshape(2, 64, SEQ),
                     "kn": kn[r * 256:(r + 1) * 256].reshape(2, 128, SEQ), "kp": kp, "v": vt[:, r * 256:(r + 1) * 256], "cmask": cmask})
    res = _run(pa, maps)
    att = np.concatenate([res[r]["att"] for r in R], axis=0)
    dbg["att"] = att
    if dbg.get("stop") == "attn":
        return None
    attT = _tsplit(att)
    p2 = _prog("L2", _b_L2)
    w_in = np.asarray(mlstm_w_in[0])
    ml_in = {"w_o": mla_w_o[0], "m_wq": np.ascontiguousarray(w_in[:, :1024]), "m_wk": np.ascontiguousarray(w_in[:, 1024:2048]),
             "m_wv": np.ascontiguousarray(w_in[:, 2048:4096]), "m_wo": np.ascontiguousarray(w_in[:, 4096:6144]),
             "m_wg": np.ascontiguousarray(w_in[:, 6144:6152]), "m_bg": np.asarray(mlstm_b_gates[0], f32).reshape(8, 1)}
    maps = []
    for r in R:
        m = dict(common, xT=x1T[r], attT=attT[r], w_o=ml_in["w_o"])
        m.update(ffnw(1))
        maps.append(m)
    res = _run(p2, maps)
    x2T = [res[r]["x2T"] for r in R]
    dbg["x2T"] = x2T
    if dbg.get("stop") == "L2":
        return None
    p2b = _prog("L2b", _b_L2b)
    res = _run(p2b, [dict(common, xT=x2T[r], **ml_in) for r in R])
    sigo = [res[r]["sigoT"] for r in R]
    mq = np.concatenate([res[r]["mqT"] for r in R], axis=1)
    mk = np.concatenate([res[r]["mkT"] for r in R], axis=1)
    mktm = np.concatenate([res[r]["mk_tm"] for r in R], axis=0)
    mvtm = np.concatenate([res[r]["mv_tm"] for r in R], axis=0)
    gi = np.concatenate([res[r]["gi"] for r in R], axis=1)
    gf = np.concatenate([res[r]["gf"] for r in R], axis=1)
    dbg.update(mq=mq, mk=mk, mktm=mktm, mvtm=mvtm, gi=gi, gf=gf, sigo=sigo)
    if dbg.get("stop") == "L2b":
        return None
    pl = _prog("mlstm", build_mlstm)
    tri = np.triu(np.ones((128, 128), f32))
    maps = []
    for r in R:
        hd, hf = r // 2, r % 2
        maps.append({"qT": mq[hd * 256:(hd + 1) * 256], "kT": mk[hd * 256:(hd + 1) * 256],
                     "k_tm": mktm[:, hd * 256:(hd + 1) * 256], "v_tm": mvtm[:, hd * 512 + hf * 256: hd * 512 + (hf + 1) * 256],
                     "lf": np.ascontiguousarray(gf[4 + hd].reshape(SEQ // 128, 128).T), "ig": np.ascontiguousarray(gi[hd].reshape(SEQ // 128, 128).T),
                     "tri": tri})
    res = _run(pl, maps)
    mh = np.concatenate([res[r]["h_tm"] for r in R], axis=1)
    dbg["mh"] = mh
    if dbg.get("stop") == "mlstm":
        return None
    mhT = _tsplit(np.ascontiguousarray(mh.T))
    p3 = _prog("L3", _b_L3)
    maps = []
    for r in R:
        m = dict(common, xT=x2T[r], mhT=mhT[r], sigoT_in=sigo[r], m_wout=mlstm_w_out[0], hngT=_pp(mlstm_head_norm_g[0], 16), cw_in1=conv_w_in[1])
        m.update(ffnw(2))
        maps.append(m)
    res = _run(p3, maps)
    x3T = [res[r]["x3T"] for r in R]
    cB = [res[r]["convB"] for r in R]
    cCU = [res[r]["convCU"] for r in R]
    dbg["x3T"] = x3T
    if dbg.get("stop") == "L3":
        return None
    hl = halos(cCU)
    p4 = _prog("L4", _b_L4)
    maps = []
    for r in R:
        m = dict(common, xT=x3T[r], convB_in=cB[r], convCU_in=cCU[r], halo=hl[r], fngT=_pp(final_norm_g, 16))
        m.update(convw(1)); m.update(ffnw(3))
        maps.append(m)
    res = _run(p4, maps)
    outT = np.concatenate([res[r]["outT"] for r in R], axis=1)
    return np.ascontiguousarray(outT.T)[None].astype(np.float32)
```

```python
import math
import os
from contextlib import ExitStack
import numpy as np
import ml_dtypes
import concourse.bass as bass
import concourse.mybir as mybir
from concourse.bass_utils import run_bass_kernel_spmd

F32 = mybir.dt.float32
BF16 = mybir.dt.bfloat16
I32 = mybir.dt.int32
ALU = mybir.AluOpType
AF = mybir.ActivationFunctionType
NPBF = ml_dtypes.bfloat16

NCORES = 8
D = 2048
SEQ = 8192
T = SEQ // NCORES
DFF = 5632
EPS = 1e-6
COMPUTE = ("pe", "act", "dve", "pool")
ALLENG = ("pe", "act", "dve", "pool", "sp")


class Buf:
    __slots__ = ("name", "writer", "readers", "dma_sem", "dma_cnt", "dma_last", "dma_idx")

    def __init__(self, name):
        self.name = name
        self.writer = None
        self.readers = []
        self.dma_sem = None
        self.dma_cnt = 0
        self.dma_last = None


class View:
    __slots__ = ("ap", "bufs")

    def __init__(self, ap, bufs):
        self.ap = ap
        self.bufs = list(bufs)

    def __getitem__(self, idx):
        return View(self.ap[idx], self.bufs)


class Op:
    __slots__ = ("eng", "fn", "waits", "signal", "idx", "is_dma", "dsem", "dval")

    def __init__(self, eng, fn):
        self.eng = eng
        self.fn = fn
        self.waits = []
        self.signal = False
        self.idx = None
        self.is_dma = False
        self.dsem = None
        self.dval = 0


class Mega:
    NDS = 88

    def __init__(self):
        self.nc = bass.Bass("TRN2", target_bir_lowering=False)
        nc = self.nc
        self.es = ExitStack()
        self.eng_sem = {e: self.es.enter_context(nc.semaphore("s_" + e)) for e in COMPUTE}
        self.eng_cnt = {e: 0 for e in COMPUTE}
        self.dsem = [self.es.enter_context(nc.semaphore(f"dq{i}")) for i in range(self.NDS)]
        self.dcnt = [0] * self.NDS
        self.cc_sems = [self.es.enter_context(nc.semaphore(f"cc_sem{i}")) for i in range(7)]
        self.cc_cnt = 0
        self.ph_sem = self.es.enter_context(nc.semaphore("ph_sem"))
        self.ph_cnt = 0
        self.phase = 0
        self.ext_in = {}
        self.ext_out = {}
        self.internal = {}
        nonce = os.urandom(4).hex()
        self.bar_src = nc.dram_tensor("bar_src_" + nonce, [1, 16], F32).ap()
        self.bar_dst = nc.dram_tensor("bar_dst_" + nonce, [1, 16], F32).ap()

    def inp(self, name, shape, dt):
        if name not in self.ext_in:
            self.ext_in[name] = self.nc.dram_tensor(name, list(shape), dt, kind="ExternalInput").ap()
        return self.ext_in[name]

    def out(self, name, shape, dt):
        if name not in self.ext_out:
            self.ext_out[name] = self.nc.dram_tensor(name, list(shape), dt, kind="ExternalOutput").ap()
        return self.ext_out[name]

    def itn(self, name, shape=None, dt=None):
        if name not in self.internal:
            self.internal[name] = self.nc.dram_tensor("i_" + name, list(shape), dt).ap()
        return self.internal[name]

    def close(self):
        self.es.close()


class Sched:
    def __init__(self, mega, same_engine_sync=True):
        self.mega = mega
        self.nc = mega.nc
        mega.phase += 1
        self.pid = mega.phase
        self.es = ExitStack()
        self.ops = {e: [] for e in ALLENG}
        self.same_engine_sync = same_engine_sync
        self.nbuf = 0
        self.nds = 0
        self.dma_ops = []

    def buf(self, name=None):
        self.nbuf += 1
        return Buf(name or f"b{self.nbuf}")

    def sb(self, name, shape, dtype):
        t = self.es.enter_context(self.nc.sbuf_tensor(f"sb{self.pid}_" + name, list(shape), dtype))
        return View(t[:], [self.buf(name)])

    def ps(self, name, shape, dtype=F32):
        t = self.es.enter_context(self.nc.psum_tensor(f"ps{self.pid}_" + name, list(shape), dtype))
        return View(t[:], [self.buf(name)])

    def _deps(self, op, reads, writes):
        deps = []
        for v in reads:
            for b in v.bufs:
                if b.writer is not None:
                    deps.append(b.writer)
        for v in writes:
            for b in v.bufs:
                if b.writer is not None:
                    deps.append(b.writer)
                deps.extend(b.readers)
        seen = set()
        for d in deps:
            if d is op or id(d) in seen:
                continue
            seen.add(id(d))
            if (not d.is_dma) and d.eng == op.eng:
                if d.eng == "pe" or not self.same_engine_sync:
                    continue
            op.waits.append(d)
        for v in reads:
            for b in v.bufs:
                b.readers.append(op)
        for v in writes:
            for b in v.bufs:
                b.writer = op
                b.readers = []

    def op(self, eng, fn, reads=(), writes=()):
        o = Op(eng, fn)
        self._deps(o, reads, writes)
        self.ops[eng].append(o)
        return o

    def dma(self, eng, out, in_, chan=None, extra=None):
        pairs = [(out, in_)] + list(extra or [])
        if chan is None:
            chan = out.bufs[0] if out.bufs else in_.bufs[0]
        if chan.dma_sem is None:
            assert self.nds < self.mega.NDS, "out of DMA semaphores"
            chan.dma_idx = self.nds
            chan.dma_sem = self.mega.dsem[self.nds]
            chan.dma_cnt = self.mega.dcnt[self.nds]
            self.nds += 1
        sem = chan.dma_sem

        def fn(e, pairs=pairs, sem=sem):
            for (o_, i_) in pairs:
                e.dma_start(out=o_.ap, in_=i_.ap).then_inc(sem, 16)
            return None

        o = Op(eng, fn)
        o.is_dma = True
        o.dsem = sem
        chan.dma_cnt += 16 * len(pairs)
        self.mega.dcnt[chan.dma_idx] = chan.dma_cnt
        o.dval = chan.dma_cnt
        self.dma_ops.append(o)
        if chan.dma_last is not None:
            o.waits.append(chan.dma_last)
        chan.dma_last = o
        self._deps(o, [p[1] for p in pairs], [p[0] for p in pairs])
        self.ops[eng].append(o)
        return o

    def allgather(self, in_ap, out_ap):
        mega = self.mega
        csem = mega.cc_sems[mega.cc_cnt]
        mega.cc_cnt += 1

        def fn(e):
            e.collective_compute("AllGather", ALU.bypass, replica_groups=[list(range(NCORES))],
                                 ins=[in_ap], outs=[out_ap]).then_inc(csem, 1)
            return None
        o = Op("pool", fn)
        o.is_dma = True
        o.dsem = csem
        o.dval = 1
        self.ops["pool"].append(o)
        self.dma_ops.append(o)
        return o

    def emit(self):
        nc = self.nc
        mega = self.mega
        sems = mega.eng_sem
        mk = self.sb("marker", [1, 8], F32)
        markers = []
        markers.append(self.op("act", lambda e: e.activation(out=mk.ap[:, 0:1], in_=mk.ap[:, 4:5], func=AF.Copy)))
        markers.append(self.op("dve", lambda e: e.memset(mk.ap[:, 1:2], 0.0)))
        markers.append(self.op("pool", lambda e: e.memset(mk.ap[:, 2:3], 0.0)))
        for m_ in markers:
            m_.signal = True
        for e in ALLENG:
            for o in self.ops[e]:
                for d in o.waits:
                    if not d.is_dma:
                        d.signal = True
        for e in COMPUTE:
            c = mega.eng_cnt[e]
            for o in self.ops[e]:
                if o.signal and not o.is_dma:
                    c += 1
                    o.idx = c
            mega.eng_cnt[e] = c
        ph_wait = mega.ph_cnt

        def run_stream(e, eng):
            seen = {}
            if ph_wait > 0:
                eng.wait_ge(mega.ph_sem, ph_wait)
            for o in self.ops[e]:
                for d in o.waits:
                    if d.is_dma:
                        key = ("d", id(d.dsem))
                        if seen.get(key, 0) >= d.dval:
                            continue
                        seen[key] = d.dval
                        eng.wait_ge(d.dsem, d.dval)
                    else:
                        key = ("c", d.eng)
                        if seen.get(key, 0) >= d.idx:
                            continue
                        seen[key] = d.idx
                        eng.wait_ge(sems[d.eng], d.idx)
                ins = o.fn(eng)
                if o.signal and not o.is_dma:
                    ins.then_inc(sems[e], 1)
            if e == "sp":
                fin = {}
                for d in self.dma_ops:
                    fin[id(d.dsem)] = (d.dsem, max(d.dval, fin.get(id(d.dsem), (None, 0))[1]))
                for (sm, v) in fin.values():
                    eng.wait_ge(sm, v)
                for m_ in markers:
                    eng.wait_ge(sems[m_.eng], m_.idx)
                eng.dma_start(out=mega.bar_dst, in_=mega.bar_src).then_inc(mega.ph_sem, 16)

        with nc.Block() as block:
            @block.tensor
            def _(eng):
                run_stream("pe", eng)

            @block.scalar
            def _(eng):
                run_stream("act", eng)

            @block.vector
            def _(eng):
                run_stream("dve", eng)

            @block.gpsimd
            def _(eng):
                run_stream("pool", eng)

            @block.sync
            def _(eng):
                run_stream("sp", eng)
        mega.ph_cnt += 16
        self.es.close()


class FM:
    def __init__(self, p, name, nch, Tn, dt, tw=512):
        self.t = p.S.es.enter_context(p.nc.sbuf_tensor(f"fm{p.S.pid}_" + name, [128, nch, Tn], dt))
        self.tw = tw
        self.ntt = Tn // tw
        self.b = [[p.S.buf(f"{name}_{c}_{t}") for t in range(self.ntt)] for c in range(nch)]

    def v(self, c, tt, rows=128):
        return View(self.t[:rows, c, tt * self.tw:(tt + 1) * self.tw], [self.b[c][tt]])

    def vc(self, c, rows=128):
        return View(self.t[:rows, c, :], self.b[c])


class Prog:
    def __init__(self, mega, wslots=3, slot_elems=8192):
        self.mega = mega
        self.nc = mega.nc
        self.S = Sched(mega)
        self.banks = [self.S.ps(f"psb{i}", [128, 512], F32) for i in range(8)]
        self.rr = {}
        self.scr_pool = {}
        self.ones = self.S.sb("ones_bf", [128, 128], BF16)
        o = self.ones
        self.S.op("pool", lambda e: e.memset(o.ap, 1.0), writes=[o])
        self.slots = [self.S.sb(f"wslot{i}", [128, slot_elems], BF16) for i in range(wslots)]
        self.slot_elems = slot_elems
        self.slot_i = 0

    def inp(self, name, shape, dt):
        return self.mega.inp(name, shape, dt)

    def out(self, name, shape, dt):
        return self.mega.out(name, shape, dt)

    def itn(self, name, shape=None, dt=None):
        return self.mega.itn(name, shape, dt)

    def load(self, sbv, dram_ap, eng="sp"):
        return self.S.dma(eng, sbv, View(dram_ap, []))

    def store(self, dram_ap, sbv, eng="sp"):
        return self.S.dma(eng, View(dram_ap, []), sbv, chan=sbv.bufs[0])

    def bank(self, lo=0, hi=6):
        k = (lo, hi)
        i = self.rr.get(k, 0)
        self.rr[k] = (i + 1) % (hi - lo)
        return self.banks[lo + i]

    def scr(self, name, shape, dt, n=2):
        if name not in self.scr_pool:
            self.scr_pool[name] = ([self.S.sb(f"{name}{i}", shape, dt) for i in range(n)], [0])
        lst, ctr = self.scr_pool[name]
        v = lst[ctr[0] % len(lst)]
        ctr[0] += 1
        return v

    def wslot(self):
        s = self.slots[self.slot_i % len(self.slots)]
        self.slot_i += 1
        return s

    def finish(self):
        self.S.emit()


def linear(p, K, groups, rhs, ntt, evac, interleave=False, tw=512):
    S = p.S
    KC = K // 128
    for gi, segs in enumerate(groups):
        slot = p.wslot()
        tot = sum(s[2] for s in segs)
        assert KC * tot <= p.slot_elems, (KC, tot)
        sv3 = slot.ap[:, :KC * tot].rearrange("p (k n) -> p k n", n=tot)
        pairs = []
        offs = []
        off = 0
        for (w, c0, n) in segs:
            pairs.append((View(sv3[:, :, off:off + n], slot.bufs),
                          View(w[:, c0:c0 + n].rearrange("(k p) n -> p k n", p=128), [])))
            offs.append(off)
            off += n
        S.dma("pool", pairs[0][0], pairs[0][1], extra=pairs[1:])
        chunks = []
        for si, (w, c0, n) in enumerate(segs):
            for ci in range(0, n, 128):
                chunks.append((ci // 128, si, offs[si] + ci, min(128, n - ci)))
        if interleave:
            chunks.sort(key=lambda t: (t[0], t[1]))
        for (ci, si, o0, m) in chunks:
            for tt in range(ntt):
                ps = p.bank()
                for k in range(KC):
                    r = rhs(k, tt)
                    S.op("pe", lambda e, ps=ps, k=k, r=r, o0=o0, m=m, sv3=sv3: e.matmul(
                        ps.ap[:m, :tw], lhsT=sv3[:, k, o0:o0 + m], rhs=r.ap, start=(k == 0), stop=(k == KC - 1)),
                        reads=[slot, r], writes=[ps])
                evac(gi, si, ci, m, tt, ps)


def linear_tm(p, K, w, ncols, lhs, ntb, evac, cw=512):
    S = p.S
    KC = K // 128
    gcols = (p.slot_elems // KC) // cw * cw
    for g0 in range(0, ncols, gcols):
        gn = min(gcols, ncols - g0)
        slot = p.wslot()
        sv3 = slot.ap[:, :KC * gn].rearrange("p (k n) -> p k n", n=gn)
        S.dma("pool", View(sv3, slot.bufs), View(w[:, g0:g0 + gn].rearrange("(k p) n -> p k n", p=128), []))
        for tb in range(ntb):
            for c0 in range(0, gn, cw):
                ps = p.bank()
                for k in range(KC):
                    l = lhs(k, tb)
                    S.op("pe", lambda e, ps=ps, k=k, l=l, c0=c0, sv3=sv3: e.matmul(
                        ps.ap[:, :cw], lhsT=l.ap, rhs=sv3[:, k, c0:c0 + cw], start=(k == 0), stop=(k == KC - 1)),
                        reads=[slot, l], writes=[ps])
                evac(tb, g0 + c0, ps)


def norm_fm(p, src, nch, ntt, a, b, dst, inv_n):
    S = p.S
    for tt in range(ntt):
        ps = p.bank(6, 8)
        for c in range(nch):
            sq = p.scr("sq", [128, 512], BF16, 3)
            s_ = src(c, tt)
            S.op("act", lambda e, o=sq, i=s_: e.activation(out=o.ap, in_=i.ap, func=AF.Square), reads=[s_], writes=[sq])
            S.op("pe", lambda e, ps=ps, sq=sq, c=c: e.matmul(ps.ap, lhsT=p.ones.ap, rhs=sq.ap, start=(c == 0), stop=(c == nch - 1)),
                 reads=[sq, p.ones], writes=[ps])
        r = p.scr("rstd", [128, 512], F32, 2)
        S.op("act", lambda e, r=r, ps=ps: e.activation(out=r.ap, in_=ps.ap, func=AF.Sqrt, bias=EPS, scale=inv_n), reads=[ps], writes=[r])
        S.op("dve", lambda e, r=r: e.reciprocal(out=r.ap, in_=r.ap), reads=[r], writes=[r])
        for c in range(nch):
            tmp = p.scr("ntmp", [128, 512], F32, 3)
            s_ = src(c, tt)
            d_ = dst(c, tt)
            S.op("dve", lambda e, tmp=tmp, s_=s_, r=r: e.tensor_tensor(out=tmp.ap, in0=s_.ap, in1=r.ap, op=ALU.mult), reads=[s_, r], writes=[tmp])
            if b is not None:
                S.op("act", lambda e, d_=d_, tmp=tmp, c=c: e.activation(out=d_.ap, in_=tmp.ap, func=AF.Identity, bias=b.ap[:, c:c + 1], scale=a.ap[:, c:c + 1]),
                     reads=[tmp, a, b], writes=[d_])
            else:
                S.op("act", lambda e, d_=d_, tmp=tmp, c=c: e.activation(out=d_.ap, in_=tmp.ap, func=AF.Identity, bias=0.0, scale=a.ap[:, c:c + 1]),
                     reads=[tmp, a], writes=[d_])


class TL(Prog):
    def __init__(self, mega, resident=True, wslots=3):
        super().__init__(mega, wslots=wslots)
        p = self
        self.resident = resident
        if resident:
            self.xT = FM(p, "xT", 16, T, F32)
        self.hT = FM(p, "hT", 16, T, BF16)
        self.modT = self.S.sb("modT", [128, 384], F32)
        self.ng1 = self.S.sb("ng1", [128, 64], F32)
        self.ng2 = self.S.sb("ng2", [128, 64], F32)
        mod_all = self.itn("mod_all")
        mstg = self.S.sb("modstg", [128, 8, 48], F32)
        self.load(mstg, mod_all.rearrange("(r p) f -> p r f", p=128))
        m4 = self.modT.ap.rearrange("p (l r j) -> p l r j", l=4, r=8)
        for l in range(4):
            self.S.op("dve", lambda e, l=l: e.tensor_copy(out=m4[:, l], in_=mstg.ap[:, :, l * 12:(l + 1) * 12]), reads=[mstg], writes=[self.modT])
        self.load(self.ng1, self.inp("ng1T", [128, 64], F32))
        self.load(self.ng2, self.inp("ng2T", [128, 64], F32))

    def load_x(self, name):
        xin = self.itn(name) if name in self.mega.internal else self.inp(name, [D, T], F32)
        if not self.resident:
            def src(c, tt):
                t_ = self.scr("xs", [128, 512], F32, 3)
                self.load(t_, xin[c * 128:(c + 1) * 128, tt * 512:(tt + 1) * 512])
                return t_
            self.x_src = src
            return
        self.x_src = self.xT.v
        for c in range(16):
            self.load(self.xT.vc(c), xin[c * 128:(c + 1) * 128, :])

    def store_x(self, name):
        xo = self.itn(name, [D, T], F32)
        for c in range(16):
            self.store(xo[c * 128:(c + 1) * 128, :], self.xT.vc(c))

    def adaln(self, layer, which):
        S = self.S
        base = layer * 96 + which * 48
        ng = self.ng1 if which == 0 else self.ng2
        a = S.sb(f"ada_{layer}_{which}", [128, 16], F32)
        S.op("dve", lambda e: e.scalar_tensor_tensor(out=a.ap, in0=self.modT.ap[:, base + 16:base + 32], scalar=1.0,
                                                      in1=ng.ap[:, layer * 16:(layer + 1) * 16], op0=ALU.add, op1=ALU.mult),
             reads=[self.modT, ng], writes=[a])
        sh = View(self.modT.ap[:, base:base + 16], self.modT.bufs)
        g = View(self.modT.ap[:, base + 32:base + 48], self.modT.bufs)
        return a, sh, g

    def resid_evac(self, g, cpg=4):
        S = self.S

        def ev(gi, si, ci, m, tt, ps, g=g):
            c = gi * cpg + ci
            xv = self.xT.v(c, tt)
            S.op("dve", lambda e: e.scalar_tensor_tensor(out=xv.ap, in0=ps.ap, scalar=g.ap[:, c:c + 1], in1=xv.ap, op0=ALU.mult, op1=ALU.add),
                 reads=[ps, g, xv], writes=[xv])
        return ev

    def ffn(self, layer):
        p, S = self, self.S
        a, sh, g = self.adaln(layer, 1)
        norm_fm(p, self.x_src, 16, 2, a, sh, self.hT.v, 1.0 / D)
        wg = self.inp(f"wg{layer}", [D, DFF], F32)
        wu = self.inp(f"wu{layer}", [D, DFF], F32)
        wd = self.inp(f"wd{layer}", [DFF, D], F32)
        if not hasattr(self, "aT"):
            self.aT = FM(p, "aT", 6, T, BF16)
        aT = self.aT
        parts = [6, 6, 6, 6, 5, 5, 5, 5]
        for q in range(8):
            j0 = sum(parts[:q])
            nj = parts[q]
            sizes = [2, 2, 2] if nj == 6 else [2, 2, 1]
            groups = []
            jj = j0
            gstart = []
            for sz in sizes:
                groups.append([(wg, jj * 128, sz * 128), (wu, jj * 128, sz * 128)])
                gstart.append(jj - j0)
                jj += sz
            sgs = {}

            def ev(gi, si, ci, m, tt, ps):
                jl = gstart[gi] + ci
                if si == 0:
                    sg = p.scr("sg", [128, 512], F32, 5)
                    sgs[(jl, tt)] = sg
                    S.op("act", lambda e: e.activation(out=sg.ap, in_=ps.ap, func=AF.Silu), reads=[ps], writes=[sg])
                else:
                    sg = sgs[(jl, tt)]
                    av = aT.v(jl, tt)
                    S.op("dve", lambda e: e.tensor_tensor(out=av.ap, in0=sg.ap, in1=ps.ap, op=ALU.mult), reads=[sg, ps], writes=[av])
            linear(p, D, groups, self.hT.v, 2, ev, interleave=True)
            wdq = wd[j0 * 128:(j0 + nj) * 128, :]
            linear(p, nj * 128, [[(wdq, n * 1024, 1024)] for n in range(2)], aT.v, 2, self.resid_evac(g, 8))

    def conv1(self, layer, widx):
        p, S = self, self.S
        a, sh, g = self.adaln(layer, 0)
        norm_fm(p, self.x_src, 16, 2, a, sh, self.hT.v, 1.0 / D)
        w = self.inp(f"cw_in{widx}", [D, 3 * D], F32)
        oB = self.itn(f"convB{layer}", [D, T], BF16)
        oCU = self.itn(f"convCU{layer}", [D, T], F32)
        oTL = self.itn(f"tail_in{layer}", [128, 32], F32)
        groups = [[(w, n * 128, 128), (w, D + n * 128, 128), (w, 2 * D + n * 128, 128)] for n in range(16)]
        st = {}

        def ev(gi, si, ci, m, tt, ps):
            cs = slice(tt * 512, (tt + 1) * 512)
            if si == 0:
                if tt == 0:
                    st["B"] = p.scr("stB", [128, T], BF16, 2)
                b_ = st["B"]
                S.op("act", lambda e: e.activation(out=b_.ap[:, cs], in_=ps.ap, func=AF.Copy), reads=[ps], writes=[b_])
                if tt == 1:
                    p.store(oB[gi * 128:(gi + 1) * 128, :], b_)
            elif si == 1:
                c_ = p.scr("cC", [128, 512], F32, 3)
                st[("C", tt)] = c_
                S.op("act", lambda e: e.activation(out=c_.ap, in_=ps.ap, func=AF.Copy), reads=[ps], writes=[c_])
            else:
                if tt == 0:
                    st["CU"] = p.scr("stCU", [128, T], F32, 2)
                cu = st["CU"]
                c_ = st[("C", tt)]
                S.op("dve", lambda e: e.tensor_tensor(out=cu.ap[:, cs], in0=c_.ap, in1=ps.ap, op=ALU.mult), reads=[c_, ps], writes=[cu])
                if tt == 1:
                    p.store(oCU[gi * 128:(gi + 1) * 128, :], cu)
                    p.store(oTL[:, 2 * gi:2 * gi + 2], View(cu.ap[:, T - 2:T], cu.bufs))
        linear(p, D, groups, self.hT.v, 2, ev)

    def conv2(self, layer, widx):
        p, S = self, self.S
        base = layer * 96
        g = View(self.modT.ap[:, base + 32:base + 48], self.modT.bufs)
        iB = self.itn(f"convB{layer}")
        iCU = self.itn(f"convCU{layer}")
        tl_all = self.itn(f"tail_all{layer}")
        tls = S.sb("tls", [128, 8, 32], F32)
        self.load(tls, tl_all.rearrange("(j p) f -> p j f", p=128))
        selp = S.sb("selp", [128, 8], F32)
        self.load(selp, self.inp("selprev", [128, 8], F32))
        halo = S.sb("halo", [128, 32], F32)
        for j in range(8):
            srcj = tls.ap[:, j]
            if j == 0:
                S.op("dve", lambda e, srcj=srcj: e.tensor_scalar(out=halo.ap, in0=srcj, scalar1=selp.ap[:, 0:1], scalar2=None, op0=ALU.mult),
                     reads=[tls, selp], writes=[halo])
            else:
                S.op("dve", lambda e, srcj=srcj, j=j: e.scalar_tensor_tensor(out=halo.ap, in0=srcj, scalar=selp.ap[:, j:j + 1], in1=halo.ap, op0=ALU.mult, op1=ALU.add),
                     reads=[tls, selp, halo], writes=[halo])
        icw = self.inp(f"cw{widx}", [128, 48], F32)
        wout = self.inp(f"cw_out{widx}", [D, D], F32)
        cw = S.sb("convw", [128, 48], F32)
        self.load(cw, icw)
        for c in range(16):
            cu = p.scr("cu_in", [128, T + 2], F32, 2)
            bc = p.scr("b_in", [128, T], BF16, 2)
            S.dma("sp", View(cu.ap[:, 2:], cu.bufs), View(iCU[c * 128:(c + 1) * 128, :], []))
            S.op("dve", lambda e, cu=cu, c=c: e.tensor_copy(out=cu.ap[:, 0:2], in_=halo.ap[:, 2 * c:2 * c + 2]), reads=[halo], writes=[cu])
            self.load(bc, iB[c * 128:(c + 1) * 128, :])
            z = p.scr("convz", [128, T], F32, 2)
            S.op("dve", lambda e, z=z, cu=cu, c=c: e.tensor_scalar(out=z.ap, in0=cu.ap[:, 2:2 + T], scalar1=cw.ap[:, 32 + c:33 + c], scalar2=None, op0=ALU.mult),
                 reads=[cu, cw], writes=[z])
            S.op("dve", lambda e, z=z, cu=cu, c=c: e.scalar_tensor_tensor(out=z.ap, in0=cu.ap[:, 1:1 + T], scalar=cw.ap[:, 16 + c:17 + c], in1=z.ap, op0=ALU.mult, op1=ALU.add),
                 reads=[cu, cw, z], writes=[z])
            S.op("dve", lambda e, z=z, cu=cu, c=c: e.scalar_tensor_tensor(out=z.ap, in0=cu.ap[:, 0:T], scalar=cw.ap[:, c:c + 1], in1=z.ap, op0=ALU.mult, op1=ALU.add),
                 reads=[cu, cw, z], writes=[z])
            hv = self.hT.vc(c)
            S.op("pool", lambda e, z=z, bc=bc, hv=hv: e.tensor_tensor(out=hv.ap, in0=z.ap, in1=bc.ap, op=ALU.mult), reads=[z, bc], writes=[hv])
        linear(p, D, [[(wout, n * 512, 512)] for n in range(4)], self.hT.v, 2, self.resid_evac(g))

    def final(self):
        p, S = self, self.S
        fg = S.sb("fng", [128, 16], F32)
        self.load(fg, self.inp("fngT", [128, 16], F32))
        oo = self.out("outT", [D, T], F32)
        for tt in range(2):
            ps = p.bank(6, 8)
            for c in range(16):
                sq = p.scr("sq", [128, 512], BF16, 3)
                s_ = self.xT.v(c, tt)
                S.op("act", lambda e, o=sq, i=s_: e.activation(out=o.ap, in_=i.ap, func=AF.Square), reads=[s_], writes=[sq])
                S.op("pe", lambda e, ps=ps, sq=sq, c=c: e.matmul(ps.ap, lhsT=p.ones.ap, rhs=sq.ap, start=(c == 0), stop=(c == 15)),
                     reads=[sq, p.ones], writes=[ps])
            r = p.scr("rstd", [128, 512], F32, 2)
            S.op("act", lambda e, r=r, ps=ps: e.activation(out=r.ap, in_=ps.ap, func=AF.Sqrt, bias=EPS, scale=1.0 / D), reads=[ps], writes=[r])
            S.op("dve", lambda e, r=r: e.reciprocal(out=r.ap, in_=r.ap), reads=[r], writes=[r])
            for c in range(16):
                s_ = self.xT.v(c, tt)
                d_ = p.scr("ntmp", [128, 512], F32, 3)
                S.op("dve", lambda e, d_=d_, s_=s_, r=r, c=c: e.scalar_tensor_tensor(out=d_.ap, in0=s_.ap, scalar=fg.ap[:, c:c + 1], in1=r.ap, op0=ALU.mult, op1=ALU.mult),
                     reads=[s_, r, fg], writes=[d_])
                p.store(oo[c * 128:(c + 1) * 128, tt * 512:(tt + 1) * 512], d_)

    def evac_store(self, name, out_ap, dt, row_of, scale=None, eng="act"):
        p, S = self, self.S
        st = {}

        def ev(gi, si, ci, m, tt, ps):
            cs = slice(tt * 512, (tt + 1) * 512)
            if tt == 0:
                st["t"] = p.scr("st_" + name, [128, T], dt, 2)
            t_ = st["t"]
            if scale is None:
                S.op("act", lambda e: e.activation(out=t_.ap[:m, cs], in_=ps.ap[:m, :], func=AF.Copy), reads=[ps], writes=[t_])
            else:
                S.op("act", lambda e: e.activation(out=t_.ap[:m, cs], in_=ps.ap[:m, :], func=AF.Copy, scale=scale), reads=[ps], writes=[t_])
            if tt == 1:
                r0 = row_of(gi, si, ci)
                p.store(out_ap[r0:r0 + m, :], View(t_.ap[:m, :], t_.bufs))
        return ev

    def rope_tables(self, pos_ap):
        p, S = self, self.S
        if not hasattr(self, "_rt"):
            rc = S.sb("ropec", [128, 2], F32)
            self.load(rc, self.inp("ropec", [128, 2], F32))
            self._rt = (S.sb("posi", [128, T], I32), rc, S.sb("ang", [128, T], F32),
                        S.sb("rt_sin", [128, T], F32), S.sb("rt_cos", [128, T], F32))
        posi, rc, ang, ysin, ycos = self._rt
        self.load(posi, pos_ap)
        S.op("dve", lambda e: e.tensor_copy(out=ang.ap, in_=posi.ap), reads=[posi], writes=[ang])
        S.op("dve", lambda e: e.tensor_scalar(out=ang.ap, in0=ang.ap, scalar1=rc.ap[:, 0:1], scalar2=None, op0=ALU.mult), reads=[ang, rc], writes=[ang])
        C1 = 6.28125
        C2 = 2 * math.pi - C1
        outs = []
        for name, shift in (("sin", 0.0), ("cos", math.pi / 2)):
            y = ysin if name == "sin" else ycos
            ni = p.scr("rt_ni", [128, T], I32, 1)
            nf = p.scr("rt_nf", [128, T], F32, 1)
            S.op("dve", lambda e, y=y, shift=shift: e.tensor_scalar(out=y.ap, in0=ang.ap, scalar1=shift, scalar2=None, op0=ALU.add), reads=[ang], writes=[y])
            S.op("dve", lambda e, y=y, nf=nf: e.tensor_scalar(out=nf.ap, in0=y.ap, scalar1=1.0 / (2 * math.pi), scalar2=None, op0=ALU.mult), reads=[y], writes=[nf])
            S.op("dve", lambda e, ni=ni, nf=nf: e.tensor_copy(out=ni.ap, in_=nf.ap), reads=[nf], writes=[ni])
            S.op("dve", lambda e, ni=ni, nf=nf: e.tensor_copy(out=nf.ap, in_=ni.ap), reads=[ni], writes=[nf])
            S.op("dve", lambda e, y=y, nf=nf: e.scalar_tensor_tensor(out=y.ap, in0=nf.ap, scalar=-C1, in1=y.ap, op0=ALU.mult, op1=ALU.add), reads=[nf, y], writes=[y])
            S.op("dve", lambda e, y=y, nf=nf: e.scalar_tensor_tensor(out=y.ap, in0=nf.ap, scalar=-C2, in1=y.ap, op0=ALU.mult, op1=ALU.add), reads=[nf, y], writes=[y])
            S.op("dve", lambda e, y=y: e.tensor_scalar(out=y.ap, in0=y.ap, scalar1=math.pi, scalar2=-math.pi, op0=ALU.min, op1=ALU.max), reads=[y], writes=[y])
            S.op("act", lambda e, y=y: e.activation(out=y.ap, in_=y.ap, func=AF.Sin), reads=[y], writes=[y])
            outs.append(y)
        sin, cos = outs
        S.op("dve", lambda e: e.tensor_scalar(out=sin.ap, in0=sin.ap, scalar1=rc.ap[:, 1:2], scalar2=-1.0, op0=ALU.mult, op1=ALU.mult), reads=[sin, rc], writes=[sin])
        self.cos2, self.sin2s = cos, sin

    def rope_evac(self, name, out_ap, row_of):
        p, S = self, self.S
        st = {}

        def ev(gi, si, ci, m, tt, ps):
            cs = slice(tt * 512, (tt + 1) * 512)
            if si == 0:
                t1 = p.scr("rp_t1", [128, 512], F32, 4)
                st[(ci, tt)] = t1
                S.op("dve", lambda e: e.tensor_tensor(out=t1.ap[:m, :], in0=ps.ap[:m, :], in1=self.cos2.ap[:m, cs], op=ALU.mult), reads=[ps, self.cos2], writes=[t1])
            else:
                t1 = st[(ci, tt)]
                t2 = p.scr("rp_t2", [128, 512], F32, 2)
                S.op("dve", lambda e: e.tensor_tensor(out=t2.ap[:m, :], in0=ps.ap[:m, :], in1=self.sin2s.ap[:m, cs], op=ALU.mult), reads=[ps, self.sin2s], writes=[t2])
                if tt == 0:
                    st["o"] = p.scr("st_" + name, [128, T], BF16, 2)
                o_ = st["o"]
                S.op("pool", lambda e: e.tensor_tensor(out=o_.ap[:m, cs], in0=t1.ap[:m, :], in1=t2.ap[:m, :], op=ALU.add), reads=[t1, t2], writes=[o_])
                if tt == 1:
                    r0 = row_of(gi, ci)
                    p.store(out_ap[r0:r0 + m, :], View(o_.ap[:m, :], o_.bufs))
        return ev

    def mla_lat(self):
        p, S = self, self.S
        a, sh, g = self.adaln(1, 0)
        norm_fm(p, self.x_src, 16, 2, a, sh, self.hT.v, 1.0 / D)
        self.rope_tables(self.inp("pos_own", [128, T], I32))
        w_dq = self.inp("w_dq", [D, 768], F32)
        w_dkvc = self.inp("w_dkvc", [D, 512], F32)
        w_dkvp = self.inp("w_dkvp", [D, 64], F32)
        w_dkvps = self.inp("w_dkvps", [D, 64], F32)
        qng = S.sb("qng", [128, 6], F32)
        kvng = S.sb("kvng", [128, 4], F32)
        self.load(qng, self.inp("qngT", [128, 6], F32))
        self.load(kvng, self.inp("kvngT", [128, 4], F32))
        lat = self.itn("lat_in", [1344, T], BF16)
        cqpre = FM(p, "cqpre", 6, T, F32)
        cq = FM(p, "cq", 6, T, BF16)

        def ev_pre(cbase):
            def ev(gi, si, ci, m, tt, ps):
                d_ = cqpre.v(cbase(gi) + ci, tt)
                S.op("act", lambda e: e.activation(out=d_.ap, in_=ps.ap, func=AF.Copy), reads=[ps], writes=[d_])
            return ev
        linear(p, D, [[(w_dq, 0, 512)], [(w_dq, 512, 256)]], self.hT.v, 2, ev_pre(lambda gi: gi * 4))
        norm_fm(p, cqpre.v, 6, 2, qng, None, cq.v, 1.0 / 768)
        for c in range(6):
            p.store(lat[c * 128:(c + 1) * 128, :], cq.vc(c))
        linear(p, D, [[(w_dkvc, 0, 512)]], self.hT.v, 2, ev_pre(lambda gi: 0))
        norm_fm(p, cqpre.v, 4, 2, kvng, None, cq.v, 1.0 / 512)
        for c in range(4):
            p.store(lat[768 + c * 128:768 + (c + 1) * 128, :], cq.vc(c))
        linear(p, D, [[(w_dkvp, 0, 64), (w_dkvps, 0, 64)]], self.hT.v, 2,
               self.rope_evac("kp", lat[1280:1344, :], lambda gi, ci: 0), interleave=True)

    def load_sel(self):
        if not hasattr(self, "sel"):
            self.sel = self.S.sb("sel", [128, 8], F32)
            self.load(self.sel, self.inp("sel", [128, 8], F32))
        return self.sel

    def attn_out(self):
        p, S = self, self.S
        base = 1 * 96
        g = View(self.modT.ap[:, base + 32:base + 48], self.modT.bufs)
        att_all = self.itn("att_all")
        wo = self.inp("w_o", [D, D], F32)
        sel = self.load_sel()
        k = 0
        for c in range(16):
            hv = self.hT.vc(c)
            for half in range(2):
                blk = p.scr("att_blk", [128, 4 * T], BF16, 2)
                self.load(blk, att_all[c * 128:(c + 1) * 128, half * 4 * T:(half + 1) * 4 * T])
                for b4 in range(4):
                    b = half * 4 + b4
                    eng = "dve"
                    src = blk.ap[:, b4 * T:(b4 + 1) * T]
                    if b == 0:
                        S.op(eng, lambda e, src=src, hv=hv: e.tensor_scalar(out=hv.ap, in0=src, scalar1=sel.ap[:, 0:1], scalar2=None, op0=ALU.mult),
                             reads=[blk, sel], writes=[hv])
                    else:
                        S.op(eng, lambda e, src=src, hv=hv, b=b: e.scalar_tensor_tensor(out=hv.ap, in0=src, scalar=sel.ap[:, b:b + 1], in1=hv.ap, op0=ALU.mult, op1=ALU.add),
                             reads=[blk, sel, hv], writes=[hv])
        linear(p, D, [[(wo, n * 512, 512)] for n in range(4)], self.hT.v, 2, self.resid_evac(g))

    def mlstm_tl(self):
        p, S = self, self.S
        a, sh, g = self.adaln(2, 0)
        norm_fm(p, self.x_src, 16, 2, a, sh, self.hT.v, 1.0 / D)
        h_in = self.itn("h_in", [D, T], BF16)
        for c in range(16):
            p.store(h_in[c * 128:(c + 1) * 128, :], self.hT.vc(c))
        wo = self.inp("m_wo", [D, 2048], F32)
        o_so = self.itn("sigoT", [D, T], BF16)
        st = {}

        def ev_so(gi, si, ci, m, tt, ps):
            cs = slice(tt * 512, (tt + 1) * 512)
            if tt == 0:
                st["t"] = p.scr("st_so", [128, T], BF16, 2)
            t_ = st["t"]
            S.op("act", lambda e: e.activation(out=t_.ap[:, cs], in_=ps.ap, func=AF.Sigmoid), reads=[ps], writes=[t_])
            if tt == 1:
                r0 = gi * 512 + ci * 128
                p.store(o_so[r0:r0 + 128, :], t_)
        linear(p, D, [[(wo, n * 512, 512)] for n in range(4)], self.hT.v, 2, ev_so)

    def mlstm_post(self):
        p, S = self, self.S
        base = 2 * 96
        g = View(self.modT.ap[:, base + 32:base + 48], self.modT.bufs)
        mh_all = self.itn("mh_all")
        iso = self.itn("sigoT")
        wout = self.inp("m_wout", [D, D], F32)
        hng = S.sb("hng", [128, 16], F32)
        self.load(hng, self.inp("hngT", [128, 16], F32))
        ident = S.sb("ident", [128, 128], BF16)
        self.load(ident, self.inp("ident", [128, 128], BF16))
        sel = self.load_sel()
        mh5 = mh_all.rearrange("(i b t p) d -> i t p b d", i=8, b=8, t=8, p=128)
        k = 0
        for tb in range(8):
            hs = p.scr("mh_sel", [128, 2048], BF16, 2)
            for i in range(8):
                blk = p.scr("mh_blk", [128, 8, 256], BF16, 3)
                self.load(blk, mh5[i, tb])
                dst = hs.ap[:, i * 256:(i + 1) * 256]
                eng = "dve"
                for b in range(8):
                    if b == 0:
                        S.op(eng, lambda e, blk=blk, dst=dst: e.tensor_scalar(out=dst, in0=blk.ap[:, 0, :], scalar1=sel.ap[:, 0:1], scalar2=None, op0=ALU.mult),
                             reads=[blk, sel], writes=[hs])
                    else:
                        S.op(eng, lambda e, blk=blk, dst=dst, b=b: e.scalar_tensor_tensor(out=dst, in0=blk.ap[:, b, :], scalar=sel.ap[:, b:b + 1], in1=dst, op0=ALU.mult, op1=ALU.add),
                             reads=[blk, sel, hs], writes=[hs])
            for hd in range(4):
                hsl = hs.ap[:, hd * 512:(hd + 1) * 512]
                junk = p.scr("mh_junk", [128, 512], BF16, 2)
                ss = p.scr("mh_ss", [128, 1], F32, 4)
                S.op("act", lambda e, junk=junk, hsl=hsl, ss=ss: e.activation(out=junk.ap, in_=hsl, func=AF.Square, accum_out=ss.ap), reads=[hs], writes=[junk, ss])
                S.op("act", lambda e, ss=ss: e.activation(out=ss.ap, in_=ss.ap, func=AF.Sqrt, bias=EPS, scale=1.0 / 512), reads=[ss], writes=[ss])
                S.op("dve", lambda e, ss=ss: e.reciprocal(out=ss.ap, in_=ss.ap), reads=[ss], writes=[ss])
                hn = p.scr("mh_hn", [128, 512], BF16, 2)
                S.op("dve", lambda e, hn=hn, hsl=hsl, ss=ss: e.tensor_scalar(out=hn.ap, in0=hsl, scalar1=ss.ap[:, 0:1], scalar2=None, op0=ALU.mult), reads=[hs, ss], writes=[hn])
                ps = p.bank()
                for cc in range(4):
                    S.op("pe", lambda e, ps=ps, hn=hn, cc=cc: e.matmul(ps.ap[:, cc * 128:(cc + 1) * 128], lhsT=hn.ap[:, cc * 128:(cc + 1) * 128], rhs=ident.ap, start=True, stop=True),
                         reads=[hn, ident], writes=[ps])
                for cc in range(4):
                    c = hd * 4 + cc
                    hv = View(self.hT.t[:, c, tb * 128:(tb + 1) * 128], [self.hT.b[c][tb // 4]])
                    S.op("act", lambda e, hv=hv, ps=ps, cc=cc, c=c: e.activation(out=hv.ap, in_=ps.ap[:, cc * 128:(cc + 1) * 128], func=AF.Identity, bias=0.0, scale=hng.ap[:, c:c + 1]),
                         reads=[ps, hng], writes=[hv])
        for c in range(16):
            so = p.scr("so_in", [128, T], BF16, 2)
            self.load(so, iso[c * 128:(c + 1) * 128, :])
            hv = self.hT.vc(c)
            S.op("pool", lambda e, hv=hv, so=so: e.tensor_tensor(out=hv.ap, in0=hv.ap, in1=so.ap, op=ALU.mult), reads=[hv, so], writes=[hv])
        linear(p, D, [[(wout, n * 512, 512)] for n in range(4)], self.hT.v, 2, self.resid_evac(g))


def phase_gather(mega, in_name, out_name, rows, cols, dt):
    p = Prog(mega, wslots=1, slot_elems=16)
    i_ap = mega.itn(in_name)
    o_ap = mega.itn(out_name, [NCORES * rows, cols], dt)
    p.S.allgather(i_ap, o_ap)
    p.finish()


def phase_mods(mega):
    p = Prog(mega, wslots=2, slot_elems=16 * 512)
    S = p.S
    ic = p.inp("c_col", [128, 16], F32)
    iw = p.inp("mod_w_s", [4, D, 1536], F32)
    ib = p.inp("mod_b_s", [1, 4 * 1536], F32)
    oo = p.itn("mod_in", [128, 48], F32)
    cc = S.sb("cc", [128, 16], F32)
    p.load(cc, ic)
    cb = S.sb("cb", [128, 16], BF16)
    S.op("act", lambda e: e.activation(out=cb.ap, in_=cc.ap, func=AF.Silu), reads=[cc], writes=[cb])
    mb = S.sb("mb", [1, 4 * 1536], F32)
    p.load(mb, ib)
    row = S.sb("row", [1, 4 * 1536], F32)
    one1 = S.sb("one1", [1, 1], F32)
    S.op("dve", lambda e: e.memset(one1.ap, 1.0), writes=[one1])
    res = S.sb("res", [128, 48], F32)
    pst = p.banks[7]
    for l in range(4):
        for g in range(3):
            slot = p.wslot()
            sv3 = slot.ap.rearrange("p (k n) -> p k n", n=512)
            S.dma("pool", View(sv3, slot.bufs), View(iw[l, :, g * 512:(g + 1) * 512].rearrange("(k p) n -> p k n", p=128), []))
            ps = p.bank()
            for k in range(16):
                S.op("pe", lambda e, ps=ps, k=k, sv3=sv3: e.matmul(ps.ap[0:1, :], lhsT=cb.ap[:, k:k + 1], rhs=sv3[:, k, :], start=(k == 0), stop=(k == 15)),
                     reads=[slot, cb], writes=[ps])
            c0 = l * 1536 + g * 512
            S.op("dve", lambda e, ps=ps, c0=c0: e.tensor_tensor(out=row.ap[:, c0:c0 + 512], in0=ps.ap[0:1, :], in1=mb.ap[:, c0:c0 + 512], op=ALU.add),
                 reads=[ps, mb], writes=[row])
    for j in range(48):
        S.op("pe", lambda e, j=j: e.matmul(pst.ap[:, j:j + 1], lhsT=row.ap[0:1, j * 128:(j + 1) * 128], rhs=one1.ap, start=True, stop=True),
             reads=[row, one1], writes=[pst])
    S.op("dve", lambda e: e.tensor_copy(out=res.ap, in_=pst.ap[:, 0:48]), reads=[pst], writes=[res])
    p.store(oo, res)
    p.finish()


def phase_attn_pre(mega):
    p = TL(mega, resident=False, wslots=2)
    S = p.S
    lat = mega.itn("lat_all").rearrange("(i r) t -> i r t", i=NCORES)
    wqn = p.inp("a_wqn", [768, 256], F32)
    wqp = p.inp("a_wqp", [768, 128], F32)
    wqps = p.inp("a_wqps", [768, 128], F32)
    wuk = p.inp("a_wuk", [512, 256], F32)
    wuv = p.inp("a_wuv", [512, 256], F32)
    pos_all = p.inp("pos_all", [128, SEQ], I32)
    qn_i = mega.itn("a_qn", [256, SEQ], BF16)
    qp_i = mega.itn("a_qp", [128, SEQ], BF16)
    kn_i = mega.itn("a_kn", [256, SEQ], BF16)
    kp_i = mega.itn("a_kp", [64, SEQ], BF16)
    v_i = mega.itn("a_v", [SEQ, 256], BF16)
    cq = FM(p, "cq", 6, T, BF16)
    ckv = FM(p, "ckv", 4, T, BF16)
    for i in range(NCORES):
        cs = slice(i * T, (i + 1) * T)
        for c in range(6):
            p.load(cq.vc(c), lat[i, c * 128:(c + 1) * 128, :])
        for c in range(4):
            p.load(ckv.vc(c), lat[i, 768 + c * 128:768 + (c + 1) * 128, :])
        kpt = p.scr("kp_cp", [64, T], BF16, 2)
        p.load(kpt, lat[i, 1280:1344, :])
        p.store(kp_i[:, cs], kpt)
        p.rope_tables(pos_all[:, cs])
        linear(p, 768, [[(wqn, 0, 256)]], cq.v, 2, p.evac_store("qn", qn_i[:, cs], BF16, lambda gi, si, ci: ci * 128))
        linear(p, 768, [[(wqp, 0, 128), (wqps, 0, 128)]], cq.v, 2, p.rope_evac("qp", qp_i[:, cs], lambda gi, ci: 0), interleave=True)
        linear(p, 512, [[(wuk, 0, 256)]], ckv.v, 2, p.evac_store("kn", kn_i[:, cs], BF16, lambda gi, si, ci: ci * 128))

        def ev_v(tb, c0, ps, i=i):
            t_ = p.scr("st_v", [128, 256], BF16, 3)
            S.op("act", lambda e: e.activation(out=t_.ap, in_=ps.ap[:, :256], func=AF.Copy), reads=[ps], writes=[t_])
            p.store(v_i[i * T + tb * 128:i * T + (tb + 1) * 128, :], t_)
        linear_tm(p, 512, wuv, 256, lambda k, tb: View(ckv.t[:, k, tb * 128:(tb + 1) * 128], [ckv.b[k][tb // 4]]), 8, ev_v, cw=256)
    p.finish()


def phase_mlstm_hs(mega):
    p = TL(mega, resident=False, wslots=2)
    S = p.S
    h_all = mega.itn("h_all").rearrange("(i r) t -> i r t", i=NCORES)
    wq = p.inp("h_wq", [D, 256], F32)
    wk = p.inp("h_wk", [D, 256], F32)
    wv = p.inp("h_wv", [D, 256], F32)
    wg = p.inp("h_wg", [D, 2], F32)
    bg = S.sb("m_bg", [2, 1], F32)
    p.load(bg, p.inp("h_bg", [2, 1], F32))
    o_q = mega.itn("m_qT", [256, SEQ], BF16)
    o_k = mega.itn("m_kT", [256, SEQ], BF16)
    o_ktm = mega.itn("m_ktm", [SEQ, 256], BF16)
    o_vtm = mega.itn("m_vtm", [SEQ, 256], BF16)
    o_ig = mega.itn("m_ig", [1, SEQ], F32)
    o_lf = mega.itn("m_lf", [1, SEQ], F32)
    for i in range(NCORES):
        cs = slice(i * T, (i + 1) * T)
        for c in range(16):
            p.load(p.hT.vc(c), h_all[i, c * 128:(c + 1) * 128, :])
        linear(p, D, [[(wq, 0, 256)]], p.hT.v, 2, p.evac_store("mq", o_q[:, cs], BF16, lambda gi, si, ci: ci * 128))
        linear(p, D, [[(wk, 0, 256)]], p.hT.v, 2, p.evac_store("mk", o_k[:, cs], BF16, lambda gi, si, ci: ci * 128, scale=1.0 / 16))
        gst = {}

        def ev_g(gi, si, ci, m, tt, ps, cs=cs):
            cl = slice(tt * 512, (tt + 1) * 512)
            if tt == 0:
                gst["i"] = p.scr("g_i", [2, T], F32, 2)
                gst["f"] = p.scr("g_f", [2, T], F32, 2)
            gi_, gf_ = gst["i"], gst["f"]
            S.op("act", lambda e: e.activation(out=gi_.ap[:, cl], in_=ps.ap[:2, :], func=AF.Identity, bias=bg.ap[:, 0:1], scale=1.0), reads=[ps, bg], writes=[gi_])
            S.op("act", lambda e: e.activation(out=gi_.ap[:, cl], in_=gi_.ap[:, cl], func=AF.Tanh, scale=1.0 / 15), reads=[gi_], writes=[gi_])
            S.op("dve", lambda e: e.tensor_scalar(out=gi_.ap[:, cl], in0=gi_.ap[:, cl], scalar1=15.0, scalar2=None, op0=ALU.mult), reads=[gi_], writes=[gi_])
            S.op("act", lambda e: e.activation(out=gf_.ap[:, cl], in_=gi_.ap[:, cl], func=AF.Exp, scale=-1.0), reads=[gi_], writes=[gf_])
            S.op("act", lambda e: e.activation(out=gf_.ap[:, cl], in_=gf_.ap[:, cl], func=AF.Ln, bias=1.0, scale=1.0), reads=[gf_], writes=[gf_])
            S.op("dve", lambda e: e.tensor_scalar(out=gf_.ap[:, cl], in0=gf_.ap[:, cl], scalar1=-1.0, scalar2=None, op0=ALU.mult), reads=[gf_], writes=[gf_])
            if tt == 1:
                p.store(o_ig[:, cs], View(gi_.ap[0:1, :], gi_.bufs))
                p.store(o_lf[:, cs], View(gf_.ap[1:2, :], gf_.bufs))
        linear(p, D, [[(wg, 0, 2)]], p.hT.v, 2, ev_g)

        def lhs(k, tb):
            return View(p.hT.t[:, k, tb * 128:(tb + 1) * 128], [p.hT.b[k][tb // 4]])

        def ev_k(tb, c0, ps, i=i):
            t_ = p.scr("st_v", [128, 256], BF16, 3)
            S.op("act", lambda e: e.activation(out=t_.ap, in_=ps.ap[:, :256], func=AF.Copy, scale=1.0 / 16), reads=[ps], writes=[t_])
            p.store(o_ktm[i * T + tb * 128:i * T + (tb + 1) * 128, :], t_)

        def ev_v(tb, c0, ps, i=i):
            t_ = p.scr("st_v", [128, 256], BF16, 3)
            S.op("act", lambda e: e.activation(out=t_.ap, in_=ps.ap[:, :256], func=AF.Copy), reads=[ps], writes=[t_])
            p.store(o_vtm[i * T + tb * 128:i * T + (tb + 1) * 128, :], t_)
        linear_tm(p, D, wk, 256, lhs, 8, ev_k, cw=256)
        linear_tm(p, D, wv, 256, lhs, 8, ev_v, cw=256)
    p.finish()


def phase_attn(mega):
    p = Prog(mega, wslots=1, slot_elems=16)
    S = p.S
    scale = 192 ** -0.5
    iqn = mega.itn("a_qn").rearrange("(h d) s -> h d s", h=2)
    iqp = mega.itn("a_qp").rearrange("(h d) s -> h d s", h=2)
    ikn = mega.itn("a_kn").rearrange("(h d) s -> h d s", h=2)
    ikp = mega.itn("a_kp")
    iv = mega.itn("a_v")
    imask = p.inp("cmask", [128, 4 * 512], BF16)
    oat = mega.itn("att_in", [256, SEQ], BF16)
    NT = SEQ // 512
    kn = [[S.sb(f"kn{h}_{t}", [128, 512], BF16) for t in range(NT)] for h in range(2)]
    kp = [S.sb(f"kp_{t}", [64, 512], BF16) for t in range(NT)]
    vv = [S.sb(f"v_{t}", [128, 4, 256], BF16) for t in range(NT)]
    mask = S.sb("cmask", [128, 4 * 512], BF16)
    p.load(mask, imask)
    for t in range(NT):
        cs = slice(t * 512, (t + 1) * 512)
        for h in range(2):
            p.load(kn[h][t], ikn[h, :, cs])
        p.load(kp[t], ikp[:, cs])
        p.load(vv[t], iv[cs, :].rearrange("(b p) d -> p b d", p=128))
    ones = p.ones
    bias = []
    for h in range(2):
        mx = {}
        for nm in ("k", "q"):
            m_ = S.sb(f"mx{nm}{h}", [128, 1], F32)
            S.op("dve", lambda e, m_=m_: e.memset(m_.ap, 0.0), writes=[m_])
            mx[nm] = m_
        for t in range(NT):
            cs = slice(t * 512, (t + 1) * 512)
            for nm in ("k", "q"):
                if nm == "k":
                    a_, b_ = kn[h][t], kp[t]
                else:
                    a_ = p.scr("qn_n", [128, 512], BF16, 2)
                    b_ = p.scr("qp_n", [64, 512], BF16, 2)
                    p.load(a_, iqn[h, :, cs])
                    p.load(b_, iqp[h, :, cs])
                s1 = p.scr("nsq1", [128, 512], BF16, 2)
                s2 = p.scr("nsq2", [64, 512], BF16, 2)
                S.op("act", lambda e, s1=s1, a_=a_: e.activation(out=s1.ap, in_=a_.ap, func=AF.Square), reads=[a_], writes=[s1])
                S.op("act", lambda e, s2=s2, b_=b_: e.activation(out=s2.ap, in_=b_.ap, func=AF.Square), reads=[b_], writes=[s2])
                ps = p.bank(6, 8)
                S.op("pe", lambda e, ps=ps, s1=s1: e.matmul(ps.ap, lhsT=ones.ap, rhs=s1.ap, start=True, stop=False), reads=[s1, ones], writes=[ps])
                S.op("pe", lambda e, ps=ps, s2=s2: e.matmul(ps.ap, lhsT=ones.ap[:64, :], rhs=s2.ap, start=False, stop=True), reads=[s2, ones], writes=[ps])
                tm = p.scr("nmx", [128, 1], F32, 2)
                S.op("dve", lambda e, tm=tm, ps=ps: e.reduce_max(out=tm.ap, in_=ps.ap, axis=mybir.AxisListType.X), reads=[ps], writes=[tm])
                m_ = mx[nm]
                S.op("dve", lambda e, tm=tm, m_=m_: e.tensor_tensor(out=m_.ap, in0=m_.ap, in1=tm.ap, op=ALU.max), reads=[tm, m_], writes=[m_])
        bb = S.sb(f"bias{h}", [128, 1], F32)
        S.op("dve", lambda e, bb=bb, mx=mx: e.tensor_tensor(out=bb.ap, in0=mx["k"].ap, in1=mx["q"].ap, op=ALU.mult), reads=[mx["k"], mx["q"]], writes=[bb])
        S.op("act", lambda e, bb=bb: e.activation(out=bb.ap, in_=bb.ap, func=AF.Sqrt), reads=[bb], writes=[bb])
        S.op("dve", lambda e, bb=bb: e.tensor_scalar(out=bb.ap, in0=bb.ap, scalar1=-scale, scalar2=None, op0=ALU.mult), reads=[bb], writes=[bb])
        bias.append(bb)
    def qtile(h, qi):
        cs = slice(qi * 512, (qi + 1) * 512)
        qn = p.scr("qn_m", [128, 512], BF16, 2)
        qp = p.scr("qp_m", [64, 512], BF16, 2)
        p.load(qn, iqn[h, :, cs])
        p.load(qp, iqp[h, :, cs])
        po = p.bank(4, 6)
        psum_ = p.bank(6, 8)
        nkb = 4 * (qi + 1)
        bh = bias[h]
        for kb in range(nkb):
            t, r = kb // 4, kb % 4
            ks = slice(r * 128, (r + 1) * 128)
            ps = p.bank(0, 4)
            knt, kpt, vt_ = kn[h][t], kp[t], vv[t]
            S.op("pe", lambda e, ps=ps, knt=knt, ks=ks: e.matmul(ps.ap, lhsT=knt.ap[:, ks], rhs=qn.ap, start=True, stop=False),
                 reads=[knt, qn], writes=[ps])
            S.op("pe", lambda e, ps=ps, kpt=kpt, ks=ks: e.matmul(ps.ap, lhsT=kpt.ap[:, ks], rhs=qp.ap, start=False, stop=True),
                 reads=[kpt, qp], writes=[ps])
            pt = p.scr("pT", [128, 512], BF16, 4)
            S.op("act", lambda e, pt=pt, ps=ps: e.activation(out=pt.ap, in_=ps.ap, func=AF.Exp, bias=bh.ap[:, 0:1], scale=scale),
                 reads=[ps, bh], writes=[pt])
            if t == qi:
                S.op("pool", lambda e, pt=pt, r=r: e.tensor_tensor(out=pt.ap, in0=pt.ap, in1=mask.ap[:, r * 512:(r + 1) * 512], op=ALU.mult),
                     reads=[pt, mask], writes=[pt])
            S.op("pe", lambda e, pt=pt, vt_=vt_, r=r, kb=kb: e.matmul(po.ap, lhsT=vt_.ap[:, r, h * 128:(h + 1) * 128], rhs=pt.ap, start=(kb == 0), stop=(kb == nkb - 1)),
                 reads=[vt_, pt], writes=[po])
            S.op("pe", lambda e, pt=pt, kb=kb: e.matmul(psum_.ap, lhsT=ones.ap, rhs=pt.ap, start=(kb == 0), stop=(kb == nkb - 1)),
                 reads=[ones, pt], writes=[psum_])
        rs = p.scr("rs", [128, 512], F32, 2)
        S.op("dve", lambda e: e.reciprocal(out=rs.ap, in_=psum_.ap), reads=[psum_], writes=[rs])
        ot = p.scr("ot", [128, 512], BF16, 2)
        S.op("dve", lambda e: e.tensor_tensor(out=ot.ap, in0=po.ap, in1=rs.ap, op=ALU.mult), reads=[po, rs], writes=[ot])
        p.store(oat[h * 128:(h + 1) * 128, cs], ot)

    for h in range(2):
        for qi in range(NT):
            qtile(h, qi)
    p.finish()


def phase_mlstm(mega):
    p = Prog(mega, wslots=1, slot_elems=16)
    S = p.S
    NCH = SEQ // 128
    iq = mega.itn("m_qT")
    ik = mega.itn("m_kT")
    iktm = mega.itn("m_ktm")
    ivtm = mega.itn("m_vtm")
    ilf = mega.itn("m_lf")
    iig = mega.itn("m_ig")
    itri = p.inp("tri", [128, 128], F32)
    oh = mega.itn("mh_in", [SEQ, 256], BF16)
    qT = [S.sb(f"qT{d}", [128, SEQ], BF16) for d in range(2)]
    kT = [S.sb(f"kT{d}", [128, SEQ], BF16) for d in range(2)]
    G = 8
    ktm = [S.sb(f"ktm{g}", [128, G, 256], BF16) for g in range(NCH // G)]
    vtm = [S.sb(f"vtm{g}", [128, G, 257], BF16) for g in range(NCH // G)]
    for d in range(2):
        for hf in range(4):
            cs = slice(hf * 2048, (hf + 1) * 2048)
            S.dma("sp", View(qT[d].ap[:, cs], qT[d].bufs), View(iq[d * 128:(d + 1) * 128, cs], []))
            S.dma("sp", View(kT[d].ap[:, cs], kT[d].bufs), View(ik[d * 128:(d + 1) * 128, cs], []))
    for g in range(NCH // G):
        rs_ = slice(g * G * 128, (g + 1) * G * 128)
        p.load(ktm[g], iktm[rs_, :].rearrange("(b p) d -> p b d", p=128))
        S.op("pool", lambda e, g=g: e.memset(vtm[g].ap[:, :, 256:257], 1.0), writes=[vtm[g]])
        S.dma("sp", View(vtm[g].ap[:, :, 0:256], vtm[g].bufs), View(ivtm[rs_, :].rearrange("(b p) d -> p b d", p=128), []))
    a = S.sb("lf", [128, NCH], F32)
    ig = S.sb("ig", [128, NCH], F32)
    tri = S.sb("tri", [128, 128], F32)
    tri_b = S.sb("tri_b", [128, 128], BF16)
    onesf = S.sb("onesf", [128, 128], F32)
    p.load(tri, itri)
    S.op("dve", lambda e: e.memset(onesf.ap, 1.0), writes=[onesf])
    one1 = S.sb("one1", [1, 1], F32)
    S.op("dve", lambda e: e.memset(one1.ap, 1.0), writes=[one1])
    for (src_row, dst_t) in ((ilf, a), (iig, ig)):
        pg = p.bank(6, 8)
        for pc in range(8):
            rw = p.scr("grow", [1, 1024], F32, 2)
            p.load(rw, src_row[:, pc * 1024:(pc + 1) * 1024])
            for cl in range(8):
                cidx = pc * 8 + cl
                S.op("pe", lambda e, pg=pg, rw=rw, cl=cl, cidx=cidx: e.matmul(pg.ap[:, cidx:cidx + 1], lhsT=rw.ap[0:1, cl * 128:(cl + 1) * 128], rhs=one1.ap, start=True, stop=True),
                     reads=[rw, one1], writes=[pg])
        S.op("dve", lambda e, pg=pg, dst_t=dst_t: e.tensor_copy(out=dst_t.ap, in_=pg.ap[:, :NCH]), reads=[pg], writes=[dst_t])
    F_ = S.sb("F", [128, NCH], F32)
    FL = S.sb("FL", [128, NCH], F32)
    ps = p.bank(6, 8)
    S.op("pe", lambda e: e.matmul(ps.ap[:, :NCH], lhsT=tri.ap, rhs=a.ap, start=True, stop=True), reads=[tri, a], writes=[ps])
    S.op("dve", lambda e: e.tensor_copy(out=F_.ap, in_=ps.ap[:, :NCH]), reads=[ps], writes=[F_])
    ps2 = p.bank(6, 8)
    S.op("pe", lambda e: e.matmul(ps2.ap[:, :NCH], lhsT=onesf.ap, rhs=a.ap, start=True, stop=True), reads=[onesf, a], writes=[ps2])
    S.op("dve", lambda e: e.tensor_copy(out=FL.ap, in_=ps2.ap[:, :NCH]), reads=[ps2], writes=[FL])
    imF = S.sb("imF", [128, NCH], F32)
    S.op("dve", lambda e: e.tensor_tensor(out=imF.ap, in0=ig.ap, in1=F_.ap, op=ALU.subtract), reads=[ig, F_], writes=[imF])
    w_ = S.sb("w_s", [128, NCH], F32)
    S.op("dve", lambda e: e.tensor_tensor(out=w_.ap, in0=imF.ap, in1=FL.ap, op=ALU.add), reads=[imF, FL], writes=[w_])
    S.op("act", lambda e: e.activation(out=w_.ap, in_=w_.ap, func=AF.Exp), reads=[w_], writes=[w_])
    dec = S.sb("decay", [128, NCH], F32)
    S.op("act", lambda e: e.activation(out=dec.ap, in_=FL.ap, func=AF.Exp), reads=[FL], writes=[dec])
    C = [S.sb(f"C{d}", [128, 257], F32) for d in range(2)]
    Cb = [S.sb(f"Cb{d}", [128, 257], BF16) for d in range(2)]
    for d in range(2):
        S.op("dve", lambda e, d=d: e.memset(C[d].ap, 0.0), writes=[C[d]])
        S.op("pool", lambda e, d=d: e.memset(Cb[d].ap, 0.0), writes=[Cb[d]])
    hout = None
    for c in range(NCH):
        cs = slice(c * 128, (c + 1) * 128)
        g, gl = c // G, c % G
        ta = p.scr("ta", [128, 128], F32, 2)
        S.op("pool", lambda e, ta=ta, c=c: e.tensor_scalar(out=ta.ap, in0=tri.ap, scalar1=a.ap[:, c:c + 1], scalar2=None, op0=ALU.mult), reads=[tri, a], writes=[ta])
        pf = p.bank(6, 8)
        S.op("pe", lambda e, pf=pf, ta=ta: e.matmul(pf.ap[:, :128], lhsT=onesf.ap, rhs=ta.ap, start=True, stop=True), reads=[onesf, ta], writes=[pf])
        z = p.scr("z", [128, 128], F32, 2)
        S.op("dve", lambda e, z=z, pf=pf, c=c: e.tensor_scalar(out=z.ap, in0=pf.ap[:, :128], scalar1=imF.ap[:, c:c + 1], scalar2=20.0, op0=ALU.add, op1=ALU.min),
             reads=[pf, imF], writes=[z])
        S.op("act", lambda e, z=z: e.activation(out=z.ap, in_=z.ap, func=AF.Exp), reads=[z], writes=[z])
        dm = p.scr("dm", [128, 128], F32, 2)
        S.op("pool", lambda e, dm=dm, z=z: e.tensor_tensor(out=dm.ap, in0=z.ap, in1=tri.ap, op=ALU.mult), reads=[z, tri], writes=[dm])
        ef = p.scr("ef", [128, 128], F32, 2)
        S.op("act", lambda e, ef=ef, pf=pf: e.activation(out=ef.ap, in_=pf.ap[:, :128], func=AF.Exp), reads=[pf], writes=[ef])
        qt = p.scr("qtil", [128, 2, 128], BF16, 2)
        for d in range(2):
            S.op("pool", lambda e, qt=qt, d=d, ef=ef, cs=cs: e.tensor_tensor(out=qt.ap[:, d, :], in0=qT[d].ap[:, cs], in1=ef.ap, op=ALU.mult), reads=[qT[d], ef], writes=[qt])
        kw = p.scr("kw", [128, 256], BF16, 2)
        S.op("pool", lambda e, kw=kw, g=g, gl=gl, c=c: e.tensor_scalar(out=kw.ap, in0=ktm[g].ap[:, gl, :], scalar1=w_.ap[:, c:c + 1], scalar2=None, op0=ALU.mult),
             reads=[ktm[g], w_], writes=[kw])
        pS = p.bank(0, 2)
        for d in range(2):
            S.op("pe", lambda e, pS=pS, d=d, cs=cs: e.matmul(pS.ap[:, :128], lhsT=kT[d].ap[:, cs], rhs=qT[d].ap[:, cs], start=(d == 0), stop=(d == 1)),
                 reads=[kT[d], qT[d]], writes=[pS])
        pt = p.scr("pTm", [128, 128], BF16, 2)
        S.op("dve", lambda e, pt=pt, pS=pS, dm=dm: e.tensor_tensor(out=pt.ap, in0=pS.ap[:, :128], in1=dm.ap, op=ALU.mult), reads=[pS, dm], writes=[pt])
        pn = p.bank(2, 4)
        S.op("pe", lambda e, pn=pn, pt=pt, g=g, gl=gl: e.matmul(pn.ap[:, :257], lhsT=pt.ap, rhs=vtm[g].ap[:, gl, :], start=True, stop=False), reads=[pt, vtm[g]], writes=[pn])
        for d in range(2):
            S.op("pe", lambda e, pn=pn, qt=qt, d=d: e.matmul(pn.ap[:, :257], lhsT=qt.ap[:, d, :], rhs=Cb[d].ap, start=False, stop=(d == 1)), reads=[qt, Cb[d]], writes=[pn])
        den = p.scr("den", [128, 1], F32, 2)
        S.op("act", lambda e, den=den, pn=pn: e.activation(out=den.ap, in_=pn.ap[:, 256:257], func=AF.Abs), reads=[pn], writes=[den])
        S.op("dve", lambda e, den=den: e.tensor_scalar(out=den.ap, in0=den.ap, scalar1=1.0, scalar2=None, op0=ALU.max), reads=[den], writes=[den])
        S.op("dve", lambda e, den=den: e.reciprocal(out=den.ap, in_=den.ap), reads=[den], writes=[den])
        if gl == 0:
            hout = p.scr("hout", [128, G, 256], BF16, 2)
        S.op("act", lambda e, hout=hout, gl=gl, pn=pn, den=den: e.activation(out=hout.ap[:, gl, :], in_=pn.ap[:, :256], func=AF.Identity, bias=0.0, scale=den.ap[:, 0:1]),
             reads=[pn, den], writes=[hout])
        if gl == G - 1:
            p.store(oh[g * G * 128:(g + 1) * G * 128, :].rearrange("(b p) d -> p b d", p=128), hout)
        if c < NCH - 1:
            for d in range(2):
                pc = p.bank(4, 6)
                S.op("pe", lambda e, pc=pc, kw=kw, d=d, g=g, gl=gl: e.matmul(pc.ap[:, :257], lhsT=kw.ap[:, d * 128:(d + 1) * 128], rhs=vtm[g].ap[:, gl, :], start=True, stop=True),
                     reads=[kw, vtm[g]], writes=[pc])
                S.op("dve", lambda e, pc=pc, d=d, c=c: e.scalar_tensor_tensor(out=C[d].ap, in0=C[d].ap, scalar=dec.ap[:, c:c + 1], in1=pc.ap[:, :257], op0=ALU.mult, op1=ALU.add),
                     reads=[C[d], dec, pc], writes=[C[d]])
                S.op("act", lambda e, d=d: e.activation(out=Cb[d].ap, in_=C[d].ap, func=AF.Copy), reads=[C[d]], writes=[Cb[d]])
    p.finish()


def _phases():
    def p1(mega):
        p = TL(mega)
        p.load_x("xT")
        p.conv1(0, 0)
        p.finish()

    def p2(mega):
        p = TL(mega)
        p.load_x("xT")
        p.conv2(0, 0)
        p.ffn(0)
        p.store_x("x1T")
        p.finish()

    def p3(mega):
        p = TL(mega, resident=False, wslots=2)
        p.load_x("x1T")
        p.mla_lat()
        p.finish()

    def p6(mega):
        p = TL(mega)
        p.load_x("x1T")
        p.attn_out()
        p.ffn(1)
        p.store_x("x2T")
        p.finish()

    def p7(mega):
        p = TL(mega, resident=False, wslots=2)
        p.load_x("x2T")
        p.mlstm_tl()
        p.finish()

    def p9a(mega):
        p = TL(mega)
        p.load_x("x2T")
        p.mlstm_post()
        p.store_x("x2bT")
        p.finish()

    def p9b(mega):
        p = TL(mega)
        p.load_x("x2bT")
        p.ffn(2)
        p.store_x("x3T")
        p.conv1(3, 1)
        p.finish()

    def p10(mega):
        p = TL(mega)
        p.load_x("x3T")
        p.conv2(3, 1)
        p.ffn(3)
        p.final()
        p.finish()

    G = phase_gather
    return [
        phase_mods,
        lambda m: G(m, "mod_in", "mod_all", 128, 48, F32),
        p1,
        lambda m: G(m, "tail_in0", "tail_all0", 128, 32, F32),
        p2,
        p3,
        lambda m: G(m, "lat_in", "lat_all", 1344, T, BF16),
        phase_attn_pre,
        phase_attn,
        lambda m: G(m, "att_in", "att_all", 256, SEQ, BF16),
        p6,
        p7,
        lambda m: G(m, "h_in", "h_all", D, T, BF16),
        phase_mlstm_hs,
        phase_mlstm,
        lambda m: G(m, "mh_in", "mh_all", SEQ, 256, BF16),
        p9a,
        p9b,
        lambda m: G(m, "tail_in3", "tail_all3", 128, 32, F32),
        p10,
    ]


def phase_dummy(mega):
    p = Prog(mega, wslots=1, slot_elems=16)
    oo = p.out("outT", [D, T], F32)
    z = p.S.sb("zz", [128, 16], F32)
    p.S.op("dve", lambda e: e.memset(z.ap, 1.0), writes=[z])
    p.store(oo[0:128, 0:16], z)
    p.finish()


def build_mega(stop=None):
    mega = Mega()
    ph = _phases()
    for k, f in enumerate(ph):
        if stop is not None and k >= stop:
            break
        f(mega)
    if stop is not None and stop < len(ph):
        phase_dummy(mega)
    mega.close()
    return mega


_cache = {}


def _pp(v, nch):
    return np.ascontiguousarray(np.asarray(v, np.float32).reshape(nch, 128).T)


def kernel(x, c, positions, mod_w, mod_b, norm1_g, norm2_g, ffn_w_gate, ffn_w_up, ffn_w_down,
           conv_w_in, conv_w, conv_w_out,
           mla_w_dq, mla_q_norm_g, mla_w_uq, mla_w_dkv, mla_kv_norm_g, mla_w_ukv, mla_w_o,
           mlstm_w_in, mlstm_b_gates, mlstm_head_norm_g, mlstm_w_out, final_norm_g, _stop=None):
    f32 = np.float32
    R = range(NCORES)
    if "mega" not in _cache:
        _cache["mega"] = build_mega(_stop)
    mega = _cache["mega"]
    A = np.asarray
    C_ = np.ascontiguousarray
    sh = {}
    sh["c_col"] = _pp(A(c, f32).reshape(-1), 16)
    sh["ng1T"] = np.concatenate([_pp(norm1_g[l], 16) for l in range(4)], axis=1)
    sh["ng2T"] = np.concatenate([_pp(norm2_g[l], 16) for l in range(4)], axis=1)
    for l in range(4):
        sh[f"wg{l}"] = A(ffn_w_gate[l]); sh[f"wu{l}"] = A(ffn_w_up[l]); sh[f"wd{l}"] = A(ffn_w_down[l])
    for j in range(2):
        sh[f"cw_in{j}"] = A(conv_w_in[j])
        sh[f"cw{j}"] = np.concatenate([_pp(conv_w[j][t], 16) for t in range(3)], axis=1)
        sh[f"cw_out{j}"] = A(conv_w_out[j])
    w_uq = A(mla_w_uq[0]).reshape(768, 16, 192)
    w_uqn = w_uq[:, :, :128].reshape(768, 2048)
    pe = w_uq[:, :, 128:]
    w_uqp = pe.reshape(768, 1024)
    w_uqps = np.concatenate([pe[:, :, 32:], pe[:, :, :32]], axis=2).reshape(768, 1024)
    w_dkv = A(mla_w_dkv[0])
    sh["w_dq"] = A(mla_w_dq[0])
    sh["w_dkvc"] = C_(w_dkv[:, :512]); sh["w_dkvp"] = C_(w_dkv[:, 512:])
    sh["w_dkvps"] = C_(np.concatenate([w_dkv[:, 544:], w_dkv[:, 512:544]], axis=1))
    w_ukv = A(mla_w_ukv[0]).reshape(512, 16, 256)
    w_uk = w_ukv[:, :, :128].reshape(512, 2048)
    w_uv = w_ukv[:, :, 128:].reshape(512, 2048)
    sh["qngT"] = _pp(mla_q_norm_g[0], 6); sh["kvngT"] = _pp(mla_kv_norm_g[0], 4)
    pidx = np.arange(128)
    invf = (10000.0 ** (-2.0 * (pidx % 32).astype(np.float64) / 64)).astype(f32)
    sgn = np.where((pidx % 64) < 32, 1.0, -1.0).astype(f32)
    sh["ropec"] = np.stack([invf, sgn], axis=1).astype(f32)
    pos = A(positions).reshape(-1).astype(np.int32)
    sh["pos_all"] = C_(np.broadcast_to(pos[None, :], (128, SEQ)))
    jj = np.arange(512)[None, :]
    pp_ = np.arange(128)[:, None]
    sh["cmask"] = np.concatenate([(jj >= (128 * r_ + pp_)).astype(f32) for r_ in range(4)], axis=1).astype(NPBF)
    sh["w_o"] = A(mla_w_o[0])
    w_in = A(mlstm_w_in[0])
    sh["m_wo"] = C_(w_in[:, 4096:6144])
    sh["m_wout"] = A(mlstm_w_out[0])
    sh["hngT"] = _pp(mlstm_head_norm_g[0], 16)
    sh["tri"] = np.triu(np.ones((128, 128), f32))
    sh["ident"] = np.eye(128, dtype=f32).astype(NPBF)
    sh["fngT"] = _pp(final_norm_g, 16)
    xTfull = A(x, f32)[0].T
    bgates = A(mlstm_b_gates[0], f32)
    maps = []
    for r in R:
        m = dict(sh)
        cs = slice(r * 1536, (r + 1) * 1536)
        m["mod_w_s"] = A(mod_w)[:, :, cs]
        m["mod_b_s"] = A(mod_b)[:, cs].reshape(1, -1)
        m["xT"] = xTfull[:, r * T:(r + 1) * T]
        sel = np.zeros((128, 8), f32); sel[:, r] = 1.0
        selp = np.zeros((128, 8), f32)
        if r > 0:
            selp[:, r - 1] = 1.0
        m["sel"] = sel; m["selprev"] = selp
        m["pos_own"] = sh["pos_all"][:, r * T:(r + 1) * T]
        m["a_wqn"] = w_uqn[:, r * 256:(r + 1) * 256]; m["a_wqp"] = w_uqp[:, r * 128:(r + 1) * 128]
        m["a_wqps"] = w_uqps[:, r * 128:(r + 1) * 128]
        m["a_wuk"] = w_uk[:, r * 256:(r + 1) * 256]; m["a_wuv"] = w_uv[:, r * 256:(r + 1) * 256]
        hd, hf = r // 2, r % 2
        m["h_wq"] = w_in[:, hd * 256:(hd + 1) * 256]
        m["h_wk"] = w_in[:, 1024 + hd * 256:1024 + (hd + 1) * 256]
        m["h_wv"] = w_in[:, 2048 + hd * 512 + hf * 256:2048 + hd * 512 + (hf + 1) * 256]
        m["h_wg"] = w_in[:, [6144 + hd, 6148 + hd]]
        m["h_bg"] = np.array([[bgates[hd]], [bgates[4 + hd]]], f32)
        maps.append({k: C_(m[k]) for k in mega.ext_in})
    res = run_bass_kernel_spmd(mega.nc, maps, core_ids=list(R))
    outT = np.concatenate([res.results[r]["outT"] for r in R], axis=1)
    return np.ascontiguousarray(outT.T)[None].astype(np.float32)
```
